# Optimizing a Trainium2 kernel written in Bass

```python
import math
import jax
import jax.numpy as jnp
from jax import lax
import numpy as np

D_MODEL = 1024
BATCH = 16
SEQ = 4096
DEPTH = 2

CTX_LEN = 256
GRID_W = 64
EPS = 1e-6

D_CHUNK = D_MODEL // 2
CHUNK = 128
A_GROUPS = 4
A_GROUP_W = D_CHUNK // A_GROUPS

D_SSM = D_MODEL
SSM_HEAD_DIM = 64
SSM_HEADS = D_SSM // SSM_HEAD_DIM
SSM_GROUPS = 2
SSM_STATE = 128
SSM_CONV = 5
SSM_CHUNK = 128

D_ATT = D_MODEL // 2
ATT_QK_DIM = 64
ATT_V_DIM = 2 * ATT_QK_DIM
ATT_HEADS = D_ATT // ATT_V_DIM
D_QK = ATT_HEADS * 2 * ATT_QK_DIM
ROPE_THETA = 10000.0
Q_BLOCK = 128

D_MIX = D_CHUNK + D_SSM + D_ATT
N_CONV_CH = D_SSM + 2 * SSM_GROUPS * SSM_STATE
N_DT = 2 * SSM_HEADS
CTX_SIZES = (N_CONV_CH, N_DT, D_QK, D_ATT)
LAT_SIZES = (D_CHUNK, D_CHUNK, D_CHUNK, D_SSM, D_QK, D_ATT)
N_CTX_COLS = N_CONV_CH + N_DT + D_QK + D_ATT
D_IN_PROJ = N_CTX_COLS + 3 * D_CHUNK + D_SSM + D_QK + D_ATT

kernel_name = "hybrid_diffusion_trunk"


def _cuts(sizes):
    cuts, acc = [], 0
    for s in sizes[:-1]:
        acc += s
        cuts.append(acc)
    return cuts


def rms_norm(x, gain=None):
    xf = x.astype(jnp.float32)
    y = xf * lax.rsqrt(jnp.mean(xf * xf, axis=-1, keepdims=True) + EPS)
    if gain is not None:
        y = y * gain.astype(jnp.float32)
    return y.astype(x.dtype)


def axial_rope(n_tokens):
    rows = n_tokens // GRID_W
    row = jnp.repeat(jnp.arange(rows), GRID_W).astype(jnp.float32)
    col = jnp.tile(jnp.arange(GRID_W), rows).astype(jnp.float32)
    n_freq = ATT_QK_DIM // 4
    inv = ROPE_THETA ** (-jnp.arange(n_freq, dtype=jnp.float32) / n_freq)
    ang = jnp.concatenate([row[:, None] * inv, col[:, None] * inv], axis=-1)
    return jnp.cos(ang), jnp.sin(ang)


def apply_rope(t, cos, sin):
    half = t.shape[-1] // 2
    c = cos[None, :, None, None, :].astype(t.dtype)
    s = sin[None, :, None, None, :].astype(t.dtype)
    t1, t2 = t[..., :half], t[..., half:]
    return jnp.concatenate([t1 * c - t2 * s, t1 * s + t2 * c], axis=-1)


def chunk_mlp(u, v, g, norm_g, ws, bs):
    b, n, _ = u.shape
    nc = n // CHUNK
    u = jax.nn.gelu(u)
    v = rms_norm(jax.nn.gelu(v), norm_g)
    vr = v.reshape(b, nc, CHUNK, A_GROUPS, A_GROUP_W)
    mixed = jnp.einsum('gts,bcsgw->bctgw', ws.astype(vr.dtype), vr) + bs.T[None, None, :, :, None].astype(vr.dtype)
    y = u * mixed.reshape(b, n, D_CHUNK)
    return y * jax.nn.silu(g)


def dwconv_centred(x, w, bias):
    ch = x.shape[-1]
    k = w.shape[0]
    y = lax.conv_general_dilated(x, w[:, None, :].astype(x.dtype), window_strides=(1,),
                                 padding=[(k // 2, k // 2)], dimension_numbers=('NWC', 'WIO', 'NWC'),
                                 feature_group_count=ch)
    return y + bias.astype(x.dtype)


def segsum_exp(a):
    L = a.shape[-1]
    cs = jnp.cumsum(a, axis=-1)
    diff = cs[..., :, None] - cs[..., None, :]
    mask = jnp.tril(jnp.ones((L, L), dtype=bool))
    return jnp.where(mask, jnp.exp(jnp.where(mask, diff, 0.0)), 0.0)


def ssd_prepare(xbc, dt_raw, conv_w, conv_b, dt_bias):
    xbc = jax.nn.silu(dwconv_centred(xbc, conv_w, conv_b))
    xs, bm, cm = jnp.split(xbc, [D_SSM, D_SSM + SSM_GROUPS * SSM_STATE], axis=-1)
    b, n, _ = xs.shape
    xh = xs.reshape(b, n, SSM_HEADS, SSM_HEAD_DIM)
    bm = bm.reshape(b, n, SSM_GROUPS, SSM_STATE)
    cm = cm.reshape(b, n, SSM_GROUPS, SSM_STATE)
    dt = jax.nn.softplus(dt_raw.reshape(b, n, 2, SSM_HEADS).astype(jnp.float32) + dt_bias.astype(jnp.float32))
    return xh, bm, cm, dt


def ssd_scan(xh, dt, a_neg, bm, cm, h0, with_output):
    b, n, H, P = xh.shape
    G, S = bm.shape[2], bm.shape[3]
    R = H // G
    L = SSM_CHUNK
    nc = n // L
    xdt = (xh.astype(jnp.float32) * dt[..., None]).reshape(b, nc, L, G, R, P)
    a = jnp.moveaxis((dt * a_neg).reshape(b, nc, L, G, R), 2, -1)
    a_cs = jnp.cumsum(a, axis=-1)
    bc = bm.astype(jnp.float32).reshape(b, nc, L, G, S)
    cc = cm.astype(jnp.float32).reshape(b, nc, L, G, S)
    decay_to_end = jnp.exp(a_cs[..., -1:] - a_cs)
    states = jnp.einsum('bclgs,bcgrl,bclgrp->bcgrps', bc, decay_to_end, xdt)
    states = jnp.concatenate([h0.astype(jnp.float32).reshape(b, 1, G, R, P, S), states], axis=1)
    chunk_tot = jnp.pad(jnp.moveaxis(a_cs[..., -1], 1, -1), ((0, 0), (0, 0), (0, 0), (1, 0)))
    chunk_decay = segsum_exp(chunk_tot)
    states = jnp.einsum('bgrzc,bcgrps->bzgrps', chunk_decay, states)
    h_final = states[:, -1].reshape(b, H, P, S)
    if not with_output:
        return None, h_final
    lmat = segsum_exp(a)
    cb = jnp.einsum('bclgs,bcmgs->bcglm', cc, bc)
    y_diag = jnp.einsum('bcglm,bcgrlm,bcmgrp->bclgrp', cb, lmat, xdt)
    y_off = jnp.einsum('bclgs,bcgrps,bcgrl->bclgrp', cc, states[:, :-1], jnp.exp(a_cs))
    return (y_diag + y_off).reshape(b, n, H, P), h_final


def ssd_bidir(xh, bm, cm, dt, a_neg, h0_f, h0_b, with_output):
    rev = lambda t: jnp.flip(t, axis=1)
    y_f, h_f = ssd_scan(xh, dt[:, :, 0], a_neg[0], bm, cm, h0_f, with_output)
    y_b, h_b = ssd_scan(rev(xh), rev(dt[:, :, 1]), a_neg[1], rev(bm), rev(cm), h0_b, with_output)
    y = y_f + rev(y_b) if with_output else None
    return y, h_f, h_b


def ssd_output(y, xh, z, d_skip, norm_g):
    b, n = z.shape[:2]
    y = (y + d_skip[:, None].astype(jnp.float32) * xh.astype(jnp.float32)).reshape(b, n, D_SSM)
    y = y * jax.nn.silu(z.astype(jnp.float32))
    y = rms_norm(y.reshape(b, n, SSM_GROUPS, D_SSM // SSM_GROUPS)).reshape(b, n, D_SSM) * norm_g.astype(jnp.float32)
    return y.astype(z.dtype)


def diff_lambda(lam_p, layer_idx):
    lam_init = 0.8 - 0.6 * math.exp(-0.3 * layer_idx)
    lp = lam_p.astype(jnp.float32)
    lam = jnp.exp(jnp.sum(lp[0] * lp[1])) - jnp.exp(jnp.sum(lp[2] * lp[3])) + lam_init
    return lam, lam_init


def diff_attend(q, k, v, lam):
    s = jnp.einsum('bqhid,bkhid->bhiqk', q, k).astype(jnp.float32) * (ATT_QK_DIM ** -0.5)
    p = jax.nn.softmax(s, axis=-1)
    w = p[:, :, 0] - lam * p[:, :, 1]
    return jnp.einsum('bhqk,bkhe->bqhe', w.astype(v.dtype), v)


def latent_diff_attention(q, k, v, k_ctx, v_ctx, lam):
    b, n = q.shape[:2]
    k_all = jnp.concatenate([k, k_ctx.astype(k.dtype)], axis=1)
    v_all = jnp.concatenate([v, v_ctx.astype(v.dtype)], axis=1)
    qb = q.reshape(b, n // Q_BLOCK, Q_BLOCK, ATT_HEADS, 2, ATT_QK_DIM).swapaxes(0, 1)
    ob = lax.map(lambda qq: diff_attend(qq, k_all, v_all, lam), qb)
    return ob.swapaxes(0, 1).reshape(b, n, ATT_HEADS, ATT_V_DIM)


def diff_post(o, subln_g, lam_init, g):
    b, n = o.shape[:2]
    o = rms_norm(o, subln_g) * (1.0 - lam_init)
    return o.reshape(b, n, D_ATT) * jax.nn.silu(g)


def layer_forward(x, xc, cos, sin, c, c_ctx, w_ada, b_ada, w_in, w_out, chunk_norm_g, chunk_ws,
                  chunk_bs, conv_w, conv_b, dt_bias, a_log, d_skip, ssm_norm_g, lam_p, subln_g,
                  layer_idx, update_ctx):
    b, n, _ = x.shape
    m = xc.shape[1]
    shift, scale, gate = jnp.split(jax.nn.silu(c) @ w_ada + b_ada, 3, axis=-1)
    shift_c, scale_c, gate_c = jnp.split(jax.nn.silu(c_ctx) @ w_ada + b_ada, 3, axis=-1)
    h = rms_norm(x) * (1.0 + scale[:, None]) + shift[:, None]
    hc = rms_norm(xc) * (1.0 + scale_c) + shift_c
    lam, lam_init = diff_lambda(lam_p, layer_idx)
    a_neg = -jnp.exp(a_log.astype(jnp.float32))

    pc = hc @ (w_in if update_ctx else w_in[:, :N_CTX_COLS])
    xbc_c, dt_c, k_c, v_c = jnp.split(pc[..., :N_CTX_COLS], _cuts(CTX_SIZES), axis=-1)
    xh_c, bm_c, cm_c, dts_c = ssd_prepare(xbc_c, dt_c, conv_w, conv_b, dt_bias)
    h0 = jnp.zeros((b, SSM_HEADS, SSM_HEAD_DIM, SSM_STATE), jnp.float32)
    ys_c, hf_c, hb_c = ssd_bidir(xh_c, bm_c, cm_c, dts_c, a_neg, h0, h0, update_ctx)
    k_c = k_c.reshape(b, m, ATT_HEADS, 2, ATT_QK_DIM)
    v_c = v_c.reshape(b, m, ATT_HEADS, ATT_V_DIM)
    if update_ctx:
        u_c, uv_c, ga_c, z_c, q_c, gc_c = jnp.split(pc[..., N_CTX_COLS:], _cuts(LAT_SIZES), axis=-1)
        ya_c = chunk_mlp(u_c, uv_c, ga_c, chunk_norm_g, chunk_ws, chunk_bs)
        yb_c = ssd_output(ys_c, xh_c, z_c, d_skip, ssm_norm_g)
        o_c = diff_attend(q_c.reshape(b, m, ATT_HEADS, 2, ATT_QK_DIM), k_c, v_c, lam)
        yc_c = diff_post(o_c, subln_g, lam_init, gc_c)
        xc = xc + gate_c * (jnp.concatenate([ya_c, yb_c, yc_c], axis=-1) @ w_out)
    else:
        xc = None

    p = h @ w_in
    xbc, dt_raw, k, v, u, uv, g_a, z, q, g_c = jnp.split(p, _cuts(CTX_SIZES + LAT_SIZES), axis=-1)
    ya = chunk_mlp(u, uv, g_a, chunk_norm_g, chunk_ws, chunk_bs)
    xh, bm, cm, dts = ssd_prepare(xbc, dt_raw, conv_w, conv_b, dt_bias)
    ys, _, _ = ssd_bidir(xh, bm, cm, dts, a_neg, hf_c, hb_c, True)
    yb = ssd_output(ys, xh, z, d_skip, ssm_norm_g)
    q = apply_rope(q.reshape(b, n, ATT_HEADS, 2, ATT_QK_DIM), cos, sin)
    k = apply_rope(k.reshape(b, n, ATT_HEADS, 2, ATT_QK_DIM), cos, sin)
    o = latent_diff_attention(q, k, v.reshape(b, n, ATT_HEADS, ATT_V_DIM), k_c, v_c, lam)
    yc = diff_post(o, subln_g, lam_init, g_c)
    x = x + gate[:, None] * (jnp.concatenate([ya, yb, yc], axis=-1) @ w_out)
    return x, xc


def setup_inputs(seed: int = 0) -> dict:
    key = jax.random.key(seed)
    ks = jax.random.split(key, 20)
    f32 = jnp.float32

    def nrm(k, shape, scale):
        return jax.random.normal(k, shape, f32) * scale

    dt0 = jnp.exp(jax.random.uniform(ks[13], (DEPTH, 2, SSM_HEADS), f32, math.log(1e-3), math.log(1e-1)))
    return {
        "x": nrm(ks[0], (BATCH, SEQ, D_MODEL), 1.0),
        "c": nrm(ks[1], (BATCH, D_MODEL), 1.0),
        "ctx": nrm(ks[2], (BATCH, CTX_LEN, D_MODEL), 1.0),
        "c_ctx": nrm(ks[3], (D_MODEL,), 1.0),
        "w_ada": nrm(ks[4], (DEPTH, D_MODEL, 3 * D_MODEL), 0.5 * D_MODEL ** -0.5),
        "b_ada": nrm(ks[5], (DEPTH, 3 * D_MODEL), 0.01),
        "w_in": nrm(ks[6], (DEPTH, D_MODEL, D_IN_PROJ), D_MODEL ** -0.5),
        "w_out": nrm(ks[7], (DEPTH, D_MIX, D_MODEL), D_MIX ** -0.5),
        "chunk_norm_g": 1.0 + nrm(ks[8], (DEPTH, D_CHUNK), 0.01),
        "chunk_ws": nrm(ks[9], (DEPTH, A_GROUPS, CHUNK, CHUNK), CHUNK ** -0.5),
        "chunk_bs": 1.0 + nrm(ks[10], (DEPTH, A_GROUPS, CHUNK), 0.01),
        "ssm_conv_w": nrm(ks[11], (DEPTH, SSM_CONV, N_CONV_CH), SSM_CONV ** -0.5),
        "ssm_conv_b": nrm(ks[12], (DEPTH, N_CONV_CH), 0.01),
        "ssm_dt_bias": dt0 + jnp.log(-jnp.expm1(-dt0)),
        "ssm_a_log": jnp.log(jax.random.uniform(ks[14], (DEPTH, 2, SSM_HEADS), f32, 1.0, 16.0)),
        "ssm_d": 1.0 + nrm(ks[15], (DEPTH, SSM_HEADS), 0.01),
        "ssm_norm_g": 1.0 + nrm(ks[16], (DEPTH, D_SSM), 0.01),
        "diff_lambda_p": nrm(ks[17], (DEPTH, 4, ATT_QK_DIM), 0.1),
        "diff_subln_g": 1.0 + nrm(ks[18], (DEPTH, ATT_V_DIM), 0.01),
        "final_norm_g": 1.0 + nrm(ks[19], (D_MODEL,), 0.01),
    }


def reference(x, c, ctx, c_ctx, w_ada, b_ada, w_in, w_out, chunk_norm_g, chunk_ws, chunk_bs,
              ssm_conv_w, ssm_conv_b, ssm_dt_bias, ssm_a_log, ssm_d, ssm_norm_g,
              diff_lambda_p, diff_subln_g, final_norm_g):
    cos, sin = axial_rope(x.shape[1])
    xc = ctx
    for li in range(DEPTH):
        x, xc = layer_forward(x, xc, cos, sin, c, c_ctx, w_ada[li], b_ada[li], w_in[li], w_out[li],
                              chunk_norm_g[li], chunk_ws[li], chunk_bs[li], ssm_conv_w[li], ssm_conv_b[li],
                              ssm_dt_bias[li], ssm_a_log[li], ssm_d[li], ssm_norm_g[li],
                              diff_lambda_p[li], diff_subln_g[li], li, li < DEPTH - 1)
    return rms_norm(x, final_norm_g)
```

```python
import math
import os
from contextlib import ExitStack
import numpy as np
import concourse.bass as bass
import concourse.mybir as mybir
from concourse.bass_utils import run_bass_kernel_spmd

F32 = mybir.dt.float32
BF16 = mybir.dt.bfloat16
AF = mybir.ActivationFunctionType
ALU = mybir.AluOpType

D = 1024
NCTX_FULL = 256
NLAT_FULL = 4096
DEPTH = 2
EPS = 1e-6
DIN = 6176
C_XBC, C_DT, C_K, C_V, C_U, C_VM, C_GA, C_Z, C_Q, C_GC = 0, 1536, 1568, 2080, 2592, 3104, 3616, 4128, 5152, 5664
NEG = -30000.0
NRING = 8


class Buf:
    __slots__ = ("t", "w", "r", "const")

    def __init__(self, t, const=False):
        self.t = t
        self.w = None
        self.r = []
        self.const = const

    def __getitem__(self, k):
        return self.t[k]


class Rot:
    def __init__(self, bufs):
        self.bufs = bufs
        self.i = 0

    def next(self):
        b = self.bufs[self.i % len(self.bufs)]
        self.i += 1
        return b


class Sch:
    def __init__(self, nc):
        self.nc = nc
        self.eng = {"pe": nc.tensor, "act": nc.scalar, "dve": nc.vector, "pool": nc.gpsimd, "sp": nc.sync}
        self.sem = {k: nc.alloc_semaphore("s_" + k) for k in ("pe", "act", "dve", "pool")}
        self.cnt = {k: 0 for k in self.sem}
        self.seen = {k: {} for k in self.eng}
        self.rings = {q: [nc.alloc_semaphore(f"d_{q}{i}") for i in range(NRING)] for q in ("sp", "pool", "act")}
        self.rcnt = {q: [0] * NRING for q in self.rings}
        self.rpos = {q: 0 for q in self.rings}
        self.nwait = 0

    def _wait(self, e, ev):
        if ev is None:
            return
        key, sem, val = ev
        if e == "pe" and key == "pe":
            return
        if self.seen[e].get(key, 0) >= val:
            return
        self.eng[e].wait_ge(sem, val)
        self.seen[e][key] = val
        self.nwait += 1

    def _deps(self, e, reads, writes):
        for b in reads:
            self._wait(e, b.w)
        for b in writes:
            self._wait(e, b.w)
            for ev in b.r:
                self._wait(e, ev)

    def _mark(self, ev, reads, writes):
        for b in reads:
            if not b.const:
                b.r.append(ev)
        for b in writes:
            b.w = ev
            b.r = []

    def op(self, e, fn, reads=(), writes=()):
        self._deps(e, reads, writes)
        ins = fn(self.eng[e])
        self.cnt[e] += 1
        ins.then_inc(self.sem[e], 1)
        self._mark((e, self.sem[e], self.cnt[e]), reads, writes)

    def dma(self, q, out, in_, reads=(), writes=(), **kw):
        i = self.rpos[q]
        self.rpos[q] = (i + 1) % NRING
        sem = self.rings[q][i]
        if self.rcnt[q][i] > 0:
            self._wait(q, ((q, i), sem, self.rcnt[q][i]))
        self._deps(q, reads, writes)
        self.eng[q].dma_start(out=out, in_=in_, **kw).then_inc(sem, 16)
        self.rcnt[q][i] += 16
        self._mark(((q, i), sem, self.rcnt[q][i]), reads, writes)

    def barrier(self):
        evs = [(k, self.sem[k], self.cnt[k]) for k in self.sem if self.cnt[k] > 0]
        for q in self.rings:
            for i, s in enumerate(self.rings[q]):
                if self.rcnt[q][i] > 0:
                    evs.append(((q, i), s, self.rcnt[q][i]))
        for e in self.eng:
            for ev in evs:
                self._wait(e, ev)


def build(NB, NCTX, NLAT, depth=DEPTH, dbg=(), skip=()):
    T = NCTX + NLAT
    NCH = T // 128
    NCC = NCTX // 128
    nc = bass.Bass("TRN2", target_bir_lowering=False)
    S = Sch(nc)

    def din(name, shape, dt=F32):
        return nc.dram_tensor(name, list(shape), dt, kind="ExternalInput").ap()

    def dscr(name, shape, dt=F32):
        kind = "ExternalOutput" if name in dbg else "Internal"
        return nc.dram_tensor(name, list(shape), dt, kind=kind).ap()

    x_in = din("x", [NB, NLAT, D])
    ctx_in = din("ctx", [NB, NCTX, D])
    cvec = din("cvec", [NB + 1, D])
    w_ada = din("w_ada", [depth, D, 3 * D])
    b_ada = din("b_ada", [depth, 3 * D])
    w_in = din("w_in", [depth, D, DIN])
    w_out = din("w_out", [depth, 2048, D])
    chunk_norm_g = din("chunk_norm_g", [depth, 512])
    chunk_ws = din("chunk_ws", [depth, 4, 128, 128])
    chunk_bs = din("chunk_bs", [depth, 4, 128])
    conv_w = din("ssm_conv_w", [depth, 5, 1536])
    conv_b = din("ssm_conv_b", [depth, 1536])
    dt_bias = din("ssm_dt_bias", [depth, 32])
    a_log = din("ssm_a_log", [depth, 32])
    ssm_d = din("ssm_d", [depth, 16])
    ssm_norm_g = din("ssm_norm_g", [depth, 1024])
    lam_p = din("diff_lambda_p", [depth, 256])
    subln_g = din("diff_subln_g", [depth, 128])
    final_g = din("final_norm_g", [1, D])
    k_const = din("k_const", [6, 128, 128])
    k_rt = din("k_rt", [128, 128])
    k_rope = din("k_rope", [2, 128, NLAT])
    y_out = nc.dram_tensor("y", [NB, NLAT, D], F32, kind="ExternalOutput").ap()

    modrow = dscr("modrow", [depth, NB + 1, 3 * D])
    xbcT = dscr("xbcT", [NB, 1536, T], BF16)
    dtraw = dscr("dtraw", [NB, T, 32])
    dta = dscr("dta", [NB, T, 64])
    kT = dscr("kT", [NB, 512, T], BF16)
    qT = dscr("qT", [NB, 512, T], BF16)
    vtok = dscr("vtok", [NB, T, 512], BF16)
    zs = dscr("zs", [NB, T, 1024], BF16)
    gcT = dscr("gcT", [NB, 512, T], BF16)
    catT = dscr("catT", [NB, 2048, T], BF16)
    xh_d = dscr("xh", [NB, T, 1024], BF16)
    btm_d = dscr("btm", [NB, T, 256], BF16)
    bT_d = dscr("bT", [NB, 256, T], BF16)
    cT_d = dscr("cT", [NB, 256, T], BF16)
    yf_d = dscr("yf", [NB, T, 1024])
    yb_d = dscr("yb", [NB, T, 1024])
    x1_d = dscr("x1", [NB, T, D])

    uid = [0]

    def alloc(es, name, shape, dt, n=1, const=False):
        uid[0] += 1
        bufs = [Buf(es.enter_context(nc.sbuf_tensor(f"{name}{i}_{uid[0]}", list(shape), dt)), const) for i in range(n)]
        return bufs[0] if n == 1 else Rot(bufs)

    def palloc(es, name, shape, dt, n=1):
        uid[0] += 1
        bufs = [Buf(es.enter_context(nc.psum_tensor(f"{name}{i}_{uid[0]}", list(shape), dt))) for i in range(n)]
        return bufs[0] if n == 1 else Rot(bufs)

    def src_rows(L, b, t0, n):
        if L == 0:
            if t0 < NCTX:
                return ctx_in[b, t0:t0 + n, :]
            return x_in[b, t0 - NCTX:t0 - NCTX + n, :]
        return x1_d[b, t0:t0 + n, :]

    ev_ctr = [0]

    def evac_copy(out, in_, reads, writes):
        ev_ctr[0] += 1
        if ev_ctr[0] % 2:
            S.op("act", lambda E: E.activation(out=out, in_=in_, func=AF.Copy), reads, writes)
        else:
            S.op("dve", lambda E: E.tensor_copy(out=out, in_=in_), reads, writes)

    def rstd_from_ms(ms, rs, n):
        S.op("dve", lambda E: E.tensor_scalar(out=rs[:, 0:n], in0=ms[:, 0:n], scalar1=EPS, scalar2=None, op0=ALU.add), [ms], [rs])
        S.op("act", lambda E: E.activation(out=rs[:, 0:n], in_=rs[:, 0:n], func=AF.Sqrt), [rs], [rs])
        S.op("dve", lambda E: E.reciprocal(out=rs[:, 0:n], in_=rs[:, 0:n]), [rs], [rs])

    with ExitStack() as g_es:
        kc_f = alloc(g_es, "kc_f", [128, 6, 128], F32, const=True)
        ident_bf = alloc(g_es, "ident_bf", [128, 128], BF16, const=True)
        neg_bf = alloc(g_es, "neg_bf", [128, 2, 4, 128], BF16, const=True)
        rt_f = alloc(g_es, "rt_f", [128, 128], F32, const=True)
        S.dma("sp", kc_f[:, :, :], k_const.rearrange("c p n -> p c n"), [], [kc_f])
        S.dma("sp", rt_f[:, :], k_rt[:, :], [], [rt_f])
        S.op("dve", lambda E: E.tensor_copy(out=ident_bf[:, :], in_=kc_f[:, 0, :]), [kc_f], [ident_bf])
        for d_ in range(2):
            S.op("dve", lambda E: E.tensor_copy(out=neg_bf[:, d_, :, :], in_=kc_f[:, 4 + d_, :].unsqueeze(1).to_broadcast([128, 4, 128])),
                 [kc_f], [neg_bf])
        tri = [kc_f[:, 1, :], kc_f[:, 2, :]]
        ones_f = kc_f[:, 3, :]

        for L in range(depth):
            last = (L == depth - 1)
            lam_init = 0.8 - 0.6 * math.exp(-0.3 * L)
            S.barrier()
            with ExitStack() as es:
                cT_sb = alloc(es, "a_cT", [128, 8, NB + 1], F32)
                sc_sb = alloc(es, "a_sc", [128, 8, NB + 1], F32)
                wa = alloc(es, "a_wa", [128, 8, 512], F32, n=2)
                bab = alloc(es, "a_bab", [NB + 1, 3 * D], F32)
                mr = alloc(es, "a_mr", [NB + 1, 3 * D], F32)
                psa = palloc(es, "a_ps", [128, 512], F32, n=2)
                for r_ in range(NB + 1):
                    S.dma("sp", cT_sb[:, :, r_], cvec[r_].rearrange("(k p) -> p k", p=128), [], [cT_sb], allow_slow_non_contiguous=True)
                S.dma("sp", bab[:, :], b_ada[L:L + 1, :].partition_broadcast(NB + 1), [], [bab])
                S.op("act", lambda E: E.activation(out=sc_sb[:, :, :], in_=cT_sb[:, :, :], func=AF.Silu), [cT_sb], [sc_sb])
                for n in range(6):
                    w = wa.next()
                    S.dma("sp", w[:, :, :], w_ada[L, :, n * 512:(n + 1) * 512].rearrange("(k p) n -> p k n", p=128), [], [w])
                    ps = psa.next()

                    def mm(E):
                        for k in range(8):
                            ins = E.matmul(ps[0:NB + 1, :], sc_sb[:, k, :], w[:, k, :], start=(k == 0), stop=(k == 7))
                        return ins
                    S.op("pe", mm, [sc_sb, w], [ps])
                    S.op("dve", lambda E: E.tensor_tensor(out=mr[:, n * 512:(n + 1) * 512], in0=ps[0:NB + 1, :],
                                                         in1=bab[:, n * 512:(n + 1) * 512], op=ALU.add), [ps, bab], [mr])
                S.dma("pool", modrow[L], mr[:, :], [mr], [])
            S.barrier()

            with ExitStack() as es:
                win_bf = alloc(es, "win_bf", [128, 8, DIN], BF16)
                with ExitStack() as es2:
                    wst = alloc(es2, "wst", [128, 1544], F32, n=3)
                    ci = 0
                    for k in range(8):
                        for c4 in range(4):
                            st_ = wst.next()
                            c0 = c4 * 1544
                            S.dma("sp", st_[:, :], w_in[L, k * 128:(k + 1) * 128, c0:c0 + 1544], [], [st_])
                            e = ("act", "dve", "pool")[ci % 3]
                            ci += 1
                            if e == "act":
                                S.op("act", lambda E: E.activation(out=win_bf[:, k, c0:c0 + 1544], in_=st_[:, :], func=AF.Copy), [st_], [win_bf])
                            else:
                                S.op(e, lambda E: E.tensor_copy(out=win_bf[:, k, c0:c0 + 1544], in_=st_[:, :]), [st_], [win_bf])
                S.barrier()
                win_bf.const = True
                modT = alloc(es, "modT", [128, NB + 1, 24], F32, const=True)
                for r_ in range(NB + 1):
                    S.dma("sp", modT[:, r_, :], modrow[L, r_].rearrange("(j p) -> p j", p=128), [], [modT], allow_slow_non_contiguous=True)
                S.op("dve", lambda E: E.tensor_scalar(out=modT[:, :, 8:16], in0=modT[:, :, 8:16], scalar1=1.0, scalar2=None, op0=ALU.add),
                     [modT], [modT])
                cng_bc = alloc(es, "cng_bc", [128, 512], F32, const=True)
                S.dma("sp", cng_bc[:, :], chunk_norm_g[L:L + 1, :].partition_broadcast(128), [], [cng_bc])
                bs_bc = alloc(es, "bs_bc", [128, 512], F32, const=True)
                S.dma("sp", bs_bc[:, :], chunk_bs[L:L + 1].rearrange("o g t -> o (g t)").partition_broadcast(128), [], [bs_bc])
                wsT_f = alloc(es, "wsT_f", [128, 4, 128], F32)
                wsT = alloc(es, "wsT", [128, 4, 128], BF16, const=True)
                for g_ in range(4):
                    S.dma("sp", wsT_f[:, g_, :], chunk_ws[L, g_].rearrange("t s -> s t"), [], [wsT_f], allow_slow_non_contiguous=True)
                S.op("dve", lambda E: E.tensor_copy(out=wsT[:, :, :], in_=wsT_f[:, :, :]), [wsT_f], [wsT])

                xt_r = alloc(es, "p_xt", [128, D], F32, n=4)
                junk = alloc(es, "p_junk", [128, D], BF16)
                ms_r = alloc(es, "p_ms", [128, 4], F32, n=8)
                rs_r = alloc(es, "p_rs", [128, 4], F32, n=8)
                xn_r = alloc(es, "p_xn", [128, D], BF16, n=4)
                hT_r = alloc(es, "p_hT", [128, 8, 512], BF16, n=2)
                hTa_r = Rot([Buf(hT_r.bufs[i].t) for i in range(2)])
                hTd_r = Rot([Buf(hT_r.bufs[i].t) for i in range(2)])
                stf_r = alloc(es, "p_stf", [128, 512], F32, n=3)
                stb_r = alloc(es, "p_stb", [128, 512], BF16, n=3)
                cs_r = alloc(es, "p_cs", [128, 2, 512], F32, n=2)
                t1_r = alloc(es, "p_t1", [128, 512], F32, n=2)
                t2_r = alloc(es, "p_t2", [128, 512], F32, n=2)
                ug_r = alloc(es, "p_ug", [128, 4, 512], BF16, n=2)
                gu_r = alloc(es, "p_gu", [128, 512], F32, n=2)
                gv_r = alloc(es, "p_gv", [128, 512], F32, n=4)
                vn_r = alloc(es, "p_vn", [128, 512], BF16, n=4)
                tm_r = alloc(es, "p_tm", [128, 512], F32, n=2)
                ya_r = alloc(es, "p_ya", [128, 512], BF16, n=2)
                dts_r = alloc(es, "p_dts", [128, 32], F32, n=2)
                psT_r = palloc(es, "p_psT", [128, 8, 128], BF16, n=2)
                psM_r = palloc(es, "p_psM", [128, 512], F32, n=4)
                psX_r = palloc(es, "p_psX", [128, 512], F32, n=2)
                print("P1 sbuf remaining", nc.sbuf_bytes_remaining)

                def p1_group(b, t0, N):
                    if True:
                        is_ctx = t0 < NCTX
                        r = NB if is_ctx else b
                        ctx_only = is_ctx and last
                        nt = N // 128
                        hT = hT_r.next()
                        hTs = [hTa_r.next(), hTd_r.next()]
                        xns = []
                        xts = []
                        ms = ms_r.next()
                        rs = rs_r.next()
                        for i in range(nt):
                            xt = xt_r.next()
                            S.dma("sp", xt[:, :], src_rows(L, b, t0 + i * 128, 128), [], [xt])
                            S.op("act", lambda E: E.activation(out=junk[:, :], in_=xt[:, :], func=AF.Square, scale=1.0 / 32.0,
                                                               accum_out=ms[:, i:i + 1]), [xt], [junk, ms])
                            xts.append(xt)
                        rstd_from_ms(ms, rs, nt)
                        for i in range(nt):
                            xn = xn_r.next()
                            S.op("act", lambda E: E.activation(out=xn[:, :], in_=xts[i][:, :], func=AF.Identity, scale=rs[:, i:i + 1]), [xts[i], rs], [xn])
                            xns.append(xn)
                        yield
                        for i in range(nt):
                            xn = xns[i]
                            psT = psT_r.next()

                            def tr(E):
                                for k in range(8):
                                    ins = E.transpose(out=psT[:, k, :], in_=xn[:, k * 128:(k + 1) * 128], identity=ident_bf[:, :])
                                return ins
                            S.op("pe", tr, [xn, ident_bf], [psT])
                            for k in range(8):
                                if i % 2 == 0:
                                    S.op("act", lambda E: E.activation(out=hT[:, k, i * 128:(i + 1) * 128], in_=psT[:, k, :], func=AF.Identity,
                                                                       scale=modT[:, r, 8 + k:9 + k], bias=modT[:, r, k:k + 1]), [psT, modT], [hTs[0]])
                                else:
                                    S.op("dve", lambda E: E.tensor_scalar(out=hT[:, k, i * 128:(i + 1) * 128], in0=psT[:, k, :],
                                                                          scalar1=modT[:, r, 8 + k:9 + k], scalar2=modT[:, r, k:k + 1],
                                                                          op0=ALU.mult, op1=ALU.add), [psT, modT], [hTs[1]])
                        yield

                        def fm_mm(ps, col0):
                            def mm(E):
                                for k in range(8):
                                    ins = E.matmul(ps[:, 0:N], win_bf[:, k, col0:col0 + 128], hT[:, k, 0:N], start=(k == 0), stop=(k == 7))
                                return ins
                            S.op("pe", mm, [win_bf] + hTs, [ps])

                        def tm_mm(ps, i, col0, ncol):
                            def mm(E):
                                for k in range(8):
                                    ins = E.matmul(ps[:, 0:ncol], hT[:, k, i * 128:(i + 1) * 128], win_bf[:, k, col0:col0 + ncol],
                                                   start=(k == 0), stop=(k == 7))
                                return ins
                            S.op("pe", mm, [win_bf] + hTs, [ps])

                        for j in range(12):
                            ps = psM_r.next()
                            fm_mm(ps, C_XBC + j * 128)
                            st_ = stb_r.next()
                            evac_copy(st_[:, 0:N], ps[:, 0:N], [ps], [st_])
                            S.dma("pool", xbcT[b, j * 128:(j + 1) * 128, t0:t0 + N], st_[:, 0:N], [st_], [])
                        if not is_ctx:
                            cs = cs_r.next()
                            S.dma("sp", cs[:, :, 0:N], k_rope[:, :, t0 - NCTX:t0 - NCTX + N].rearrange("c p n -> p c n"), [], [cs])
                        fams = [(C_K, kT)] if ctx_only else [(C_K, kT), (C_Q, qT)]
                        tiles = [(c0, dst, j) for (c0, dst) in fams for j in range(4)]
                        pend = []
                        for ti in range(len(tiles) + 1):
                            if ti < len(tiles):
                                c0, dst, j = tiles[ti]
                                ps = psM_r.next()
                                fm_mm(ps, c0 + j * 128)
                                if is_ctx:
                                    sb = stb_r.next()
                                    evac_copy(sb[:, 0:N], ps[:, 0:N], [ps], [sb])
                                    S.dma("pool", dst[b, j * 128:(j + 1) * 128, t0:t0 + N], sb[:, 0:N], [sb], [])
                                else:
                                    qf = stf_r.next()
                                    S.op("act", lambda E: E.activation(out=qf[:, 0:N], in_=ps[:, 0:N], func=AF.Copy), [ps], [qf])
                                    pend.append((qf, dst, j))
                            if ti >= 1 and not is_ctx:
                                qf, dst, j = pend.pop(0)
                                sb = stb_r.next()
                                psr = psX_r.next()
                                S.op("pe", lambda E: E.matmul(psr[:, 0:N], rt_f[:, :], qf[:, 0:N], start=True, stop=True), [qf, rt_f], [psr])
                                t1 = t1_r.next()
                                t2 = t2_r.next()
                                S.op("pool", lambda E: E.tensor_tensor(out=t1[:, 0:N], in0=qf[:, 0:N], in1=cs[:, 0, 0:N], op=ALU.mult),
                                     [qf, cs], [t1])
                                S.op("dve", lambda E: E.tensor_tensor(out=t2[:, 0:N], in0=psr[:, 0:N], in1=cs[:, 1, 0:N], op=ALU.mult),
                                     [psr, cs], [t2])
                                S.op("pool", lambda E: E.tensor_tensor(out=sb[:, 0:N], in0=t1[:, 0:N], in1=t2[:, 0:N], op=ALU.add),
                                     [t1, t2], [sb])
                                S.dma("pool", dst[b, j * 128:(j + 1) * 128, t0:t0 + N], sb[:, 0:N], [sb], [])
                        yield
                        for i in range(nt):
                            ps = psX_r.next()
                            tm_mm(ps, i, C_DT, 32)
                            ds = dts_r.next()
                            evac_copy(ds[:, :], ps[:, 0:32], [ps], [ds])
                            S.dma("pool", dtraw[b, t0 + i * 128:t0 + (i + 1) * 128, :], ds[:, :], [ds], [])
                        for i in range(nt):
                            ps = psM_r.next()
                            tm_mm(ps, i, C_V, 512)
                            sb = stb_r.next()
                            evac_copy(sb[:, :], ps[:, :], [ps], [sb])
                            S.dma("pool", vtok[b, t0 + i * 128:t0 + (i + 1) * 128, :], sb[:, :], [sb], [])
                        if ctx_only:
                            return
                        ug = ug_r.next()
                        for j in range(4):
                            ps = psM_r.next()
                            fm_mm(ps, C_U + j * 128)
                            gu = gu_r.next()
                            S.op("act", lambda E: E.activation(out=gu[:, 0:N], in_=ps[:, 0:N], func=AF.Gelu_apprx_tanh), [ps], [gu])
                            ps2 = psM_r.next()
                            fm_mm(ps2, C_GA + j * 128)
                            sg = stf_r.next()
                            S.op("act", lambda E: E.activation(out=sg[:, 0:N], in_=ps2[:, 0:N], func=AF.Silu), [ps2], [sg])
                            S.op("pool", lambda E: E.tensor_tensor(out=ug[:, j, 0:N], in0=gu[:, 0:N], in1=sg[:, 0:N], op=ALU.mult), [gu, sg], [ug])
                        for j in range(4):
                            ps = psM_r.next()
                            fm_mm(ps, C_GC + j * 128)
                            sb = stb_r.next()
                            S.op("act", lambda E: E.activation(out=sb[:, 0:N], in_=ps[:, 0:N], func=AF.Silu), [ps], [sb])
                            S.dma("act", gcT[b, j * 128:(j + 1) * 128, t0:t0 + N], sb[:, 0:N], [sb], [])
                        vns = []
                        ms = ms_r.next()
                        rs = rs_r.next()
                        gvs = []
                        for i in range(nt):
                            ps = psM_r.next()
                            tm_mm(ps, i, C_VM, 512)
                            gv = gv_r.next()
                            S.op("act", lambda E: E.activation(out=gv[:, :], in_=ps[:, :], func=AF.Gelu_apprx_tanh), [ps], [gv])
                            S.op("act", lambda E: E.activation(out=junk[:, 0:512], in_=gv[:, :], func=AF.Square, scale=1.0 / math.sqrt(512.0),
                                                               accum_out=ms[:, i:i + 1]), [gv], [junk, ms])
                            gvs.append(gv)
                        rstd_from_ms(ms, rs, nt)
                        for i in range(nt):
                            vn = vn_r.next()
                            S.op("dve", lambda E: E.scalar_tensor_tensor(out=vn[:, :], in0=gvs[i][:, :], scalar=rs[:, i:i + 1], in1=cng_bc[:, :],
                                                                         op0=ALU.mult, op1=ALU.mult), [gvs[i], rs, cng_bc], [vn])
                            vns.append(vn)
                        for i in range(nt):
                            for (c0, dst, dc0) in ((C_Z, zs, 0), (C_Z + 512, zs, 512)):
                                ps = psM_r.next()
                                tm_mm(ps, i, c0, 512)
                                sb = stb_r.next()
                                S.op("act", lambda E: E.activation(out=sb[:, :], in_=ps[:, :], func=AF.Silu), [ps], [sb])
                                S.dma("act", dst[b, t0 + i * 128:t0 + (i + 1) * 128, dc0:dc0 + 512], sb[:, :], [sb], [])
                        for i in range(nt):
                            vn = vns[i]
                            psm = psX_r.next()

                            def mix(E):
                                for g in range(4):
                                    ins = E.matmul(psm[:, g * 128:(g + 1) * 128], vn[:, g * 128:(g + 1) * 128], wsT[:, g, :], start=True, stop=True)
                                return ins
                            S.op("pe", mix, [vn, wsT], [psm])
                            tm = tm_r.next()
                            S.op("dve", lambda E: E.tensor_tensor(out=tm[:, :], in0=psm[:, :], in1=bs_bc[:, :], op=ALU.add), [psm, bs_bc], [tm])
                            ya = ya_r.next()
                            S.op("pool", lambda E: E.tensor_tensor(out=ya[:, :].rearrange("p (g t) -> p g t", g=4),
                                                                   in0=tm[:, :].rearrange("p (g t) -> p g t", g=4),
                                                                   in1=ug[:, :, i * 128:(i + 1) * 128], op=ALU.mult), [tm, ug], [ya])
                            S.dma("pool", catT[b, 0:512, t0 + i * 128:t0 + (i + 1) * 128].rearrange("(g p) t -> p g t", p=128),
                                  ya[:, :].rearrange("p (g t) -> p g t", g=4), [ya], [])

                glist = []
                for b in range(NB):
                    glist.append(p1_group(b, 0, NCTX))
                    for g in range((NLAT + 511) // 512):
                        glist.append(p1_group(b, NCTX + g * 512, min(512, NLAT - g * 512)))
                next(glist[0])
                next(glist[0])
                for gi in range(len(glist)):
                    if gi + 1 < len(glist):
                        next(glist[gi + 1])
                    next(glist[gi])
                    if gi + 1 < len(glist):
                        next(glist[gi + 1])
                    for _ in glist[gi]:
                        pass
                S.barrier()

            with ExitStack() as es:
                cw = alloc(es, "s_cw", [128, 12, 5], F32, const=True)
                cb = alloc(es, "s_cb", [128, 12], F32, const=True)
                for k_ in range(5):
                    S.dma("sp", cw[:, :, k_], conv_w[L, k_].rearrange("(j p) -> p j", p=128), [], [cw], allow_slow_non_contiguous=True)
                S.dma("sp", cb[:, :], conv_b[L].rearrange("(j p) -> p j", p=128), [], [cb], allow_slow_non_contiguous=True)
                dtb_bc = alloc(es, "s_dtb", [128, 32], F32, const=True)
                aneg_bc = alloc(es, "s_aneg", [128, 32], F32, const=True)
                S.dma("sp", dtb_bc[:, :], dt_bias[L:L + 1, :].partition_broadcast(128), [], [dtb_bc])
                S.dma("sp", aneg_bc[:, :], a_log[L:L + 1, :].partition_broadcast(128), [], [aneg_bc])
                S.op("act", lambda E: E.activation(out=aneg_bc[:, :], in_=aneg_bc[:, :], func=AF.Exp), [aneg_bc], [aneg_bc])
                S.op("dve", lambda E: E.tensor_scalar(out=aneg_bc[:, :], in0=aneg_bc[:, :], scalar1=-1.0, scalar2=None, op0=ALU.mult),
                     [aneg_bc], [aneg_bc])
                raw_r = alloc(es, "s_raw", [128, 12, 516], BF16, n=2)
                dg = alloc(es, "s_dg", [128, 12, 5, 128], BF16, const=True)
                for j in range(12):
                    for k in range(5):
                        S.op("dve", lambda E: E.tensor_scalar(out=dg[:, j, k, :], in0=kc_f[:, 0, :], scalar1=cw[:, j, k:k + 1], scalar2=None,
                                                              op0=ALU.mult), [kc_f, cw], [dg])
                psC_r = palloc(es, "s_psC", [128, 512], F32, n=4)
                sil_r = [alloc(es, f"s_sil{j}_", [128, 512], BF16, n=2) for j in range(12)]
                xhs_r = alloc(es, "s_xhs", [128, 1024], BF16, n=2)
                bts_r = alloc(es, "s_bts", [128, 256], BF16, n=2)
                dtt_r = alloc(es, "s_dtt", [128, 4, 32], F32, n=2)
                dtw = [alloc(es, f"s_dtw{i}", [128, 4, 32], F32) for i in range(4)]
                dto_r = alloc(es, "s_dto", [128, 4, 64], F32, n=2)
                psT_r = palloc(es, "s_psT", [128, 8, 128], BF16, n=2)
                psB_r = palloc(es, "s_psB", [128, 2, 128], BF16, n=2)
                for b in range(NB):
                    groups = [(0, NCTX, 0, NCTX)] + [(NCTX + g * 512, min(512, NLAT - g * 512), NCTX, T) for g in range((NLAT + 511) // 512)]
                    for (t0, N, s0, s1) in groups:
                        nt = N // 128
                        raw = raw_r.next()
                        lo = 0
                        hi = N + 4
                        if t0 == s0:
                            S.op("pool", lambda E: E.memset(raw[:, :, 0:2], 0.0), [], [raw])
                            lo = 2
                        if t0 + N == s1:
                            S.op("pool", lambda E: E.memset(raw[:, :, N + 2:N + 4], 0.0), [], [raw])
                            hi = N + 2
                        S.dma("sp", raw[:, :, lo:hi], xbcT[b, :, t0 - 2 + lo:t0 - 2 + hi].rearrange("(j p) t -> p j t", p=128), [], [raw])
                        sils = []
                        for j in range(12):
                            acc = psC_r.next()

                            def cmm(E):
                                for k in range(5):
                                    ins = E.matmul(acc[:, 0:N], dg[:, j, k, :], raw[:, j, k:k + N], start=(k == 0), stop=(k == 4))
                                return ins
                            S.op("pe", cmm, [dg, raw], [acc])
                            sil = sil_r[j].next()
                            S.op("act", lambda E: E.activation(out=sil[:, 0:N], in_=acc[:, 0:N], func=AF.Silu, bias=cb[:, j:j + 1], scale=1.0),
                                 [acc, cb], [sil])
                            sils.append(sil)
                            if j >= 8:
                                dst = bT_d if j < 10 else cT_d
                                jj = (j - 8) % 2
                                S.dma("pool", dst[b, jj * 128:(jj + 1) * 128, t0:t0 + N], sil[:, 0:N], [sil], [])
                        for i in range(nt):
                            psT = psT_r.next()

                            def tr(E):
                                for j in range(8):
                                    ins = E.transpose(out=psT[:, j, :], in_=sils[j][:, i * 128:(i + 1) * 128], identity=ident_bf[:, :])
                                return ins
                            S.op("pe", tr, sils[0:8] + [ident_bf], [psT])
                            xhs = xhs_r.next()
                            evac_copy(xhs[:, :].rearrange("p (j t) -> p j t", j=8), psT[:, :, :], [psT], [xhs])
                            S.dma("pool", xh_d[b, t0 + i * 128:t0 + (i + 1) * 128, :], xhs[:, :], [xhs], [])
                            psB = psB_r.next()

                            def tr2(E):
                                for j in range(2):
                                    ins = E.transpose(out=psB[:, j, :], in_=sils[8 + j][:, i * 128:(i + 1) * 128], identity=ident_bf[:, :])
                                return ins
                            S.op("pe", tr2, sils[8:10] + [ident_bf], [psB])
                            bts = bts_r.next()
                            evac_copy(bts[:, :].rearrange("p (j t) -> p j t", j=2), psB[:, :, :], [psB], [bts])
                            S.dma("pool", btm_d[b, t0 + i * 128:t0 + (i + 1) * 128, :], bts[:, :], [bts], [])
                        dtt = dtt_r.next()
                        S.dma("sp", dtt[:, 0:nt, :], dtraw[b, t0:t0 + N, :].rearrange("(i p) c -> p i c", p=128), [], [dtt])
                        v_, ax, ee, ll = dtw
                        bc = dtb_bc[:, :].unsqueeze(1).to_broadcast([128, nt, 32])
                        S.op("dve", lambda E: E.tensor_tensor(out=v_[:, 0:nt, :], in0=dtt[:, 0:nt, :], in1=bc, op=ALU.add), [dtt, dtb_bc], [v_])
                        S.op("act", lambda E: E.activation(out=ax[:, 0:nt, :], in_=v_[:, 0:nt, :], func=AF.Abs), [v_], [ax])
                        S.op("act", lambda E: E.activation(out=ee[:, 0:nt, :], in_=ax[:, 0:nt, :], func=AF.Exp, scale=-1.0), [ax], [ee])
                        S.op("dve", lambda E: E.tensor_scalar(out=ee[:, 0:nt, :], in0=ee[:, 0:nt, :], scalar1=1.0, scalar2=None, op0=ALU.add), [ee], [ee])
                        S.op("act", lambda E: E.activation(out=ll[:, 0:nt, :], in_=ee[:, 0:nt, :], func=AF.Ln), [ee], [ll])
                        dto = dto_r.next()
                        S.op("dve", lambda E: E.scalar_tensor_tensor(out=dto[:, 0:nt, 0:32], in0=v_[:, 0:nt, :], scalar=0.0, in1=ll[:, 0:nt, :],
                                                                     op0=ALU.max, op1=ALU.add), [v_, ll], [dto])
                        S.op("dve", lambda E: E.tensor_tensor(out=dto[:, 0:nt, 32:64], in0=dto[:, 0:nt, 0:32],
                                                             in1=aneg_bc[:, :].unsqueeze(1).to_broadcast([128, nt, 32]), op=ALU.mult),
                             [dto, aneg_bc], [dto])
                        S.dma("pool", dta[b, t0:t0 + N, :].rearrange("(i p) c -> p i c", p=128), dto[:, 0:nt, :], [dto], [])
                S.barrier()

            with ExitStack() as es:
                def mk_stream(sn):
                    B_ = {}
                    B_["xh"] = alloc(es, f"c{sn}_xh", [128, 1024], BF16, n=2)
                    B_["btm"] = alloc(es, f"c{sn}_btm", [128, 256], BF16, n=2)
                    B_["bT"] = alloc(es, f"c{sn}_bT", [128, 2, 128], BF16, n=2)
                    B_["cT"] = alloc(es, f"c{sn}_cT", [128, 2, 128], BF16, n=2)
                    B_["dta"] = alloc(es, f"c{sn}_dta", [128, 64], F32, n=2)
                    B_["acs"] = alloc(es, f"c{sn}_acs", [128, 48], F32)
                    B_["dec"] = alloc(es, f"c{sn}_dec", [128, 48], F32)
                    B_["nacs"] = alloc(es, f"c{sn}_nacs", [128, 16], F32)
                    B_["dtd"] = alloc(es, f"c{sn}_dtd", [128, 16], F32)
                    B_["rhsD"] = alloc(es, f"c{sn}_rhsD", [128, 16, 128], F32)
                    B_["ex"] = alloc(es, f"c{sn}_ex", [128, 4, 128], BF16, n=2)
                    B_["MT"] = alloc(es, f"c{sn}_MT", [128, 16, 128], BF16)
                    B_["cbT"] = alloc(es, f"c{sn}_cbT", [128, 2, 128], BF16)
                    B_["xdt"] = alloc(es, f"c{sn}_xdt", [128, 1024], BF16)
                    B_["xdd"] = alloc(es, f"c{sn}_xdd", [128, 1024], BF16)
                    B_["yo"] = alloc(es, f"c{sn}_yo", [128, 512], F32, n=2)
                    B_["y"] = alloc(es, f"c{sn}_y", [128, 1024], F32, n=2)
                    B_["stf"] = [alloc(es, f"c{sn}_stf{g}_", [128, 512], F32) for g in range(2)]
                    B_["stt"] = alloc(es, f"c{sn}_stt", [128, 512], F32, n=2)
                    B_["stb"] = [alloc(es, f"c{sn}_stb{g}_", [128, 512], BF16) for g in range(2)]
                    B_["psS"] = palloc(es, f"c{sn}_psS", [128, 512], F32)
                    B_["psD"] = palloc(es, f"c{sn}_psD", [128, 4, 128], F32)
                    B_["psY"] = palloc(es, f"c{sn}_psY", [128, 512], F32)
                    B_["psO"] = palloc(es, f"c{sn}_psO", [128, 512], F32)
                    return B_
                streams = [mk_stream(0), mk_stream(1)]
                print("scan sbuf remaining", nc.sbuf_bytes_remaining)

                STOPAT = int(os.environ.get("STOPAT", "99"))

                def scan_stream(b, dr, B_):
                    psS, psD, psY, psO = B_["psS"], B_["psD"], B_["psY"], B_["psO"]
                    stf, stb = B_["stf"], B_["stb"]
                    if dr == 0:
                        order = list(range(NCH))
                        ydst = yf_d
                    else:
                        order = list(range(NCC - 1, -1, -1)) + list(range(NCH - 1, NCC - 1, -1))
                        ydst = yb_d
                    for g in range(2):
                        S.op("pool", lambda E: E.memset(stf[g][:, :], 0.0), [], [stf[g]])
                        S.op("pool", lambda E: E.memset(stb[g][:, :], 0.0), [], [stb[g]])
                    yield
                    for c in order:
                        tk = slice(c * 128, (c + 1) * 128)
                        xh = B_["xh"].next()
                        S.dma("sp", xh[:, :], xh_d[b, tk, :], [], [xh])
                        btm = B_["btm"].next()
                        S.dma("sp", btm[:, :], btm_d[b, tk, :], [], [btm])
                        bT = B_["bT"].next()
                        S.dma("sp", bT[:, :, :], bT_d[b, :, tk].rearrange("(g p) t -> p g t", p=128), [], [bT])
                        cT = B_["cT"].next()
                        S.dma("sp", cT[:, :, :], cT_d[b, :, tk].rearrange("(g p) t -> p g t", p=128), [], [cT])
                        da = B_["dta"].next()
                        S.dma("sp", da[:, :], dta[b, tk, :], [], [da])
                        dt_d = da[:, dr * 16:(dr + 1) * 16]
                        a_d = da[:, 32 + dr * 16:32 + (dr + 1) * 16]

                        def cs_mm(E):
                            E.matmul(psS[:, 0:16], tri[dr], a_d, start=True, stop=True)
                            E.matmul(psS[:, 16:32], ones_f, a_d, start=True, stop=True)
                            for g in range(2):
                                ins = E.matmul(psS[:, 256 + g * 128:256 + (g + 1) * 128], bT[:, g, :], cT[:, g, :], start=True, stop=True)
                            return ins
                        S.op("pe", cs_mm, [kc_f, da, bT, cT], [psS])
                        rhsD = B_["rhsD"]
                        S.op("pool", lambda E: E.tensor_tensor(out=rhsD[:, :, :], in0=a_d.unsqueeze(2).to_broadcast([128, 16, 128]),
                                                              in1=tri[dr].unsqueeze(1).to_broadcast([128, 16, 128]), op=ALU.mult), [da, kc_f], [rhsD])
                        yield
                        if STOPAT < 1:
                            continue
                        acs = B_["acs"]
                        S.op("act", lambda E: E.activation(out=acs[:, 0:32], in_=psS[:, 0:32], func=AF.Copy), [psS], [acs])
                        cbT = B_["cbT"]
                        S.op("act", lambda E: E.activation(out=cbT[:, :, :], in_=psS[:, 256:512].rearrange("p (g l) -> p g l", g=2), func=AF.Copy), [psS], [cbT])
                        yield
                        S.op("dve", lambda E: E.tensor_tensor(out=acs[:, 32:48], in0=acs[:, 16:32], in1=acs[:, 0:16], op=ALU.subtract), [acs], [acs])
                        nacs = B_["nacs"]
                        S.op("pool", lambda E: E.tensor_scalar(out=nacs[:, :], in0=acs[:, 0:16], scalar1=-1.0, scalar2=None, op0=ALU.mult), [acs], [nacs])
                        yield
                        dec = B_["dec"]
                        S.op("act", lambda E: E.activation(out=dec[:, :], in_=acs[:, :], func=AF.Exp), [acs], [dec])
                        xdt = B_["xdt"]
                        S.op("dve", lambda E: E.tensor_tensor(out=xdt[:, :].rearrange("p (h e) -> p h e", h=16),
                                                             in0=xh[:, :].rearrange("p (h e) -> p h e", h=16),
                                                             in1=dt_d.unsqueeze(2).to_broadcast([128, 16, 64]), op=ALU.mult), [xh, da], [xdt])
                        yield
                        dtd = B_["dtd"]
                        S.op("dve", lambda E: E.tensor_tensor(out=dtd[:, :], in0=dt_d, in1=dec[:, 32:48], op=ALU.mult), [da, dec], [dtd])
                        yield
                        xdd = B_["xdd"]
                        S.op("pool", lambda E: E.tensor_tensor(out=xdd[:, :].rearrange("p (h e) -> p h e", h=16),
                                                              in0=xh[:, :].rearrange("p (h e) -> p h e", h=16),
                                                              in1=dtd[:, :].unsqueeze(2).to_broadcast([128, 16, 64]), op=ALU.mult), [xh, dtd], [xdd])
                        MT = B_["MT"]
                        y = B_["y"].next()
                        if STOPAT < 2:
                            continue
                        for g in range(2):
                            for q4 in range(2):
                                h0 = g * 8 + q4 * 4

                                def d_mm(E):
                                    E.matmul(psD[:, :, :], ones_f, rhsD[:, h0:h0 + 4, :], start=True, stop=False)
                                    return E.matmul(psD[:, :, :], ident_bf[:, :], neg_bf[:, dr, :, :], start=False, stop=True)
                                S.op("pe", d_mm, [rhsD, kc_f, ident_bf, neg_bf], [psD])
                                yield
                                ex = B_["ex"].next()
                                for h4 in range(4):
                                    h = h0 + h4
                                    S.op("act", lambda E: E.activation(out=ex[:, h4, :], in_=psD[:, h4, :], func=AF.Exp, bias=nacs[:, h:h + 1], scale=1.0),
                                         [psD, nacs], [ex])
                                yield
                                S.op("dve", lambda E: E.tensor_tensor(out=MT[:, h0:h0 + 4, :], in0=ex[:, :, :],
                                                                     in1=cbT[:, g, :].unsqueeze(1).to_broadcast([128, 4, 128]), op=ALU.mult), [ex, cbT], [MT])
                                yield
                            if STOPAT < 3:
                                continue

                            def yd_mm(E):
                                for h8 in range(8):
                                    h = g * 8 + h8
                                    ins = E.matmul(psY[:, h8 * 64:(h8 + 1) * 64], MT[:, h, :], xdt[:, h * 64:(h + 1) * 64], start=True, stop=True)
                                return ins
                            S.op("pe", yd_mm, [MT, xdt], [psY])
                            S.op("pe", lambda E: E.matmul(psO[:, :], cT[:, g, :], stb[g][:, :], start=True, stop=True), [cT, stb[g]], [psO])
                            yield
                            if STOPAT < 4:
                                continue
                            yo = B_["yo"].next()
                            S.op("dve", lambda E: E.tensor_tensor(out=yo[:, :].rearrange("p (h e) -> p h e", h=8),
                                                                 in0=psO[:, :].rearrange("p (h e) -> p h e", h=8),
                                                                 in1=dec[:, g * 8:(g + 1) * 8].unsqueeze(2).to_broadcast([128, 8, 64]), op=ALU.mult),
                                 [psO, dec], [yo])
                            yield
                            S.op("dve", lambda E: E.tensor_tensor(out=y[:, g * 512:(g + 1) * 512], in0=psY[:, :], in1=yo[:, :], op=ALU.add), [psY, yo], [y])
                            S.op("pe", lambda E: E.matmul(psO[:, :], btm[:, g * 128:(g + 1) * 128], xdd[:, g * 512:(g + 1) * 512], start=True, stop=True),
                                 [btm, xdd], [psO])
                            stt = B_["stt"].next()
                            S.op("pool", lambda E: E.tensor_tensor(out=stt[:, :].rearrange("p (h e) -> p h e", h=8),
                                                                  in0=stf[g][:, :].rearrange("p (h e) -> p h e", h=8),
                                                                  in1=dec[:, 16 + g * 8:16 + (g + 1) * 8].unsqueeze(2).to_broadcast([128, 8, 64]), op=ALU.mult),
                                 [stf[g], dec], [stt])
                            yield
                            S.op("dve", lambda E: E.tensor_tensor(out=stf[g][:, :], in0=psO[:, :], in1=stt[:, :], op=ALU.add), [psO, stt], [stf[g]])
                            yield
                            S.op("act", lambda E: E.activation(out=stb[g][:, :], in_=stf[g][:, :], func=AF.Copy), [stf[g]], [stb[g]])
                            yield
                        if not (last and c < NCC) and STOPAT > 5:
                            S.dma("act", ydst[b, tk, :], y[:, :], [y], [])
                        yield

                for b in range(NB if not os.environ.get("SKIP_SCAN") else 0):
                    gens = [scan_stream(b, 0, streams[0]), scan_stream(b, 1, streams[1])]
                    alive = [True, True]
                    if os.environ.get("SEQ_SCAN"):
                        for gen in gens:
                            for _ in gen:
                                pass
                        alive = [False, False]
                    while any(alive):
                        for gi, gen in enumerate(gens):
                            if alive[gi]:
                                try:
                                    next(gen)
                                except StopIteration:
                                    alive[gi] = False
                S.barrier()

            with ExitStack() as es:
                dsk_bc = alloc(es, "c_dsk", [128, 16], F32, const=True)
                sng_bc = alloc(es, "c_sng", [128, 1024], F32, const=True)
                S.dma("sp", dsk_bc[:, :], ssm_d[L:L + 1, :].partition_broadcast(128), [], [dsk_bc])
                S.dma("sp", sng_bc[:, :], ssm_norm_g[L:L + 1, :].partition_broadcast(128), [], [sng_bc])
                xh_r = alloc(es, "e_xh", [128, 1024], BF16, n=4)
                yf_r = alloc(es, "e_yf", [128, 1024], F32, n=4)
                yb_r = alloc(es, "e_yb", [128, 1024], F32, n=4)
                zs_r = alloc(es, "e_zs", [128, 1024], BF16, n=4)
                xd_r = alloc(es, "e_xd", [128, 1024], F32, n=4)
                yn_r = alloc(es, "e_yn", [128, 1024], BF16, n=4)
                ybT_r = alloc(es, "e_ybT", [128, 8, 128], BF16, n=4)
                junk = alloc(es, "e_junk", [128, 512], BF16)
                ms_r = alloc(es, "e_ms", [128, 2], F32, n=4)
                rs_r = alloc(es, "e_rs", [128, 2], F32, n=4)
                psT_r = palloc(es, "e_psT", [128, 8, 128], BF16, n=2)
                for b in range(NB if not os.environ.get("SKIP_OUT") else 0):
                    for c in (range(NCC, NCH) if last else range(NCH)):
                        tk = slice(c * 128, (c + 1) * 128)
                        xh = xh_r.next()
                        S.dma("sp", xh[:, :], xh_d[b, tk, :], [], [xh])
                        yf = yf_r.next()
                        S.dma("sp", yf[:, :], yf_d[b, tk, :], [], [yf])
                        yb = yb_r.next()
                        S.dma("sp", yb[:, :], yb_d[b, tk, :], [], [yb])
                        zt = zs_r.next()
                        S.dma("sp", zt[:, :], zs[b, tk, :], [], [zt])
                        xd = xd_r.next()
                        S.op("pool", lambda E: E.tensor_tensor(out=xd[:, :].rearrange("p (h e) -> p h e", h=16),
                                                              in0=xh[:, :].rearrange("p (h e) -> p h e", h=16),
                                                              in1=dsk_bc[:, :].unsqueeze(2).to_broadcast([128, 16, 64]), op=ALU.mult), [xh, dsk_bc], [xd])
                        S.op("dve", lambda E: E.tensor_tensor(out=yf[:, :], in0=yf[:, :], in1=yb[:, :], op=ALU.add), [yf, yb], [yf])
                        S.op("pool", lambda E: E.tensor_tensor(out=xd[:, :], in0=xd[:, :], in1=yf[:, :], op=ALU.add), [yf, xd], [xd])
                        S.op("dve", lambda E: E.tensor_tensor(out=yb[:, :], in0=xd[:, :], in1=zt[:, :], op=ALU.mult), [xd, zt], [yb])
                        ms = ms_r.next()
                        rs = rs_r.next()
                        for g in range(2):
                            S.op("act", lambda E: E.activation(out=junk[:, :], in_=yb[:, g * 512:(g + 1) * 512], func=AF.Square,
                                                               scale=1.0 / math.sqrt(512.0), accum_out=ms[:, g:g + 1]), [yb], [junk, ms])
                        rstd_from_ms(ms, rs, 2)
                        yn = yn_r.next()
                        for g in range(2):
                            S.op("dve", lambda E: E.scalar_tensor_tensor(out=yn[:, g * 512:(g + 1) * 512], in0=yb[:, g * 512:(g + 1) * 512],
                                                                         scalar=rs[:, g:g + 1], in1=sng_bc[:, g * 512:(g + 1) * 512],
                                                                         op0=ALU.mult, op1=ALU.mult), [yb, rs, sng_bc], [yn])
                        psT = psT_r.next()

                        def tr(E):
                            for j in range(8):
                                ins = E.transpose(out=psT[:, j, :], in_=yn[:, j * 128:(j + 1) * 128], identity=ident_bf[:, :])
                            return ins
                        S.op("pe", tr, [yn, ident_bf], [psT])
                        ybT = ybT_r.next()
                        S.op("act", lambda E: E.activation(out=ybT[:, :, :], in_=psT[:, :, :], func=AF.Copy), [psT], [ybT])
                        S.dma("act", catT[b, 512:1536, tk].rearrange("(j p) t -> p j t", p=128), ybT[:, :, :], [ybT], [])
                S.barrier()

            with ExitStack() as es:
                lp = alloc(es, "t_lp", [128, 256], F32)
                lw = alloc(es, "t_lw", [128, 128], F32)
                lsum = alloc(es, "t_ls", [128, 2], F32)
                nlam = alloc(es, "t_nlam", [128, 1], F32, const=True)
                sgc = alloc(es, "t_sgc", [128, 1], F32, const=True)
                S.dma("sp", lp[:, :], lam_p[L:L + 1, :].partition_broadcast(128), [], [lp])
                S.dma("sp", sgc[:, :], subln_g[L].rearrange("(e o) -> e o", o=1), [], [sgc])
                S.op("dve", lambda E: E.tensor_scalar(out=sgc[:, :], in0=sgc[:, :], scalar1=1.0 - lam_init, scalar2=None, op0=ALU.mult),
                     [sgc], [sgc])
                for i2 in range(2):
                    S.op("dve", lambda E: E.tensor_tensor(out=lw[:, i2 * 64:(i2 + 1) * 64], in0=lp[:, i2 * 128:i2 * 128 + 64],
                                                         in1=lp[:, i2 * 128 + 64:i2 * 128 + 128], op=ALU.mult), [lp], [lw])
                    S.op("act", lambda E: E.activation(out=lp[:, i2 * 64:(i2 + 1) * 64], in_=lw[:, i2 * 64:(i2 + 1) * 64], func=AF.Copy,
                                                       accum_out=lsum[:, i2:i2 + 1]), [lw], [lp, lsum])
                S.op("act", lambda E: E.activation(out=lsum[:, :], in_=lsum[:, :], func=AF.Exp), [lsum], [lsum])
                S.op("dve", lambda E: E.tensor_tensor(out=nlam[:, :], in0=lsum[:, 1:2], in1=lsum[:, 0:1], op=ALU.subtract), [lsum], [nlam])
                S.op("dve", lambda E: E.tensor_scalar(out=nlam[:, :], in0=nlam[:, :], scalar1=-lam_init, scalar2=None, op0=ALU.add), [nlam], [nlam])
                kTh_r = alloc(es, "t_kT", [128, T], BF16, n=2)
                vh_r = alloc(es, "t_vh", [128, NCH, 128], BF16, n=2)
                qz_r = alloc(es, "t_qz", [128, 2, 512], BF16, n=2)
                for qb in qz_r.bufs:
                    S.op("pool", lambda E: E.memset(qb[:, :, :], 0.0), [], [qb])
                gc_r = alloc(es, "t_gc", [128, 512], BF16, n=3)
                pT_r = alloc(es, "t_pT", [128, 2, 512], BF16, n=5)
                pac1_r = alloc(es, "t_pac1", [128, 512], F32, n=2)
                den0_r = alloc(es, "t_den0", [128, 512], F32, n=2)
                ones_bf = alloc(es, "t_ones", [128, 128], BF16, const=True)
                S.op("dve", lambda E: E.tensor_copy(out=ones_bf[:, :], in_=ones_f), [kc_f], [ones_bf])
                fin_pending = [None]
                osb_r = alloc(es, "t_osb", [128, 2, 512], F32, n=3)
                rden_r = alloc(es, "t_rden", [128, 2, 512], F32, n=1)
                o_r = alloc(es, "t_o", [128, 512], F32, n=2)
                sq_r = alloc(es, "t_sq", [128, 512], F32, n=2)
                rsd_r = alloc(es, "t_rsd", [128, 512], F32, n=2)
                ycT_r = alloc(es, "t_ycT", [128, 512], BF16, n=2)
                psS_r = palloc(es, "t_psS", [128, 2, 512], F32, n=2)
                psO = palloc(es, "t_psO", [128, 2, 512], F32)
                psD = palloc(es, "t_psD", [128, 512], F32)
                psDen = palloc(es, "t_psDen", [128, 512], F32)
                print("attn sbuf remaining", nc.sbuf_bytes_remaining)
                for b in range(NB):
                    for h in range(4):
                        kTh = kTh_r.next()
                        S.dma("sp", kTh[:, :], kT[b, h * 128:(h + 1) * 128, :], [], [kTh])
                        vh = vh_r.next()
                        S.dma("sp", vh[:, :, :], vtok[b, :, h * 128:(h + 1) * 128].rearrange("(c p) e -> p c e", p=128), [], [vh])
                        qgroups = [(NCTX + g * 512, min(512, NLAT - g * 512), 0, NCH) for g in range((NLAT + 511) // 512)]
                        if not last:
                            qgroups = [(0, NCTX, 0, NCC)] + qgroups
                        for (t0, N, k0, k1) in qgroups:
                            qz = qz_r.next()
                            S.dma("sp", qz[0:64, 0, 0:N], qT[b, h * 128:h * 128 + 64, t0:t0 + N], [], [qz])
                            S.dma("sp", qz[64:128, 1, 0:N], qT[b, h * 128 + 64:(h + 1) * 128, t0:t0 + N], [], [qz])
                            gc = gc_r.next()
                            S.dma("sp", gc[:, 0:N], gcT[b, h * 128:(h + 1) * 128, t0:t0 + N], [], [gc])
                            pac1 = pac1_r.next()
                            pend = []
                            for idx in range(k0, k1 + 1):
                                if idx < k1:
                                    ps = psS_r.next()

                                    def smm(E):
                                        for i2 in range(2):
                                            ins = E.matmul(ps[:, i2, 0:N], kTh[:, idx * 128:(idx + 1) * 128], qz[:, i2, 0:N], start=True, stop=True)
                                        return ins
                                    S.op("pe", smm, [kTh, qz], [ps])
                                    pend.append(ps)
                                if idx > k0:
                                    kt = idx - 1
                                    ps = pend.pop(0)
                                    pT = pT_r.next()
                                    S.op("act", lambda E: E.activation(out=pT[:, :, 0:N], in_=ps[:, :, 0:N], func=AF.Exp, scale=0.125), [ps], [pT])

                                    def pv(E):
                                        for i2 in range(2):
                                            ins = E.matmul(psO[:, i2, 0:N], vh[:, kt, :], pT[:, i2, 0:N], start=(kt == k0), stop=(kt == k1 - 1))
                                        return ins
                                    S.op("pe", pv, [pT, vh], [psO])
                                    S.op("pe", lambda E: E.matmul(psDen[:, 0:N], ones_bf[:, :], pT[:, 0, 0:N], start=(kt == k0), stop=(kt == k1 - 1)),
                                         [pT, ones_bf], [psDen])
                                    if kt == k0:
                                        S.op("dve", lambda E: E.tensor_copy(out=pac1[:, 0:N], in_=pT[:, 1, 0:N]), [pT], [pac1])
                                    else:
                                        S.op("dve", lambda E: E.tensor_tensor(out=pac1[:, 0:N], in0=pac1[:, 0:N], in1=pT[:, 1, 0:N], op=ALU.add),
                                             [pT, pac1], [pac1])
                                    if fin_pending[0] is not None and (kt - k0) % 2 == 1:
                                        try:
                                            next(fin_pending[0])
                                        except StopIteration:
                                            fin_pending[0] = None
                            if fin_pending[0] is not None:
                                for _ in fin_pending[0]:
                                    pass
                                fin_pending[0] = None
                            osb = osb_r.next()
                            S.op("act", lambda E: E.activation(out=osb[:, 0, 0:N], in_=psO[:, 0, 0:N], func=AF.Copy), [psO], [osb])
                            S.op("dve", lambda E: E.tensor_copy(out=osb[:, 1, 0:N], in_=psO[:, 1, 0:N]), [psO], [osb])
                            den0 = den0_r.next()
                            S.op("act", lambda E: E.activation(out=den0[:, 0:N], in_=psDen[:, 0:N], func=AF.Copy), [psDen], [den0])

                            def fin(osb=osb, gc=gc, N=N, b=b, h=h, t0=t0, pac1=pac1, den0=den0):
                                S.op("pe", lambda E: E.matmul(psD[:, 0:N], ones_f, pac1[:, 0:N], start=True, stop=True), [kc_f, pac1], [psD])
                                yield
                                rden = rden_r
                                for q0 in range(0, N, 256):
                                    S.op("dve", lambda E: E.reciprocal(out=rden[:, 0, q0:q0 + 256], in_=den0[:, q0:q0 + 256]), [den0], [rden])
                                    yield
                                    S.op("dve", lambda E: E.reciprocal(out=rden[:, 1, q0:q0 + 256], in_=psD[:, q0:q0 + 256]), [psD], [rden])
                                    yield
                                S.op("pool", lambda E: E.tensor_tensor(out=osb[:, :, 0:N], in0=osb[:, :, 0:N], in1=rden[:, :, 0:N], op=ALU.mult),
                                     [osb, rden], [osb])
                                yield
                                o = o_r.next()
                                S.op("dve", lambda E: E.scalar_tensor_tensor(out=o[:, 0:N], in0=osb[:, 1, 0:N], scalar=nlam[:, 0:1], in1=osb[:, 0, 0:N],
                                                                             op0=ALU.mult, op1=ALU.add), [osb, nlam], [o])
                                yield
                                sq = sq_r.next()
                                S.op("pool", lambda E: E.tensor_tensor(out=sq[:, 0:N], in0=o[:, 0:N], in1=o[:, 0:N], op=ALU.mult), [o], [sq])
                                yield
                                S.op("pe", lambda E: E.matmul(psD[:, 0:N], ones_f, sq[:, 0:N], start=True, stop=True), [kc_f, sq], [psD])
                                yield
                                rsd = rsd_r.next()
                                S.op("dve", lambda E: E.tensor_scalar(out=rsd[:, 0:N], in0=psD[:, 0:N], scalar1=1.0 / 128.0, scalar2=EPS,
                                                                     op0=ALU.mult, op1=ALU.add), [psD], [rsd])
                                yield
                                S.op("act", lambda E: E.activation(out=rsd[:, 0:N], in_=rsd[:, 0:N], func=AF.Ln), [rsd], [rsd])
                                yield
                                S.op("act", lambda E: E.activation(out=rsd[:, 0:N], in_=rsd[:, 0:N], func=AF.Exp, scale=-0.5), [rsd], [rsd])
                                yield
                                S.op("pool", lambda E: E.tensor_tensor(out=o[:, 0:N], in0=o[:, 0:N], in1=rsd[:, 0:N], op=ALU.mult), [o, rsd], [o])
                                yield
                                ycT = ycT_r.next()
                                S.op("dve", lambda E: E.scalar_tensor_tensor(out=ycT[:, 0:N], in0=o[:, 0:N], scalar=sgc[:, 0:1], in1=gc[:, 0:N],
                                                                             op0=ALU.mult, op1=ALU.mult), [o, sgc, gc], [ycT])
                                yield
                                S.dma("pool", catT[b, 1536 + h * 128:1536 + (h + 1) * 128, t0:t0 + N], ycT[:, 0:N], [ycT], [])
                            fin_pending[0] = fin()
                if fin_pending[0] is not None:
                    for _ in fin_pending[0]:
                        pass
                    fin_pending[0] = None
                S.barrier()

            with ExitStack() as es:
                wo_bf = alloc(es, "o_wo", [128, 16, D], BF16)
                with ExitStack() as es2:
                    wst = alloc(es2, "o_wst", [128, D], F32, n=3)
                    for j in range(16):
                        st_ = wst.next()
                        S.dma("sp", st_[:, :], w_out[L, j * 128:(j + 1) * 128, :], [], [st_])
                        if j % 2:
                            S.op("dve", lambda E: E.tensor_copy(out=wo_bf[:, j, :], in_=st_[:, :]), [st_], [wo_bf])
                        else:
                            S.op("act", lambda E: E.activation(out=wo_bf[:, j, :], in_=st_[:, :], func=AF.Copy), [st_], [wo_bf])
                S.barrier()
                wo_bf.const = True
                gate_bc = alloc(es, "o_gate", [128, NB + 1, D], F32, const=True)
                for r in range(NB + 1):
                    S.dma("sp", gate_bc[:, r, :], modrow[L, r:r + 1, 2 * D:3 * D].partition_broadcast(128), [], [gate_bc])
                fng_bc = alloc(es, "o_fng", [128, D], F32, const=True)
                S.dma("sp", fng_bc[:, :], final_g[0:1, :].partition_broadcast(128), [], [fng_bc])
                cat_r = alloc(es, "o_cat", [128, 16, 128], BF16, n=2)
                xr_r = alloc(es, "o_xr", [128, D], F32, n=2)
                tmp_r = alloc(es, "o_tmp", [128, D], F32, n=2)
                xo_r = alloc(es, "o_xo", [128, D], F32, n=2)
                junk = alloc(es, "o_junk", [128, D], BF16)
                ms_r = alloc(es, "o_ms", [128, 2], F32, n=2)
                rs_r = alloc(es, "o_rs", [128, 2], F32, n=2)
                psO_r = palloc(es, "o_ps", [128, D], F32, n=2)
                for b in range(NB):
                    tiles = list(range(NCC, NCH)) if last else list(range(NCH))
                    for c in tiles:
                        tk = slice(c * 128, (c + 1) * 128)
                        r = NB if c < NCC else b
                        cat = cat_r.next()
                        S.dma("sp", cat[:, :, :], catT[b, :, tk].rearrange("(j p) t -> p j t", p=128), [], [cat])
                        xr = xr_r.next()
                        S.dma("sp", xr[:, :], src_rows(L, b, c * 128, 128), [], [xr])
                        ps = psO_r.next()

                        def mm(E):
                            for hf in range(2):
                                for j in range(16):
                                    ins = E.matmul(ps[:, hf * 512:(hf + 1) * 512], cat[:, j, :], wo_bf[:, j, hf * 512:(hf + 1) * 512],
                                                   start=(j == 0), stop=(j == 15))
                            return ins
                        S.op("pe", mm, [cat, wo_bf], [ps])
                        tmp = tmp_r.next()
                        S.op("dve", lambda E: E.tensor_tensor(out=tmp[:, :], in0=ps[:, :], in1=gate_bc[:, r, :], op=ALU.mult), [ps, gate_bc], [tmp])
                        xo = xo_r.next()
                        S.op("pool", lambda E: E.tensor_tensor(out=xo[:, :], in0=tmp[:, :], in1=xr[:, :], op=ALU.add), [tmp, xr], [xo])
                        if not last:
                            S.dma("pool", x1_d[b, tk, :], xo[:, :], [xo], [])
                        else:
                            ms = ms_r.next()
                            rs = rs_r.next()
                            S.op("act", lambda E: E.activation(out=junk[:, :], in_=xo[:, :], func=AF.Square, scale=1.0 / 32.0,
                                                               accum_out=ms[:, 0:1]), [xo], [junk, ms])
                            rstd_from_ms(ms, rs, 1)
                            S.op("dve", lambda E: E.scalar_tensor_tensor(out=tmp[:, :], in0=xo[:, :], scalar=rs[:, 0:1], in1=fng_bc[:, :],
                                                                         op0=ALU.mult, op1=ALU.mult), [xo, rs, fng_bc], [tmp])
                            S.dma("pool", y_out[b, (c - NCC) * 128:(c - NCC + 1) * 128, :], tmp[:, :], [tmp], [])
                S.barrier()
        S.barrier()
    print("instr counts", S.cnt, "waits", S.nwait)
    return nc


def host_consts(NLAT):
    idx = np.arange(128)
    ident = np.eye(128, dtype=np.float32)
    triF = (idx[:, None] <= idx[None, :]).astype(np.float32)
    triB = (idx[:, None] >= idx[None, :]).astype(np.float32)
    ones = np.ones((128, 128), np.float32)
    negF = np.where(idx[None, :] < idx[:, None], NEG, 0.0).astype(np.float32)
    negB = np.where(idx[None, :] > idx[:, None], NEG, 0.0).astype(np.float32)
    k_const = np.stack([ident, triF, triB, ones, negF, negB]).astype(np.float32)
    R = np.zeros((128, 128), np.float32)
    for m in range(128):
        if m % 64 < 32:
            R[m, m + 32] = -1.0
        else:
            R[m, m - 32] = 1.0
    k_rt = np.ascontiguousarray(R.T)
    n = np.arange(NLAT)
    row = (n // 64).astype(np.float32)
    col = (n % 64).astype(np.float32)
    inv = (10000.0 ** (-np.arange(16, dtype=np.float32) / 16)).astype(np.float32)
    ang = np.concatenate([row[:, None] * inv, col[:, None] * inv], axis=-1).astype(np.float32)
    cos = np.cos(ang).astype(np.float32)
    sin = np.sin(ang).astype(np.float32)
    cosT = np.tile(cos.T, (4, 1))
    sinT = np.tile(sin.T, (4, 1))
    k_rope = np.ascontiguousarray(np.stack([cosT, sinT]).astype(np.float32))
    return k_const, k_rt, k_rope


def make_in_maps(inputs, n_cores, NB, NLAT):
    k_const, k_rt, k_rope = host_consts(NLAT)
    f = lambda a: np.ascontiguousarray(np.asarray(a, dtype=np.float32))
    shared = {
        "w_ada": f(inputs["w_ada"]), "b_ada": f(inputs["b_ada"]), "w_in": f(inputs["w_in"]), "w_out": f(inputs["w_out"]),
        "chunk_norm_g": f(inputs["chunk_norm_g"]), "chunk_ws": f(inputs["chunk_ws"]), "chunk_bs": f(inputs["chunk_bs"]),
        "ssm_conv_w": f(inputs["ssm_conv_w"]), "ssm_conv_b": f(inputs["ssm_conv_b"]),
        "ssm_dt_bias": f(inputs["ssm_dt_bias"]).reshape(DEPTH, 32), "ssm_a_log": f(inputs["ssm_a_log"]).reshape(DEPTH, 32),
        "ssm_d": f(inputs["ssm_d"]), "ssm_norm_g": f(inputs["ssm_norm_g"]),
        "diff_lambda_p": f(inputs["diff_lambda_p"]).reshape(DEPTH, 256), "diff_subln_g": f(inputs["diff_subln_g"]),
        "final_norm_g": f(inputs["final_norm_g"]).reshape(1, D),
        "k_const": k_const, "k_rt": k_rt, "k_rope": k_rope,
    }
    x = f(inputs["x"])
    ctx = f(inputs["ctx"])
    c = f(inputs["c"])
    c_ctx = f(inputs["c_ctx"]).reshape(1, D)
    maps = []
    for i in range(n_cores):
        m = dict(shared)
        m["x"] = np.ascontiguousarray(x[i * NB:(i + 1) * NB, :NLAT])
        m["ctx"] = np.ascontiguousarray(ctx[i * NB:(i + 1) * NB])
        m["cvec"] = np.ascontiguousarray(np.concatenate([c[i * NB:(i + 1) * NB], c_ctx], axis=0))
        maps.append(m)
    return maps


def kernel(**inputs):
    n_cores = 8
    NB = 2
    nc = build(NB, NCTX_FULL, NLAT_FULL)
    maps = make_in_maps(inputs, n_cores, NB, NLAT_FULL)
    res = run_bass_kernel_spmd(nc, maps, core_ids=list(range(n_cores)))
    return np.concatenate([np.asarray(r["y"], dtype=np.float32) for r in res.results], axis=0)
```

```python
import math
import os
from contextlib import ExitStack
import numpy as np
import concourse.bass as bass
import concourse.mybir as mybir
from concourse.bass_utils import run_bass_kernel_spmd

F32 = mybir.dt.float32
BF16 = mybir.dt.bfloat16
AF = mybir.ActivationFunctionType
ALU = mybir.AluOpType

D = 1024
NCTX_FULL = 256
NLAT_FULL = 4096
DEPTH = 2
EPS = 1e-6
DIN = 6176
C_XBC, C_DT, C_K, C_V, C_U, C_VM, C_GA, C_Z, C_Q, C_GC = 0, 1536, 1568, 2080, 2592, 3104, 3616, 4128, 5152, 5664
NEG = -30000.0
NRING = 8


class Buf:
    __slots__ = ("t", "w", "r", "const")

    def __init__(self, t, const=False):
        self.t = t
        self.w = None
        self.r = []
        self.const = const

    def __getitem__(self, k):
        return self.t[k]


class Rot:
    def __init__(self, bufs):
        self.bufs = bufs
        self.i = 0

    def next(self):
        b = self.bufs[self.i % len(self.bufs)]
        self.i += 1
        return b


class Sch:
    def __init__(self, nc):
        self.nc = nc
        self.eng = {"pe": nc.tensor, "act": nc.scalar, "dve": nc.vector, "pool": nc.gpsimd, "sp": nc.sync}
        self.sem = {k: nc.alloc_semaphore("s_" + k) for k in ("pe", "act", "dve", "pool")}
        self.cnt = {k: 0 for k in self.sem}
        self.seen = {k: {} for k in self.eng}
        self.rings = {q: [nc.alloc_semaphore(f"d_{q}{i}") for i in range(NRING)] for q in ("sp", "pool", "act")}
        self.rcnt = {q: [0] * NRING for q in self.rings}
        self.rpos = {q: 0 for q in self.rings}
        self.nwait = 0

    def _wait(self, e, ev):
        if ev is None:
            return
        key, sem, val = ev
        if e == "pe" and key == "pe":
            return
        if self.seen[e].get(key, 0) >= val:
            return
        self.eng[e].wait_ge(sem, val)
        self.seen[e][key] = val
        self.nwait += 1

    def _deps(self, e, reads, writes):
        for b in reads:
            self._wait(e, b.w)
        for b in writes:
            self._wait(e, b.w)
            for ev in b.r:
                self._wait(e, ev)

    def _mark(self, ev, reads, writes):
        for b in reads:
            if not b.const:
                b.r.append(ev)
        for b in writes:
            b.w = ev
            b.r = []

    def op(self, e, fn, reads=(), writes=()):
        self._deps(e, reads, writes)
        ins = fn(self.eng[e])
        self.cnt[e] += 1
        ins.then_inc(self.sem[e], 1)
        self._mark((e, self.sem[e], self.cnt[e]), reads, writes)

    def dma(self, q, out, in_, reads=(), writes=(), **kw):
        i = self.rpos[q]
        self.rpos[q] = (i + 1) % NRING
        sem = self.rings[q][i]
        if self.rcnt[q][i] > 0:
            self._wait(q, ((q, i), sem, self.rcnt[q][i]))
        self._deps(q, reads, writes)
        self.eng[q].dma_start(out=out, in_=in_, **kw).then_inc(sem, 16)
        self.rcnt[q][i] += 16
        self._mark(((q, i), sem, self.rcnt[q][i]), reads, writes)

    def barrier(self):
        evs = [(k, self.sem[k], self.cnt[k]) for k in self.sem if self.cnt[k] > 0]
        for q in self.rings:
            for i, s in enumerate(self.rings[q]):
                if self.rcnt[q][i] > 0:
                    evs.append(((q, i), s, self.rcnt[q][i]))
        for e in self.eng:
            for ev in evs:
                self._wait(e, ev)


def build(NB, NCTX, NLAT, depth=DEPTH, dbg=(), skip=()):
    T = NCTX + NLAT
    NCH = T // 128
    NCC = NCTX // 128
    nc = bass.Bass("TRN2", target_bir_lowering=False)
    S = Sch(nc)

    def din(name, shape, dt=F32):
        return nc.dram_tensor(name, list(shape), dt, kind="ExternalInput").ap()

    def dscr(name, shape, dt=F32):
        kind = "ExternalOutput" if name in dbg else "Internal"
        return nc.dram_tensor(name, list(shape), dt, kind=kind).ap()

    x_in = din("x", [NB, NLAT, D])
    ctx_in = din("ctx", [NB, NCTX, D])
    cvec = din("cvec", [NB + 1, D])
    w_ada = din("w_ada", [depth, D, 3 * D])
    b_ada = din("b_ada", [depth, 3 * D])
    w_in = din("w_in", [depth, D, DIN])
    w_out = din("w_out", [depth, 2048, D])
    chunk_norm_g = din("chunk_norm_g", [depth, 512])
    chunk_ws = din("chunk_ws", [depth, 4, 128, 128])
    chunk_bs = din("chunk_bs", [depth, 4, 128])
    conv_w = din("ssm_conv_w", [depth, 5, 1536])
    conv_b = din("ssm_conv_b", [depth, 1536])
    dt_bias = din("ssm_dt_bias", [depth, 32])
    a_log = din("ssm_a_log", [depth, 32])
    ssm_d = din("ssm_d", [depth, 16])
    ssm_norm_g = din("ssm_norm_g", [depth, 1024])
    lam_p = din("diff_lambda_p", [depth, 256])
    subln_g = din("diff_subln_g", [depth, 128])
    final_g = din("final_norm_g", [1, D])
    k_const = din("k_const", [6, 128, 128])
    k_rt = din("k_rt", [128, 128])
    k_rope = din("k_rope", [2, 128, NLAT])
    y_out = nc.dram_tensor("y", [NB, NLAT, D], F32, kind="ExternalOutput").ap()

    modrow = dscr("modrow", [depth, NB + 1, 3 * D])
    xbcT = dscr("xbcT", [NB, 1536, T], BF16)
    dtraw = dscr("dtraw", [NB, T, 32])
    dta = dscr("dta", [NB, T, 64])
    kT = dscr("kT", [NB, 512, T], BF16)
    qT = dscr("qT", [NB, 512, T], BF16)
    vtok = dscr("vtok", [NB, T, 512], BF16)
    zs = dscr("zs", [NB, T, 1024], BF16)
    gcT = dscr("gcT", [NB, 512, T], BF16)
    catT = dscr("catT", [NB, 2048, T], BF16)
    xh_d = dscr("xh", [NB, T, 1024], BF16)
    btm_d = dscr("btm", [NB, T, 256], BF16)
    bT_d = dscr("bT", [NB, 256, T], BF16)
    cT_d = dscr("cT", [NB, 256, T], BF16)
    yf_d = dscr("yf", [NB, T, 1024])
    yb_d = dscr("yb", [NB, T, 1024])
    x1_d = dscr("x1", [NB, T, D])

    uid = [0]

    def alloc(es, name, shape, dt, n=1, const=False):
        uid[0] += 1
        bufs = [Buf(es.enter_context(nc.sbuf_tensor(f"{name}{i}_{uid[0]}", list(shape), dt)), const) for i in range(n)]
        return bufs[0] if n == 1 else Rot(bufs)

    def palloc(es, name, shape, dt, n=1):
        uid[0] += 1
        bufs = [Buf(es.enter_context(nc.psum_tensor(f"{name}{i}_{uid[0]}", list(shape), dt))) for i in range(n)]
        return bufs[0] if n == 1 else Rot(bufs)

    def src_rows(L, b, t0, n):
        if L == 0:
            if t0 < NCTX:
                return ctx_in[b, t0:t0 + n, :]
            return x_in[b, t0 - NCTX:t0 - NCTX + n, :]
        return x1_d[b, t0:t0 + n, :]

    ev_ctr = [0]

    def evac_copy(out, in_, reads, writes):
        ev_ctr[0] += 1
        if ev_ctr[0] % 2:
            S.op("act", lambda E: E.activation(out=out, in_=in_, func=AF.Copy), reads, writes)
        else:
            S.op("dve", lambda E: E.tensor_copy(out=out, in_=in_), reads, writes)

    def rstd_from_ms(ms, rs, n):
        S.op("dve", lambda E: E.tensor_scalar(out=rs[:, 0:n], in0=ms[:, 0:n], scalar1=EPS, scalar2=None, op0=ALU.add), [ms], [rs])
        S.op("act", lambda E: E.activation(out=rs[:, 0:n], in_=rs[:, 0:n], func=AF.Sqrt), [rs], [rs])
        S.op("dve", lambda E: E.reciprocal(out=rs[:, 0:n], in_=rs[:, 0:n]), [rs], [rs])

    with ExitStack() as g_es:
        kc_f = alloc(g_es, "kc_f", [128, 6, 128], F32, const=True)
        ident_bf = alloc(g_es, "ident_bf", [128, 128], BF16, const=True)
        neg_bf = alloc(g_es, "neg_bf", [128, 2, 4, 128], BF16, const=True)
        rt_f = alloc(g_es, "rt_f", [128, 128], F32, const=True)
        S.dma("sp", kc_f[:, :, :], k_const.rearrange("c p n -> p c n"), [], [kc_f])
        S.dma("sp", rt_f[:, :], k_rt[:, :], [], [rt_f])
        S.op("dve", lambda E: E.tensor_copy(out=ident_bf[:, :], in_=kc_f[:, 0, :]), [kc_f], [ident_bf])
        for d_ in range(2):
            S.op("dve", lambda E: E.tensor_copy(out=neg_bf[:, d_, :, :], in_=kc_f[:, 4 + d_, :].unsqueeze(1).to_broadcast([128, 4, 128])),
                 [kc_f], [neg_bf])
        tri = [kc_f[:, 1, :], kc_f[:, 2, :]]
        ones_f = kc_f[:, 3, :]

        for L in range(depth):
            last = (L == depth - 1)
            lam_init = 0.8 - 0.6 * math.exp(-0.3 * L)
            S.barrier()
            with ExitStack() as es:
                cT_sb = alloc(es, "a_cT", [128, 8, NB + 1], F32)
                sc_sb = alloc(es, "a_sc", [128, 8, NB + 1], F32)
                wa = alloc(es, "a_wa", [128, 8, 512], F32, n=2)
                bab = alloc(es, "a_bab", [NB + 1, 3 * D], F32)
                mr = alloc(es, "a_mr", [NB + 1, 3 * D], F32)
                psa = palloc(es, "a_ps", [128, 512], F32, n=2)
                for r_ in range(NB + 1):
                    S.dma("sp", cT_sb[:, :, r_], cvec[r_].rearrange("(k p) -> p k", p=128), [], [cT_sb], allow_slow_non_contiguous=True)
                S.dma("sp", bab[:, :], b_ada[L:L + 1, :].partition_broadcast(NB + 1), [], [bab])
                S.op("act", lambda E: E.activation(out=sc_sb[:, :, :], in_=cT_sb[:, :, :], func=AF.Silu), [cT_sb], [sc_sb])
                for n in range(6):
                    w = wa.next()
                    S.dma("sp", w[:, :, :], w_ada[L, :, n * 512:(n + 1) * 512].rearrange("(k p) n -> p k n", p=128), [], [w])
                    ps = psa.next()

                    def mm(E):
                        for k in range(8):
                            ins = E.matmul(ps[0:NB + 1, :], sc_sb[:, k, :], w[:, k, :], start=(k == 0), stop=(k == 7))
                        return ins
                    S.op("pe", mm, [sc_sb, w], [ps])
                    S.op("dve", lambda E: E.tensor_tensor(out=mr[:, n * 512:(n + 1) * 512], in0=ps[0:NB + 1, :],
                                                         in1=bab[:, n * 512:(n + 1) * 512], op=ALU.add), [ps, bab], [mr])
                S.dma("pool", modrow[L], mr[:, :], [mr], [])
            S.barrier()

            with ExitStack() as es:
                win_bf = alloc(es, "win_bf", [128, 8, DIN], BF16)
                with ExitStack() as es2:
                    wst = alloc(es2, "wst", [128, 1544], F32, n=3)
                    ci = 0
                    for k in range(8):
                        for c4 in range(4):
                            st_ = wst.next()
                            c0 = c4 * 1544
                            S.dma("sp", st_[:, :], w_in[L, k * 128:(k + 1) * 128, c0:c0 + 1544], [], [st_])
                            e = ("act", "dve", "pool")[ci % 3]
                            ci += 1
                            if e == "act":
                                S.op("act", lambda E: E.activation(out=win_bf[:, k, c0:c0 + 1544], in_=st_[:, :], func=AF.Copy), [st_], [win_bf])
                            else:
                                S.op(e, lambda E: E.tensor_copy(out=win_bf[:, k, c0:c0 + 1544], in_=st_[:, :]), [st_], [win_bf])
                S.barrier()
                win_bf.const = True
                modT = alloc(es, "modT", [128, NB + 1, 24], F32, const=True)
                for r_ in range(NB + 1):
                    S.dma("sp", modT[:, r_, :], modrow[L, r_].rearrange("(j p) -> p j", p=128), [], [modT], allow_slow_non_contiguous=True)
                S.op("dve", lambda E: E.tensor_scalar(out=modT[:, :, 8:16], in0=modT[:, :, 8:16], scalar1=1.0, scalar2=None, op0=ALU.add),
                     [modT], [modT])
                cng_bc = alloc(es, "cng_bc", [128, 512], F32, const=True)
                S.dma("sp", cng_bc[:, :], chunk_norm_g[L:L + 1, :].partition_broadcast(128), [], [cng_bc])
                bs_bc = alloc(es, "bs_bc", [128, 512], F32, const=True)
                S.dma("sp", bs_bc[:, :], chunk_bs[L:L + 1].rearrange("o g t -> o (g t)").partition_broadcast(128), [], [bs_bc])
                wsT_f = alloc(es, "wsT_f", [128, 4, 128], F32)
                wsT = alloc(es, "wsT", [128, 4, 128], BF16, const=True)
                for g_ in range(4):
                    S.dma("sp", wsT_f[:, g_, :], chunk_ws[L, g_].rearrange("t s -> s t"), [], [wsT_f], allow_slow_non_contiguous=True)
                S.op("dve", lambda E: E.tensor_copy(out=wsT[:, :, :], in_=wsT_f[:, :, :]), [wsT_f], [wsT])

                xt_r = alloc(es, "p_xt", [128, D], F32, n=4)
                junk = alloc(es, "p_junk", [128, D], BF16)
                ms_r = alloc(es, "p_ms", [128, 4], F32, n=8)
                rs_r = alloc(es, "p_rs", [128, 4], F32, n=8)
                xn_r = alloc(es, "p_xn", [128, D], BF16, n=4)
                hT_r = alloc(es, "p_hT", [128, 8, 512], BF16, n=2)
                hTa_r = Rot([Buf(hT_r.bufs[i].t) for i in range(2)])
                hTd_r = Rot([Buf(hT_r.bufs[i].t) for i in range(2)])
                stf_r = alloc(es, "p_stf", [128, 512], F32, n=3)
                stb_r = alloc(es, "p_stb", [128, 512], BF16, n=3)
                cs_r = alloc(es, "p_cs", [128, 2, 512], F32, n=2)
                t1_r = alloc(es, "p_t1", [128, 512], F32, n=2)
                t2_r = alloc(es, "p_t2", [128, 512], F32, n=2)
                ug_r = alloc(es, "p_ug", [128, 4, 512], BF16, n=2)
                gu_r = alloc(es, "p_gu", [128, 512], F32, n=2)
                gv_r = alloc(es, "p_gv", [128, 512], F32, n=4)
                vn_r = alloc(es, "p_vn", [128, 512], BF16, n=4)
                tm_r = alloc(es, "p_tm", [128, 512], F32, n=2)
                ya_r = alloc(es, "p_ya", [128, 512], BF16, n=2)
                dts_r = alloc(es, "p_dts", [128, 32], F32, n=2)
                psT_r = palloc(es, "p_psT", [128, 8, 128], BF16, n=2)
                psM_r = palloc(es, "p_psM", [128, 512], F32, n=4)
                psX_r = palloc(es, "p_psX", [128, 512], F32, n=2)
                print("P1 sbuf remaining", nc.sbuf_bytes_remaining)

                def p1_group(b, t0, N):
                    if True:
                        is_ctx = t0 < NCTX
                        r = NB if is_ctx else b
                        ctx_only = is_ctx and last
                        nt = N // 128
                        hT = hT_r.next()
                        hTs = [hTa_r.next(), hTd_r.next()]
                        xns = []
                        xts = []
                        ms = ms_r.next()
                        rs = rs_r.next()
                        for i in range(nt):
                            xt = xt_r.next()
                            S.dma("sp", xt[:, :], src_rows(L, b, t0 + i * 128, 128), [], [xt])
                            S.op("act", lambda E: E.activation(out=junk[:, :], in_=xt[:, :], func=AF.Square, scale=1.0 / 32.0,
                                                               accum_out=ms[:, i:i + 1]), [xt], [junk, ms])
                            xts.append(xt)
                        rstd_from_ms(ms, rs, nt)
                        for i in range(nt):
                            xn = xn_r.next()
                            S.op("act", lambda E: E.activation(out=xn[:, :], in_=xts[i][:, :], func=AF.Identity, scale=rs[:, i:i + 1]), [xts[i], rs], [xn])
                            xns.append(xn)
                        yield
                        for i in range(nt):
                            xn = xns[i]
                            psT = psT_r.next()

                            def tr(E):
                                for k in range(8):
                                    ins = E.transpose(out=psT[:, k, :], in_=xn[:, k * 128:(k + 1) * 128], identity=ident_bf[:, :])
                                return ins
                            S.op("pe", tr, [xn, ident_bf], [psT])
                            for k in range(8):
                                if i % 2 == 0:
                                    S.op("act", lambda E: E.activation(out=hT[:, k, i * 128:(i + 1) * 128], in_=psT[:, k, :], func=AF.Identity,
                                                                       scale=modT[:, r, 8 + k:9 + k], bias=modT[:, r, k:k + 1]), [psT, modT], [hTs[0]])
                                else:
                                    S.op("dve", lambda E: E.tensor_scalar(out=hT[:, k, i * 128:(i + 1) * 128], in0=psT[:, k, :],
                                                                          scalar1=modT[:, r, 8 + k:9 + k], scalar2=modT[:, r, k:k + 1],
                                                                          op0=ALU.mult, op1=ALU.add), [psT, modT], [hTs[1]])
                        yield

                        def fm_mm(ps, col0):
                            def mm(E):
                                for k in range(8):
                                    ins = E.matmul(ps[:, 0:N], win_bf[:, k, col0:col0 + 128], hT[:, k, 0:N], start=(k == 0), stop=(k == 7))
                                return ins
                            S.op("pe", mm, [win_bf] + hTs, [ps])

                        def tm_mm(ps, i, col0, ncol):
                            def mm(E):
                                for k in range(8):
                                    ins = E.matmul(ps[:, 0:ncol], hT[:, k, i * 128:(i + 1) * 128], win_bf[:, k, col0:col0 + ncol],
                                                   start=(k == 0), stop=(k == 7))
                                return ins
                            S.op("pe", mm, [win_bf] + hTs, [ps])

                        for j in range(12):
                            ps = psM_r.next()
                            fm_mm(ps, C_XBC + j * 128)
                            st_ = stb_r.next()
                            evac_copy(st_[:, 0:N], ps[:, 0:N], [ps], [st_])
                            S.dma("pool", xbcT[b, j * 128:(j + 1) * 128, t0:t0 + N], st_[:, 0:N], [st_], [])
                        if not is_ctx:
                            cs = cs_r.next()
                            S.dma("sp", cs[:, :, 0:N], k_rope[:, :, t0 - NCTX:t0 - NCTX + N].rearrange("c p n -> p c n"), [], [cs])
                        fams = [(C_K, kT)] if ctx_only else [(C_K, kT), (C_Q, qT)]
                        tiles = [(c0, dst, j) for (c0, dst) in fams for j in range(4)]
                        pend = []
                        for ti in range(len(tiles) + 1):
                            if ti < len(tiles):
                                c0, dst, j = tiles[ti]
                                ps = psM_r.next()
                                fm_mm(ps, c0 + j * 128)
                                if is_ctx:
                                    sb = stb_r.next()
                                    evac_copy(sb[:, 0:N], ps[:, 0:N], [ps], [sb])
                                    S.dma("pool", dst[b, j * 128:(j + 1) * 128, t0:t0 + N], sb[:, 0:N], [sb], [])
                                else:
                                    qf = stf_r.next()
                                    S.op("act", lambda E: E.activation(out=qf[:, 0:N], in_=ps[:, 0:N], func=AF.Copy), [ps], [qf])
                                    pend.append((qf, dst, j))
                            if ti >= 1 and not is_ctx:
                                qf, dst, j = pend.pop(0)
                                sb = stb_r.next()
                                psr = psX_r.next()
                                S.op("pe", lambda E: E.matmul(psr[:, 0:N], rt_f[:, :], qf[:, 0:N], start=True, stop=True), [qf, rt_f], [psr])
                                t1 = t1_r.next()
                                t2 = t2_r.next()
                                S.op("pool", lambda E: E.tensor_tensor(out=t1[:, 0:N], in0=qf[:, 0:N], in1=cs[:, 0, 0:N], op=ALU.mult),
                                     [qf, cs], [t1])
                                S.op("dve", lambda E: E.tensor_tensor(out=t2[:, 0:N], in0=psr[:, 0:N], in1=cs[:, 1, 0:N], op=ALU.mult),
                                     [psr, cs], [t2])
                                S.op("pool", lambda E: E.tensor_tensor(out=sb[:, 0:N], in0=t1[:, 0:N], in1=t2[:, 0:N], op=ALU.add),
                                     [t1, t2], [sb])
                                S.dma("pool", dst[b, j * 128:(j + 1) * 128, t0:t0 + N], sb[:, 0:N], [sb], [])
                        yield
                        for i in range(nt):
                            ps = psX_r.next()
                            tm_mm(ps, i, C_DT, 32)
                            ds = dts_r.next()
                            evac_copy(ds[:, :], ps[:, 0:32], [ps], [ds])
                            S.dma("pool", dtraw[b, t0 + i * 128:t0 + (i + 1) * 128, :], ds[:, :], [ds], [])
                        for i in range(nt):
                            ps = psM_r.next()
                            tm_mm(ps, i, C_V, 512)
                            sb = stb_r.next()
                            evac_copy(sb[:, :], ps[:, :], [ps], [sb])
                            S.dma("pool", vtok[b, t0 + i * 128:t0 + (i + 1) * 128, :], sb[:, :], [sb], [])
                        if ctx_only:
                            return
                        ug = ug_r.next()
                        for j in range(4):
                            ps = psM_r.next()
                            fm_mm(ps, C_U + j * 128)
                            gu = gu_r.next()
                            S.op("act", lambda E: E.activation(out=gu[:, 0:N], in_=ps[:, 0:N], func=AF.Gelu_apprx_tanh), [ps], [gu])
                            ps2 = psM_r.next()
                            fm_mm(ps2, C_GA + j * 128)
                            sg = stf_r.next()
                            S.op("act", lambda E: E.activation(out=sg[:, 0:N], in_=ps2[:, 0:N], func=AF.Silu), [ps2], [sg])
                            S.op("pool", lambda E: E.tensor_tensor(out=ug[:, j, 0:N], in0=gu[:, 0:N], in1=sg[:, 0:N], op=ALU.mult), [gu, sg], [ug])
                        for j in range(4):
                            ps = psM_r.next()
                            fm_mm(ps, C_GC + j * 128)
                            sb = stb_r.next()
                            S.op("act", lambda E: E.activation(out=sb[:, 0:N], in_=ps[:, 0:N], func=AF.Silu), [ps], [sb])
                            S.dma("act", gcT[b, j * 128:(j + 1) * 128, t0:t0 + N], sb[:, 0:N], [sb], [])
                        vns = []
                        ms = ms_r.next()
                        rs = rs_r.next()
                        gvs = []
                        for i in range(nt):
                            ps = psM_r.next()
                            tm_mm(ps, i, C_VM, 512)
                            gv = gv_r.next()
                            S.op("act", lambda E: E.activation(out=gv[:, :], in_=ps[:, :], func=AF.Gelu_apprx_tanh), [ps], [gv])
                            S.op("act", lambda E: E.activation(out=junk[:, 0:512], in_=gv[:, :], func=AF.Square, scale=1.0 / math.sqrt(512.0),
                                                               accum_out=ms[:, i:i + 1]), [gv], [junk, ms])
                            gvs.append(gv)
                        rstd_from_ms(ms, rs, nt)
                        for i in range(nt):
                            vn = vn_r.next()
                            S.op("dve", lambda E: E.scalar_tensor_tensor(out=vn[:, :], in0=gvs[i][:, :], scalar=rs[:, i:i + 1], in1=cng_bc[:, :],
                                                                         op0=ALU.mult, op1=ALU.mult), [gvs[i], rs, cng_bc], [vn])
                            vns.append(vn)
                        for i in range(nt):
                            for (c0, dst, dc0) in ((C_Z, zs, 0), (C_Z + 512, zs, 512)):
                                ps = psM_r.next()
                                tm_mm(ps, i, c0, 512)
                                sb = stb_r.next()
                                S.op("act", lambda E: E.activation(out=sb[:, :], in_=ps[:, :], func=AF.Silu), [ps], [sb])
                                S.dma("act", dst[b, t0 + i * 128:t0 + (i + 1) * 128, dc0:dc0 + 512], sb[:, :], [sb], [])
                        for i in range(nt):
                            vn = vns[i]
                            psm = psX_r.next()

                            def mix(E):
                                for g in range(4):
                                    ins = E.matmul(psm[:, g * 128:(g + 1) * 128], vn[:, g * 128:(g + 1) * 128], wsT[:, g, :], start=True, stop=True)
                                return ins
                            S.op("pe", mix, [vn, wsT], [psm])
                            tm = tm_r.next()
                            S.op("dve", lambda E: E.tensor_tensor(out=tm[:, :], in0=psm[:, :], in1=bs_bc[:, :], op=ALU.add), [psm, bs_bc], [tm])
                            ya = ya_r.next()
                            S.op("pool", lambda E: E.tensor_tensor(out=ya[:, :].rearrange("p (g t) -> p g t", g=4),
                                                                   in0=tm[:, :].rearrange("p (g t) -> p g t", g=4),
                                                                   in1=ug[:, :, i * 128:(i + 1) * 128], op=ALU.mult), [tm, ug], [ya])
                            S.dma("pool", catT[b, 0:512, t0 + i * 128:t0 + (i + 1) * 128].rearrange("(g p) t -> p g t", p=128),
                                  ya[:, :].rearrange("p (g t) -> p g t", g=4), [ya], [])

                glist = []
                for b in range(NB):
                    glist.append(p1_group(b, 0, NCTX))
                    for g in range((NLAT + 511) // 512):
                        glist.append(p1_group(b, NCTX + g * 512, min(512, NLAT - g * 512)))
                next(glist[0])
                next(glist[0])
                for gi in range(len(glist)):
                    if gi + 1 < len(glist):
                        next(glist[gi + 1])
                    next(glist[gi])
                    if gi + 1 < len(glist):
                        next(glist[gi + 1])
                    for _ in glist[gi]:
                        pass
                S.barrier()

            with ExitStack() as es:
                cw = alloc(es, "s_cw", [128, 12, 5], F32, const=True)
                cb = alloc(es, "s_cb", [128, 12], F32, const=True)
                for k_ in range(5):
                    S.dma("sp", cw[:, :, k_], conv_w[L, k_].rearrange("(j p) -> p j", p=128), [], [cw], allow_slow_non_contiguous=True)
                S.dma("sp", cb[:, :], conv_b[L].rearrange("(j p) -> p j", p=128), [], [cb], allow_slow_non_contiguous=True)
                dtb_bc = alloc(es, "s_dtb", [128, 32], F32, const=True)
                aneg_bc = alloc(es, "s_aneg", [128, 32], F32, const=True)
                S.dma("sp", dtb_bc[:, :], dt_bias[L:L + 1, :].partition_broadcast(128), [], [dtb_bc])
                S.dma("sp", aneg_bc[:, :], a_log[L:L + 1, :].partition_broadcast(128), [], [aneg_bc])
                S.op("act", lambda E: E.activation(out=aneg_bc[:, :], in_=aneg_bc[:, :], func=AF.Exp), [aneg_bc], [aneg_bc])
                S.op("dve", lambda E: E.tensor_scalar(out=aneg_bc[:, :], in0=aneg_bc[:, :], scalar1=-1.0, scalar2=None, op0=ALU.mult),
                     [aneg_bc], [aneg_bc])
                raw_r = alloc(es, "s_raw", [128, 12, 516], BF16, n=2)
                dg = alloc(es, "s_dg", [128, 12, 5, 128], BF16, const=True)
                for j in range(12):
                    for k in range(5):
                        S.op("dve", lambda E: E.tensor_scalar(out=dg[:, j, k, :], in0=kc_f[:, 0, :], scalar1=cw[:, j, k:k + 1], scalar2=None,
                                                              op0=ALU.mult), [kc_f, cw], [dg])
                psC_r = palloc(es, "s_psC", [128, 512], F32, n=4)
                sil_r = [alloc(es, f"s_sil{j}_", [128, 512], BF16, n=2) for j in range(12)]
                xhs_r = alloc(es, "s_xhs", [128, 1024], BF16, n=2)
                bts_r = alloc(es, "s_bts", [128, 256], BF16, n=2)
                dtt_r = alloc(es, "s_dtt", [128, 4, 32], F32, n=2)
                dtw = [alloc(es, f"s_dtw{i}", [128, 4, 32], F32) for i in range(4)]
                dto_r = alloc(es, "s_dto", [128, 4, 64], F32, n=2)
                psT_r = palloc(es, "s_psT", [128, 8, 128], BF16, n=2)
                psB_r = palloc(es, "s_psB", [128, 2, 128], BF16, n=2)
                for b in range(NB):
                    groups = [(0, NCTX, 0, NCTX)] + [(NCTX + g * 512, min(512, NLAT - g * 512), NCTX, T) for g in range((NLAT + 511) // 512)]
                    for (t0, N, s0, s1) in groups:
                        nt = N // 128
                        raw = raw_r.next()
                        lo = 0
                        hi = N + 4
                        if t0 == s0:
                            S.op("pool", lambda E: E.memset(raw[:, :, 0:2], 0.0), [], [raw])
                            lo = 2
                        if t0 + N == s1:
                            S.op("pool", lambda E: E.memset(raw[:, :, N + 2:N + 4], 0.0), [], [raw])
                            hi = N + 2
                        S.dma("sp", raw[:, :, lo:hi], xbcT[b, :, t0 - 2 + lo:t0 - 2 + hi].rearrange("(j p) t -> p j t", p=128), [], [raw])
                        sils = []
                        for j in range(12):
                            acc = psC_r.next()

                            def cmm(E):
                                for k in range(5):
                                    ins = E.matmul(acc[:, 0:N], dg[:, j, k, :], raw[:, j, k:k + N], start=(k == 0), stop=(k == 4))
                                return ins
                            S.op("pe", cmm, [dg, raw], [acc])
                            sil = sil_r[j].next()
                            S.op("act", lambda E: E.activation(out=sil[:, 0:N], in_=acc[:, 0:N], func=AF.Silu, bias=cb[:, j:j + 1], scale=1.0),
                                 [acc, cb], [sil])
                            sils.append(sil)
                            if j >= 8:
                                dst = bT_d if j < 10 else cT_d
                                jj = (j - 8) % 2
                                S.dma("pool", dst[b, jj * 128:(jj + 1) * 128, t0:t0 + N], sil[:, 0:N], [sil], [])
                        for i in range(nt):
                            psT = psT_r.next()

                            def tr(E):
                                for j in range(8):
                                    ins = E.transpose(out=psT[:, j, :], in_=sils[j][:, i * 128:(i + 1) * 128], identity=ident_bf[:, :])
                                return ins
                            S.op("pe", tr, sils[0:8] + [ident_bf], [psT])
                            xhs = xhs_r.next()
                            evac_copy(xhs[:, :].rearrange("p (j t) -> p j t", j=8), psT[:, :, :], [psT], [xhs])
                            S.dma("pool", xh_d[b, t0 + i * 128:t0 + (i + 1) * 128, :], xhs[:, :], [xhs], [])
                            psB = psB_r.next()

                            def tr2(E):
                                for j in range(2):
                                    ins = E.transpose(out=psB[:, j, :], in_=sils[8 + j][:, i * 128:(i + 1) * 128], identity=ident_bf[:, :])
                                return ins
                            S.op("pe", tr2, sils[8:10] + [ident_bf], [psB])
                            bts = bts_r.next()
                            evac_copy(bts[:, :].rearrange("p (j t) -> p j t", j=2), psB[:, :, :], [psB], [bts])
                            S.dma("pool", btm_d[b, t0 + i * 128:t0 + (i + 1) * 128, :], bts[:, :], [bts], [])
                        dtt = dtt_r.next()
                        S.dma("sp", dtt[:, 0:nt, :], dtraw[b, t0:t0 + N, :].rearrange("(i p) c -> p i c", p=128), [], [dtt])
                        v_, ax, ee, ll = dtw
                        bc = dtb_bc[:, :].unsqueeze(1).to_broadcast([128, nt, 32])
                        S.op("dve", lambda E: E.tensor_tensor(out=v_[:, 0:nt, :], in0=dtt[:, 0:nt, :], in1=bc, op=ALU.add), [dtt, dtb_bc], [v_])
                        S.op("act", lambda E: E.activation(out=ax[:, 0:nt, :], in_=v_[:, 0:nt, :], func=AF.Abs), [v_], [ax])
                        S.op("act", lambda E: E.activation(out=ee[:, 0:nt, :], in_=ax[:, 0:nt, :], func=AF.Exp, scale=-1.0), [ax], [ee])
                        S.op("dve", lambda E: E.tensor_scalar(out=ee[:, 0:nt, :], in0=ee[:, 0:nt, :], scalar1=1.0, scalar2=None, op0=ALU.add), [ee], [ee])
                        S.op("act", lambda E: E.activation(out=ll[:, 0:nt, :], in_=ee[:, 0:nt, :], func=AF.Ln), [ee], [ll])
                        dto = dto_r.next()
                        S.op("dve", lambda E: E.scalar_tensor_tensor(out=dto[:, 0:nt, 0:32], in0=v_[:, 0:nt, :], scalar=0.0, in1=ll[:, 0:nt, :],
                                                                     op0=ALU.max, op1=ALU.add), [v_, ll], [dto])
                        S.op("dve", lambda E: E.tensor_tensor(out=dto[:, 0:nt, 32:64], in0=dto[:, 0:nt, 0:32],
                                                             in1=aneg_bc[:, :].unsqueeze(1).to_broadcast([128, nt, 32]), op=ALU.mult),
                             [dto, aneg_bc], [dto])
                        S.dma("pool", dta[b, t0:t0 + N, :].rearrange("(i p) c -> p i c", p=128), dto[:, 0:nt, :], [dto], [])
                S.barrier()

            with ExitStack() as es:
                def mk_stream(sn):
                    B_ = {}
                    B_["xh"] = alloc(es, f"c{sn}_xh", [128, 1024], BF16, n=2)
                    B_["btm"] = alloc(es, f"c{sn}_btm", [128, 256], BF16, n=2)
                    B_["bT"] = alloc(es, f"c{sn}_bT", [128, 2, 128], BF16, n=2)
                    B_["cT"] = alloc(es, f"c{sn}_cT", [128, 2, 128], BF16, n=2)
                    B_["dta"] = alloc(es, f"c{sn}_dta", [128, 64], F32, n=2)
                    B_["acs"] = alloc(es, f"c{sn}_acs", [128, 48], F32)
                    B_["dec"] = alloc(es, f"c{sn}_dec", [128, 48], F32)
                    B_["nacs"] = alloc(es, f"c{sn}_nacs", [128, 16], F32)
                    B_["dtd"] = alloc(es, f"c{sn}_dtd", [128, 16], F32)
                    B_["rhsD"] = alloc(es, f"c{sn}_rhsD", [128, 16, 128], F32)
                    B_["ex"] = alloc(es, f"c{sn}_ex", [128, 4, 128], BF16, n=2)
                    B_["MT"] = alloc(es, f"c{sn}_MT", [128, 16, 128], BF16)
                    B_["cbT"] = alloc(es, f"c{sn}_cbT", [128, 2, 128], BF16)
                    B_["xdt"] = alloc(es, f"c{sn}_xdt", [128, 1024], BF16)
                    B_["xdd"] = alloc(es, f"c{sn}_xdd", [128, 1024], BF16)
                    B_["yo"] = alloc(es, f"c{sn}_yo", [128, 512], F32, n=2)
                    B_["y"] = alloc(es, f"c{sn}_y", [128, 1024], F32, n=2)
                    B_["stf"] = [alloc(es, f"c{sn}_stf{g}_", [128, 512], F32) for g in range(2)]
                    B_["stt"] = alloc(es, f"c{sn}_stt", [128, 512], F32, n=2)
                    B_["stb"] = [alloc(es, f"c{sn}_stb{g}_", [128, 512], BF16) for g in range(2)]
                    B_["psS"] = palloc(es, f"c{sn}_psS", [128, 512], F32)
                    B_["psD"] = palloc(es, f"c{sn}_psD", [128, 4, 128], F32)
                    B_["psY"] = palloc(es, f"c{sn}_psY", [128, 512], F32)
                    B_["psO"] = palloc(es, f"c{sn}_psO", [128, 512], F32)
                    return B_
                streams = [mk_stream(0), mk_stream(1)]
                print("scan sbuf remaining", nc.sbuf_bytes_remaining)

                STOPAT = int(os.environ.get("STOPAT", "99"))

                def scan_stream(b, dr, B_):
                    psS, psD, psY, psO = B_["psS"], B_["psD"], B_["psY"], B_["psO"]
                    stf, stb = B_["stf"], B_["stb"]
                    if dr == 0:
                        order = list(range(NCH))
                        ydst = yf_d
                    else:
                        order = list(range(NCC - 1, -1, -1)) + list(range(NCH - 1, NCC - 1, -1))
                        ydst = yb_d
                    for g in range(2):
                        S.op("pool", lambda E: E.memset(stf[g][:, :], 0.0), [], [stf[g]])
                        S.op("pool", lambda E: E.memset(stb[g][:, :], 0.0), [], [stb[g]])
                    yield
                    for c in order:
                        tk = slice(c * 128, (c + 1) * 128)
                        xh = B_["xh"].next()
                        S.dma("sp", xh[:, :], xh_d[b, tk, :], [], [xh])
                        btm = B_["btm"].next()
                        S.dma("sp", btm[:, :], btm_d[b, tk, :], [], [btm])
                        bT = B_["bT"].next()
                        S.dma("sp", bT[:, :, :], bT_d[b, :, tk].rearrange("(g p) t -> p g t", p=128), [], [bT])
                        cT = B_["cT"].next()
                        S.dma("sp", cT[:, :, :], cT_d[b, :, tk].rearrange("(g p) t -> p g t", p=128), [], [cT])
                        da = B_["dta"].next()
                        S.dma("sp", da[:, :], dta[b, tk, :], [], [da])
                        dt_d = da[:, dr * 16:(dr + 1) * 16]
                        a_d = da[:, 32 + dr * 16:32 + (dr + 1) * 16]

                        def cs_mm(E):
                            E.matmul(psS[:, 0:16], tri[dr], a_d, start=True, stop=True)
                            E.matmul(psS[:, 16:32], ones_f, a_d, start=True, stop=True)
                            for g in range(2):
                                ins = E.matmul(psS[:, 256 + g * 128:256 + (g + 1) * 128], bT[:, g, :], cT[:, g, :], start=True, stop=True)
                            return ins
                        S.op("pe", cs_mm, [kc_f, da, bT, cT], [psS])
                        rhsD = B_["rhsD"]
                        S.op("pool", lambda E: E.tensor_tensor(out=rhsD[:, :, :], in0=a_d.unsqueeze(2).to_broadcast([128, 16, 128]),
                                                              in1=tri[dr].unsqueeze(1).to_broadcast([128, 16, 128]), op=ALU.mult), [da, kc_f], [rhsD])
                        yield
                        if STOPAT < 1:
                            continue
                        acs = B_["acs"]
                        S.op("act", lambda E: E.activation(out=acs[:, 0:32], in_=psS[:, 0:32], func=AF.Copy), [psS], [acs])
                        cbT = B_["cbT"]
                        S.op("act", lambda E: E.activation(out=cbT[:, :, :], in_=psS[:, 256:512].rearrange("p (g l) -> p g l", g=2), func=AF.Copy), [psS], [cbT])
                        yield
                        S.op("dve", lambda E: E.tensor_tensor(out=acs[:, 32:48], in0=acs[:, 16:32], in1=acs[:, 0:16], op=ALU.subtract), [acs], [acs])
                        nacs = B_["nacs"]
                        S.op("pool", lambda E: E.tensor_scalar(out=nacs[:, :], in0=acs[:, 0:16], scalar1=-1.0, scalar2=None, op0=ALU.mult), [acs], [nacs])
                        yield
                        dec = B_["dec"]
                        S.op("act", lambda E: E.activation(out=dec[:, :], in_=acs[:, :], func=AF.Exp), [acs], [dec])
                        xdt = B_["xdt"]
                        S.op("dve", lambda E: E.tensor_tensor(out=xdt[:, :].rearrange("p (h e) -> p h e", h=16),
                                                             in0=xh[:, :].rearrange("p (h e) -> p h e", h=16),
                                                             in1=dt_d.unsqueeze(2).to_broadcast([128, 16, 64]), op=ALU.mult), [xh, da], [xdt])
                        yield
                        dtd = B_["dtd"]
                        S.op("dve", lambda E: E.tensor_tensor(out=dtd[:, :], in0=dt_d, in1=dec[:, 32:48], op=ALU.mult), [da, dec], [dtd])
                        yield
                        xdd = B_["xdd"]
                        S.op("pool", lambda E: E.tensor_tensor(out=xdd[:, :].rearrange("p (h e) -> p h e", h=16),
                                                              in0=xh[:, :].rearrange("p (h e) -> p h e", h=16),
                                                              in1=dtd[:, :].unsqueeze(2).to_broadcast([128, 16, 64]), op=ALU.mult), [xh, dtd], [xdd])
                        MT = B_["MT"]
                        y = B_["y"].next()
                        if STOPAT < 2:
                            continue
                        for g in range(2):
                            for q4 in range(2):
                                h0 = g * 8 + q4 * 4

                                def d_mm(E):
                                    E.matmul(psD[:, :, :], ones_f, rhsD[:, h0:h0 + 4, :], start=True, stop=False)
                                    return E.matmul(psD[:, :, :], ident_bf[:, :], neg_bf[:, dr, :, :], start=False, stop=True)
                                S.op("pe", d_mm, [rhsD, kc_f, ident_bf, neg_bf], [psD])
                                yield
                                ex = B_["ex"].next()
                                for h4 in range(4):
                                    h = h0 + h4
                                    S.op("act", lambda E: E.activation(out=ex[:, h4, :], in_=psD[:, h4, :], func=AF.Exp, bias=nacs[:, h:h + 1], scale=1.0),
                                         [psD, nacs], [ex])
                                yield
                                S.op("dve", lambda E: E.tensor_tensor(out=MT[:, h0:h0 + 4, :], in0=ex[:, :, :],
                                                                     in1=cbT[:, g, :].unsqueeze(1).to_broadcast([128, 4, 128]), op=ALU.mult), [ex, cbT], [MT])
                                yield
                            if STOPAT < 3:
                                continue

                            def yd_mm(E):
                                for h8 in range(8):
                                    h = g * 8 + h8
                                    ins = E.matmul(psY[:, h8 * 64:(h8 + 1) * 64], MT[:, h, :], xdt[:, h * 64:(h + 1) * 64], start=True, stop=True)
                                return ins
                            S.op("pe", yd_mm, [MT, xdt], [psY])
                            S.op("pe", lambda E: E.matmul(psO[:, :], cT[:, g, :], stb[g][:, :], start=True, stop=True), [cT, stb[g]], [psO])
                            yield
                            if STOPAT < 4:
                                continue
                            yo = B_["yo"].next()
                            S.op("dve", lambda E: E.tensor_tensor(out=yo[:, :].rearrange("p (h e) -> p h e", h=8),
                                                                 in0=psO[:, :].rearrange("p (h e) -> p h e", h=8),
                                                                 in1=dec[:, g * 8:(g + 1) * 8].unsqueeze(2).to_broadcast([128, 8, 64]), op=ALU.mult),
                                 [psO, dec], [yo])
                            yield
                            S.op("dve", lambda E: E.tensor_tensor(out=y[:, g * 512:(g + 1) * 512], in0=psY[:, :], in1=yo[:, :], op=ALU.add), [psY, yo], [y])
                            S.op("pe", lambda E: E.matmul(psO[:, :], btm[:, g * 128:(g + 1) * 128], xdd[:, g * 512:(g + 1) * 512], start=True, stop=True),
                                 [btm, xdd], [psO])
                            stt = B_["stt"].next()
                            S.op("pool", lambda E: E.tensor_tensor(out=stt[:, :].rearrange("p (h e) -> p h e", h=8),
                                                                  in0=stf[g][:, :].rearrange("p (h e) -> p h e", h=8),
                                                                  in1=dec[:, 16 + g * 8:16 + (g + 1) * 8].unsqueeze(2).to_broadcast([128, 8, 64]), op=ALU.mult),
                                 [stf[g], dec], [stt])
                            yield
                            S.op("dve", lambda E: E.tensor_tensor(out=stf[g][:, :], in0=psO[:, :], in1=stt[:, :], op=ALU.add), [psO, stt], [stf[g]])
                            yield
                            S.op("act", lambda E: E.activation(out=stb[g][:, :], in_=stf[g][:, :], func=AF.Copy), [stf[g]], [stb[g]])
                            yield
                        if not (last and c < NCC) and STOPAT > 5:
                            S.dma("act", ydst[b, tk, :], y[:, :], [y], [])
                        yield

                for b in range(NB if not os.environ.get("SKIP_SCAN") else 0):
                    gens = [scan_stream(b, 0, streams[0]), scan_stream(b, 1, streams[1])]
                    alive = [True, True]
                    if os.environ.get("SEQ_SCAN"):
                        for gen in gens:
                            for _ in gen:
                                pass
                        alive = [False, False]
                    while any(alive):
                        for gi, gen in enumerate(gens):
                            if alive[gi]:
                                try:
                                    next(gen)
                                except StopIteration:
                                    alive[gi] = False
                S.barrier()

            with ExitStack() as es:
                dsk_bc = alloc(es, "c_dsk", [128, 16], F32, const=True)
                sng_bc = alloc(es, "c_sng", [128, 1024], F32, const=True)
                S.dma("sp", dsk_bc[:, :], ssm_d[L:L + 1, :].partition_broadcast(128), [], [dsk_bc])
                S.dma("sp", sng_bc[:, :], ssm_norm_g[L:L + 1, :].partition_broadcast(128), [], [sng_bc])
                xh_r = alloc(es, "e_xh", [128, 1024], BF16, n=4)
                yf_r = alloc(es, "e_yf", [128, 1024], F32, n=4)
                yb_r = alloc(es, "e_yb", [128, 1024], F32, n=4)
                zs_r = alloc(es, "e_zs", [128, 1024], BF16, n=4)
                xd_r = alloc(es, "e_xd", [128, 1024], F32, n=4)
                yn_r = alloc(es, "e_yn", [128, 1024], BF16, n=4)
                ybT_r = alloc(es, "e_ybT", [128, 8, 128], BF16, n=4)
                junk = alloc(es, "e_junk", [128, 512], BF16)
                ms_r = alloc(es, "e_ms", [128, 2], F32, n=4)
                rs_r = alloc(es, "e_rs", [128, 2], F32, n=4)
                psT_r = palloc(es, "e_psT", [128, 8, 128], BF16, n=2)
                for b in range(NB if not os.environ.get("SKIP_OUT") else 0):
                    for c in (range(NCC, NCH) if last else range(NCH)):
                        tk = slice(c * 128, (c + 1) * 128)
                        xh = xh_r.next()
                        S.dma("sp", xh[:, :], xh_d[b, tk, :], [], [xh])
                        yf = yf_r.next()
                        S.dma("sp", yf[:, :], yf_d[b, tk, :], [], [yf])
                        yb = yb_r.next()
                        S.dma("sp", yb[:, :], yb_d[b, tk, :], [], [yb])
                        zt = zs_r.next()
                        S.dma("sp", zt[:, :], zs[b, tk, :], [], [zt])
                        xd = xd_r.next()
                        S.op("pool", lambda E: E.tensor_tensor(out=xd[:, :].rearrange("p (h e) -> p h e", h=16),
                                                              in0=xh[:, :].rearrange("p (h e) -> p h e", h=16),
                                                              in1=dsk_bc[:, :].unsqueeze(2).to_broadcast([128, 16, 64]), op=ALU.mult), [xh, dsk_bc], [xd])
                        S.op("dve", lambda E: E.tensor_tensor(out=yf[:, :], in0=yf[:, :], in1=yb[:, :], op=ALU.add), [yf, yb], [yf])
                        S.op("dve", lambda E: E.tensor_tensor(out=yf[:, :], in0=yf[:, :], in1=xd[:, :], op=ALU.add), [yf, xd], [yf])
                        S.op("dve", lambda E: E.tensor_tensor(out=yb[:, :], in0=yf[:, :], in1=zt[:, :], op=ALU.mult), [yf, zt], [yb])
                        ms = ms_r.next()
                        rs = rs_r.next()
                        for g in range(2):
                            S.op("act", lambda E: E.activation(out=junk[:, :], in_=yb[:, g * 512:(g + 1) * 512], func=AF.Square,
                                                               scale=1.0 / math.sqrt(512.0), accum_out=ms[:, g:g + 1]), [yb], [junk, ms])
                        rstd_from_ms(ms, rs, 2)
                        yn = yn_r.next()
                        for g in range(2):
                            S.op("dve", lambda E: E.scalar_tensor_tensor(out=yn[:, g * 512:(g + 1) * 512], in0=yb[:, g * 512:(g + 1) * 512],
                                                                         scalar=rs[:, g:g + 1], in1=sng_bc[:, g * 512:(g + 1) * 512],
                                                                         op0=ALU.mult, op1=ALU.mult), [yb, rs, sng_bc], [yn])
                        psT = psT_r.next()

                        def tr(E):
                            for j in range(8):
                                ins = E.transpose(out=psT[:, j, :], in_=yn[:, j * 128:(j + 1) * 128], identity=ident_bf[:, :])
                            return ins
                        S.op("pe", tr, [yn, ident_bf], [psT])
                        ybT = ybT_r.next()
                        S.op("act", lambda E: E.activation(out=ybT[:, :, :], in_=psT[:, :, :], func=AF.Copy), [psT], [ybT])
                        S.dma("act", catT[b, 512:1536, tk].rearrange("(j p) t -> p j t", p=128), ybT[:, :, :], [ybT], [])
                S.barrier()

            with ExitStack() as es:
                lp = alloc(es, "t_lp", [128, 256], F32)
                lw = alloc(es, "t_lw", [128, 128], F32)
                lsum = alloc(es, "t_ls", [128, 2], F32)
                nlam = alloc(es, "t_nlam", [128, 1], F32, const=True)
                sgc = alloc(es, "t_sgc", [128, 1], F32, const=True)
                S.dma("sp", lp[:, :], lam_p[L:L + 1, :].partition_broadcast(128), [], [lp])
                S.dma("sp", sgc[:, :], subln_g[L].rearrange("(e o) -> e o", o=1), [], [sgc])
                S.op("dve", lambda E: E.tensor_scalar(out=sgc[:, :], in0=sgc[:, :], scalar1=1.0 - lam_init, scalar2=None, op0=ALU.mult),
                     [sgc], [sgc])
                for i2 in range(2):
                    S.op("dve", lambda E: E.tensor_tensor(out=lw[:, i2 * 64:(i2 + 1) * 64], in0=lp[:, i2 * 128:i2 * 128 + 64],
                                                         in1=lp[:, i2 * 128 + 64:i2 * 128 + 128], op=ALU.mult), [lp], [lw])
                    S.op("act", lambda E: E.activation(out=lp[:, i2 * 64:(i2 + 1) * 64], in_=lw[:, i2 * 64:(i2 + 1) * 64], func=AF.Copy,
                                                       accum_out=lsum[:, i2:i2 + 1]), [lw], [lp, lsum])
                S.op("act", lambda E: E.activation(out=lsum[:, :], in_=lsum[:, :], func=AF.Exp), [lsum], [lsum])
                S.op("dve", lambda E: E.tensor_tensor(out=nlam[:, :], in0=lsum[:, 1:2], in1=lsum[:, 0:1], op=ALU.subtract), [lsum], [nlam])
                S.op("dve", lambda E: E.tensor_scalar(out=nlam[:, :], in0=nlam[:, :], scalar1=-lam_init, scalar2=None, op0=ALU.add), [nlam], [nlam])
                kTh_r = alloc(es, "t_kT", [128, T], BF16, n=2)
                vh_r = alloc(es, "t_vh", [128, NCH, 128], BF16, n=2)
                qz_r = alloc(es, "t_qz", [128, 2, 512], BF16, n=2)
                for qb in qz_r.bufs:
                    S.op("pool", lambda E: E.memset(qb[:, :, :], 0.0), [], [qb])
                gc_r = alloc(es, "t_gc", [128, 512], BF16, n=3)
                pT_r = alloc(es, "t_pT", [128, 2, 512], BF16, n=5)
                pac1_r = alloc(es, "t_pac1", [128, 512], F32, n=2)
                den0_r = alloc(es, "t_den0", [128, 512], F32, n=2)
                ones_bf = alloc(es, "t_ones", [128, 128], BF16, const=True)
                S.op("dve", lambda E: E.tensor_copy(out=ones_bf[:, :], in_=ones_f), [kc_f], [ones_bf])
                fin_pending = [None]
                osb_r = alloc(es, "t_osb", [128, 2, 512], F32, n=3)
                rden_r = alloc(es, "t_rden", [128, 2, 512], F32, n=1)
                o_r = alloc(es, "t_o", [128, 512], F32, n=2)
                sq_r = alloc(es, "t_sq", [128, 512], F32, n=2)
                rsd_r = alloc(es, "t_rsd", [128, 512], F32, n=2)
                ycT_r = alloc(es, "t_ycT", [128, 512], BF16, n=2)
                psS_r = palloc(es, "t_psS", [128, 2, 512], F32, n=2)
                psO = palloc(es, "t_psO", [128, 2, 512], F32)
                psD = palloc(es, "t_psD", [128, 512], F32)
                psDen = palloc(es, "t_psDen", [128, 512], F32)
                print("attn sbuf remaining", nc.sbuf_bytes_remaining)
                for b in range(NB):
                    for h in range(4):
                        kTh = kTh_r.next()
                        S.dma("sp", kTh[:, :], kT[b, h * 128:(h + 1) * 128, :], [], [kTh])
                        vh = vh_r.next()
                        S.dma("sp", vh[:, :, :], vtok[b, :, h * 128:(h + 1) * 128].rearrange("(c p) e -> p c e", p=128), [], [vh])
                        qgroups = [(NCTX + g * 512, min(512, NLAT - g * 512), 0, NCH) for g in range((NLAT + 511) // 512)]
                        if not last:
                            qgroups = [(0, NCTX, 0, NCC)] + qgroups
                        for (t0, N, k0, k1) in qgroups:
                            qz = qz_r.next()
                            S.dma("sp", qz[0:64, 0, 0:N], qT[b, h * 128:h * 128 + 64, t0:t0 + N], [], [qz])
                            S.dma("sp", qz[64:128, 1, 0:N], qT[b, h * 128 + 64:(h + 1) * 128, t0:t0 + N], [], [qz])
                            gc = gc_r.next()
                            S.dma("sp", gc[:, 0:N], gcT[b, h * 128:(h + 1) * 128, t0:t0 + N], [], [gc])
                            pac1 = pac1_r.next()
                            pend = []
                            for idx in range(k0, k1 + 1):
                                if idx < k1:
                                    ps = psS_r.next()

                                    def smm(E):
                                        for i2 in range(2):
                                            ins = E.matmul(ps[:, i2, 0:N], kTh[i2 * 64:(i2 + 1) * 64, idx * 128:(idx + 1) * 128],
                                                           qz[i2 * 64:(i2 + 1) * 64, i2, 0:N], start=True, stop=True, tile_position=(i2 * 64, 0))
                                        return ins
                                    S.op("pe", smm, [kTh, qz], [ps])
                                    pend.append(ps)
                                if idx > k0:
                                    kt = idx - 1
                                    ps = pend.pop(0)
                                    pT = pT_r.next()
                                    S.op("act", lambda E: E.activation(out=pT[:, :, 0:N], in_=ps[:, :, 0:N], func=AF.Exp, scale=0.125), [ps], [pT])

                                    def pv(E):
                                        for i2 in range(2):
                                            ins = E.matmul(psO[:, i2, 0:N], vh[:, kt, :], pT[:, i2, 0:N], start=(kt == k0), stop=(kt == k1 - 1))
                                        return ins
                                    S.op("pe", pv, [pT, vh], [psO])
                                    S.op("pe", lambda E: E.matmul(psDen[:, 0:N], ones_bf[:, :], pT[:, 0, 0:N], start=(kt == k0), stop=(kt == k1 - 1)),
                                         [pT, ones_bf], [psDen])
                                    if kt == k0:
                                        S.op("dve", lambda E: E.tensor_copy(out=pac1[:, 0:N], in_=pT[:, 1, 0:N]), [pT], [pac1])
                                    else:
                                        S.op("dve", lambda E: E.tensor_tensor(out=pac1[:, 0:N], in0=pac1[:, 0:N], in1=pT[:, 1, 0:N], op=ALU.add),
                                             [pT, pac1], [pac1])
                                    if fin_pending[0] is not None and (kt - k0) % 2 == 1:
                                        try:
                                            next(fin_pending[0])
                                        except StopIteration:
                                            fin_pending[0] = None
                            if fin_pending[0] is not None:
                                for _ in fin_pending[0]:
                                    pass
                                fin_pending[0] = None
                            osb = osb_r.next()
                            S.op("act", lambda E: E.activation(out=osb[:, 0, 0:N], in_=psO[:, 0, 0:N], func=AF.Copy), [psO], [osb])
                            S.op("dve", lambda E: E.tensor_copy(out=osb[:, 1, 0:N], in_=psO[:, 1, 0:N]), [psO], [osb])
                            den0 = den0_r.next()
                            S.op("act", lambda E: E.activation(out=den0[:, 0:N], in_=psDen[:, 0:N], func=AF.Copy), [psDen], [den0])

                            def fin(osb=osb, gc=gc, N=N, b=b, h=h, t0=t0, pac1=pac1, den0=den0):
                                S.op("pe", lambda E: E.matmul(psD[:, 0:N], ones_f, pac1[:, 0:N], start=True, stop=True), [kc_f, pac1], [psD])
                                yield
                                rden = rden_r
                                for q0 in range(0, N, 256):
                                    S.op("dve", lambda E: E.reciprocal(out=rden[:, 0, q0:q0 + 256], in_=den0[:, q0:q0 + 256]), [den0], [rden])
                                    yield
                                    S.op("dve", lambda E: E.reciprocal(out=rden[:, 1, q0:q0 + 256], in_=psD[:, q0:q0 + 256]), [psD], [rden])
                                    yield
                                S.op("pool", lambda E: E.tensor_tensor(out=osb[:, :, 0:N], in0=osb[:, :, 0:N], in1=rden[:, :, 0:N], op=ALU.mult),
                                     [osb, rden], [osb])
                                yield
                                o = o_r.next()
                                S.op("dve", lambda E: E.scalar_tensor_tensor(out=o[:, 0:N], in0=osb[:, 1, 0:N], scalar=nlam[:, 0:1], in1=osb[:, 0, 0:N],
                                                                             op0=ALU.mult, op1=ALU.add), [osb, nlam], [o])
                                yield
                                sq = sq_r.next()
                                S.op("pool", lambda E: E.tensor_tensor(out=sq[:, 0:N], in0=o[:, 0:N], in1=o[:, 0:N], op=ALU.mult), [o], [sq])
                                yield
                                S.op("pe", lambda E: E.matmul(psD[:, 0:N], ones_f, sq[:, 0:N], start=True, stop=True), [kc_f, sq], [psD])
                                yield
                                rsd = rsd_r.next()
                                S.op("dve", lambda E: E.tensor_scalar(out=rsd[:, 0:N], in0=psD[:, 0:N], scalar1=1.0 / 128.0, scalar2=EPS,
                                                                     op0=ALU.mult, op1=ALU.add), [psD], [rsd])
                                yield
                                S.op("act", lambda E: E.activation(out=rsd[:, 0:N], in_=rsd[:, 0:N], func=AF.Ln), [rsd], [rsd])
                                yield
                                S.op("act", lambda E: E.activation(out=rsd[:, 0:N], in_=rsd[:, 0:N], func=AF.Exp, scale=-0.5), [rsd], [rsd])
                                yield
                                S.op("pool", lambda E: E.tensor_tensor(out=o[:, 0:N], in0=o[:, 0:N], in1=rsd[:, 0:N], op=ALU.mult), [o, rsd], [o])
                                yield
                                ycT = ycT_r.next()
                                S.op("dve", lambda E: E.scalar_tensor_tensor(out=ycT[:, 0:N], in0=o[:, 0:N], scalar=sgc[:, 0:1], in1=gc[:, 0:N],
                                                                             op0=ALU.mult, op1=ALU.mult), [o, sgc, gc], [ycT])
                                yield
                                S.dma("pool", catT[b, 1536 + h * 128:1536 + (h + 1) * 128, t0:t0 + N], ycT[:, 0:N], [ycT], [])
                            fin_pending[0] = fin()
                if fin_pending[0] is not None:
                    for _ in fin_pending[0]:
                        pass
                    fin_pending[0] = None
                S.barrier()

            with ExitStack() as es:
                wo_bf = alloc(es, "o_wo", [128, 16, D], BF16)
                with ExitStack() as es2:
                    wst = alloc(es2, "o_wst", [128, D], F32, n=3)
                    for j in range(16):
                        st_ = wst.next()
                        S.dma("sp", st_[:, :], w_out[L, j * 128:(j + 1) * 128, :], [], [st_])
                        if j % 2:
                            S.op("dve", lambda E: E.tensor_copy(out=wo_bf[:, j, :], in_=st_[:, :]), [st_], [wo_bf])
                        else:
                            S.op("act", lambda E: E.activation(out=wo_bf[:, j, :], in_=st_[:, :], func=AF.Copy), [st_], [wo_bf])
                S.barrier()
                wo_bf.const = True
                gate_bc = alloc(es, "o_gate", [128, NB + 1, D], F32, const=True)
                for r in range(NB + 1):
                    S.dma("sp", gate_bc[:, r, :], modrow[L, r:r + 1, 2 * D:3 * D].partition_broadcast(128), [], [gate_bc])
                fng_bc = alloc(es, "o_fng", [128, D], F32, const=True)
                S.dma("sp", fng_bc[:, :], final_g[0:1, :].partition_broadcast(128), [], [fng_bc])
                cat_r = alloc(es, "o_cat", [128, 16, 128], BF16, n=2)
                xr_r = alloc(es, "o_xr", [128, D], F32, n=2)
                tmp_r = alloc(es, "o_tmp", [128, D], F32, n=2)
                xo_r = alloc(es, "o_xo", [128, D], F32, n=2)
                junk = alloc(es, "o_junk", [128, D], BF16)
                ms_r = alloc(es, "o_ms", [128, 2], F32, n=2)
                rs_r = alloc(es, "o_rs", [128, 2], F32, n=2)
                psO_r = palloc(es, "o_ps", [128, D], F32, n=2)
                for b in range(NB):
                    tiles = list(range(NCC, NCH)) if last else list(range(NCH))
                    for c in tiles:
                        tk = slice(c * 128, (c + 1) * 128)
                        r = NB if c < NCC else b
                        cat = cat_r.next()
                        S.dma("sp", cat[:, :, :], catT[b, :, tk].rearrange("(j p) t -> p j t", p=128), [], [cat])
                        xr = xr_r.next()
                        S.dma("sp", xr[:, :], src_rows(L, b, c * 128, 128), [], [xr])
                        ps = psO_r.next()

                        def mm(E):
                            for hf in range(2):
                                for j in range(16):
                                    ins = E.matmul(ps[:, hf * 512:(hf + 1) * 512], cat[:, j, :], wo_bf[:, j, hf * 512:(hf + 1) * 512],
                                                   start=(j == 0), stop=(j == 15))
                            return ins
                        S.op("pe", mm, [cat, wo_bf], [ps])
                        tmp = tmp_r.next()
                        S.op("dve", lambda E: E.tensor_tensor(out=tmp[:, :], in0=ps[:, :], in1=gate_bc[:, r, :], op=ALU.mult), [ps, gate_bc], [tmp])
                        xo = xo_r.next()
                        S.op("pool", lambda E: E.tensor_tensor(out=xo[:, :], in0=tmp[:, :], in1=xr[:, :], op=ALU.add), [tmp, xr], [xo])
                        if not last:
                            S.dma("pool", x1_d[b, tk, :], xo[:, :], [xo], [])
                        else:
                            ms = ms_r.next()
                            rs = rs_r.next()
                            S.op("act", lambda E: E.activation(out=junk[:, :], in_=xo[:, :], func=AF.Square, scale=1.0 / 32.0,
                                                               accum_out=ms[:, 0:1]), [xo], [junk, ms])
                            rstd_from_ms(ms, rs, 1)
                            S.op("dve", lambda E: E.scalar_tensor_tensor(out=tmp[:, :], in0=xo[:, :], scalar=rs[:, 0:1], in1=fng_bc[:, :],
                                                                         op0=ALU.mult, op1=ALU.mult), [xo, rs, fng_bc], [tmp])
                            S.dma("pool", y_out[b, (c - NCC) * 128:(c - NCC + 1) * 128, :], tmp[:, :], [tmp], [])
                S.barrier()
        S.barrier()
    print("instr counts", S.cnt, "waits", S.nwait)
    return nc


def host_consts(NLAT):
    idx = np.arange(128)
    ident = np.eye(128, dtype=np.float32)
    triF = (idx[:, None] <= idx[None, :]).astype(np.float32)
    triB = (idx[:, None] >= idx[None, :]).astype(np.float32)
    ones = np.ones((128, 128), np.float32)
    negF = np.where(idx[None, :] < idx[:, None], NEG, 0.0).astype(np.float32)
    negB = np.where(idx[None, :] > idx[:, None], NEG, 0.0).astype(np.float32)
    k_const = np.stack([ident, triF, triB, ones, negF, negB]).astype(np.float32)
    R = np.zeros((128, 128), np.float32)
    for m in range(128):
        if m % 64 < 32:
            R[m, m + 32] = -1.0
        else:
            R[m, m - 32] = 1.0
    k_rt = np.ascontiguousarray(R.T)
    n = np.arange(NLAT)
    row = (n // 64).astype(np.float32)
    col = (n % 64).astype(np.float32)
    inv = (10000.0 ** (-np.arange(16, dtype=np.float32) / 16)).astype(np.float32)
    ang = np.concatenate([row[:, None] * inv, col[:, None] * inv], axis=-1).astype(np.float32)
    cos = np.cos(ang).astype(np.float32)
    sin = np.sin(ang).astype(np.float32)
    cosT = np.tile(cos.T, (4, 1))
    sinT = np.tile(sin.T, (4, 1))
    k_rope = np.ascontiguousarray(np.stack([cosT, sinT]).astype(np.float32))
    return k_const, k_rt, k_rope


def make_in_maps(inputs, n_cores, NB, NLAT):
    k_const, k_rt, k_rope = host_consts(NLAT)
    f = lambda a: np.ascontiguousarray(np.asarray(a, dtype=np.float32))
    shared = {
        "w_ada": f(inputs["w_ada"]), "b_ada": f(inputs["b_ada"]), "w_in": f(inputs["w_in"]), "w_out": f(inputs["w_out"]),
        "chunk_norm_g": f(inputs["chunk_norm_g"]), "chunk_ws": f(inputs["chunk_ws"]), "chunk_bs": f(inputs["chunk_bs"]),
        "ssm_conv_w": f(inputs["ssm_conv_w"]), "ssm_conv_b": f(inputs["ssm_conv_b"]),
        "ssm_dt_bias": f(inputs["ssm_dt_bias"]).reshape(DEPTH, 32), "ssm_a_log": f(inputs["ssm_a_log"]).reshape(DEPTH, 32),
        "ssm_d": f(inputs["ssm_d"]), "ssm_norm_g": f(inputs["ssm_norm_g"]),
        "diff_lambda_p": f(inputs["diff_lambda_p"]).reshape(DEPTH, 256), "diff_subln_g": f(inputs["diff_subln_g"]),
        "final_norm_g": f(inputs["final_norm_g"]).reshape(1, D),
        "k_const": k_const, "k_rt": k_rt, "k_rope": k_rope,
    }
    x = f(inputs["x"])
    ctx = f(inputs["ctx"])
    c = f(inputs["c"])
    c_ctx = f(inputs["c_ctx"]).reshape(1, D)
    maps = []
    for i in range(n_cores):
        m = dict(shared)
        m["x"] = np.ascontiguousarray(x[i * NB:(i + 1) * NB, :NLAT])
        m["ctx"] = np.ascontiguousarray(ctx[i * NB:(i + 1) * NB])
        m["cvec"] = np.ascontiguousarray(np.concatenate([c[i * NB:(i + 1) * NB], c_ctx], axis=0))
        maps.append(m)
    return maps


def kernel(**inputs):
    n_cores = 8
    NB = 2
    nc = build(NB, NCTX_FULL, NLAT_FULL)
    maps = make_in_maps(inputs, n_cores, NB, NLAT_FULL)
    res = run_bass_kernel_spmd(nc, maps, core_ids=list(range(n_cores)))
    return np.concatenate([np.asarray(r["y"], dtype=np.float32) for r in res.results], axis=0)
```

```python
import math
import os
from contextlib import ExitStack
import numpy as np
import concourse.bass as bass
import concourse.mybir as mybir
from concourse.bass_utils import run_bass_kernel_spmd

F32 = mybir.dt.float32
BF16 = mybir.dt.bfloat16
AF = mybir.ActivationFunctionType
ALU = mybir.AluOpType

D = 1024
NCTX_FULL = 256
NLAT_FULL = 4096
DEPTH = 2
EPS = 1e-6
DIN = 6176
C_XBC, C_DT, C_K, C_V, C_U, C_VM, C_GA, C_Z, C_Q, C_GC = 0, 1536, 1568, 2080, 2592, 3104, 3616, 4128, 5152, 5664
NEG = -30000.0
NRING = 8


class Buf:
    __slots__ = ("t", "w", "r", "const")

    def __init__(self, t, const=False):
        self.t = t
        self.w = None
        self.r = []
        self.const = const

    def __getitem__(self, k):
        return self.t[k]


class Rot:
    def __init__(self, bufs):
        self.bufs = bufs
        self.i = 0

    def next(self):
        b = self.bufs[self.i % len(self.bufs)]
        self.i += 1
        return b


class Sch:
    def __init__(self, nc):
        self.nc = nc
        self.eng = {"pe": nc.tensor, "act": nc.scalar, "dve": nc.vector, "pool": nc.gpsimd, "sp": nc.sync}
        self.sem = {k: nc.alloc_semaphore("s_" + k) for k in ("pe", "act", "dve", "pool")}
        self.cnt = {k: 0 for k in self.sem}
        self.seen = {k: {} for k in self.eng}
        self.rings = {q: [nc.alloc_semaphore(f"d_{q}{i}") for i in range(NRING)] for q in ("sp", "pool", "act")}
        self.rcnt = {q: [0] * NRING for q in self.rings}
        self.rpos = {q: 0 for q in self.rings}
        self.nwait = 0

    def _wait(self, e, ev):
        if ev is None:
            return
        key, sem, val = ev
        if e == "pe" and key == "pe":
            return
        if self.seen[e].get(key, 0) >= val:
            return
        self.eng[e].wait_ge(sem, val)
        self.seen[e][key] = val
        self.nwait += 1

    def _deps(self, e, reads, writes):
        for b in reads:
            self._wait(e, b.w)
        for b in writes:
            self._wait(e, b.w)
            for ev in b.r:
                self._wait(e, ev)

    def _mark(self, ev, reads, writes):
        for b in reads:
            if not b.const:
                b.r.append(ev)
        for b in writes:
            b.w = ev
            b.r = []

    def op(self, e, fn, reads=(), writes=()):
        self._deps(e, reads, writes)
        ins = fn(self.eng[e])
        self.cnt[e] += 1
        ins.then_inc(self.sem[e], 1)
        self._mark((e, self.sem[e], self.cnt[e]), reads, writes)

    def dma(self, q, out, in_, reads=(), writes=(), **kw):
        i = self.rpos[q]
        self.rpos[q] = (i + 1) % NRING
        sem = self.rings[q][i]
        if self.rcnt[q][i] > 0:
            self._wait(q, ((q, i), sem, self.rcnt[q][i]))
        self._deps(q, reads, writes)
        self.eng[q].dma_start(out=out, in_=in_, **kw).then_inc(sem, 16)
        self.rcnt[q][i] += 16
        self._mark(((q, i), sem, self.rcnt[q][i]), reads, writes)

    def barrier(self):
        evs = [(k, self.sem[k], self.cnt[k]) for k in self.sem if self.cnt[k] > 0]
        for q in self.rings:
            for i, s in enumerate(self.rings[q]):
                if self.rcnt[q][i] > 0:
                    evs.append(((q, i), s, self.rcnt[q][i]))
        for e in self.eng:
            for ev in evs:
                self._wait(e, ev)


def build(NB, NCTX, NLAT, depth=DEPTH, dbg=(), skip=()):
    T = NCTX + NLAT
    NCH = T // 128
    NCC = NCTX // 128
    nc = bass.Bass("TRN2", target_bir_lowering=False)
    S = Sch(nc)

    def din(name, shape, dt=F32):
        return nc.dram_tensor(name, list(shape), dt, kind="ExternalInput").ap()

    def dscr(name, shape, dt=F32):
        kind = "ExternalOutput" if name in dbg else "Internal"
        return nc.dram_tensor(name, list(shape), dt, kind=kind).ap()

    x_in = din("x", [NB, NLAT, D])
    ctx_in = din("ctx", [NB, NCTX, D])
    cvec = din("cvec", [NB + 1, D])
    w_ada = din("w_ada", [depth, D, 3 * D])
    b_ada = din("b_ada", [depth, 3 * D])
    w_in = din("w_in", [depth, D, DIN])
    w_out = din("w_out", [depth, 2048, D])
    chunk_norm_g = din("chunk_norm_g", [depth, 512])
    chunk_ws = din("chunk_ws", [depth, 4, 128, 128])
    chunk_bs = din("chunk_bs", [depth, 4, 128])
    conv_w = din("ssm_conv_w", [depth, 5, 1536])
    conv_b = din("ssm_conv_b", [depth, 1536])
    dt_bias = din("ssm_dt_bias", [depth, 32])
    a_log = din("ssm_a_log", [depth, 32])
    ssm_d = din("ssm_d", [depth, 16])
    ssm_norm_g = din("ssm_norm_g", [depth, 1024])
    lam_p = din("diff_lambda_p", [depth, 256])
    subln_g = din("diff_subln_g", [depth, 128])
    final_g = din("final_norm_g", [1, D])
    k_const = din("k_const", [6, 128, 128])
    k_rt = din("k_rt", [128, 128])
    k_rope = din("k_rope", [2, 128, NLAT])
    y_out = nc.dram_tensor("y", [NB, NLAT, D], F32, kind="ExternalOutput").ap()

    modrow = dscr("modrow", [depth, NB + 1, 3 * D])
    xbcT = dscr("xbcT", [NB, 1536, T], BF16)
    dtraw = dscr("dtraw", [NB, T, 32])
    dta = dscr("dta", [NB, T, 64])
    kT = dscr("kT", [NB, 512, T], BF16)
    qT = dscr("qT", [NB, 512, T], BF16)
    vtok = dscr("vtok", [NB, T, 512], BF16)
    zs = dscr("zs", [NB, T, 1024], BF16)
    gcT = dscr("gcT", [NB, 512, T], BF16)
    catT = dscr("catT", [NB, 2048, T], BF16)
    xh_d = dscr("xh", [NB, T, 1024], BF16)
    btm_d = dscr("btm", [NB, T, 256], BF16)
    bT_d = dscr("bT", [NB, 256, T], BF16)
    cT_d = dscr("cT", [NB, 256, T], BF16)
    yf_d = dscr("yf", [NB, T, 1024])
    yb_d = dscr("yb", [NB, T, 1024])
    x1_d = dscr("x1", [NB, T, D])

    uid = [0]

    def alloc(es, name, shape, dt, n=1, const=False):
        uid[0] += 1
        bufs = [Buf(es.enter_context(nc.sbuf_tensor(f"{name}{i}_{uid[0]}", list(shape), dt)), const) for i in range(n)]
        return bufs[0] if n == 1 else Rot(bufs)

    def palloc(es, name, shape, dt, n=1):
        uid[0] += 1
        bufs = [Buf(es.enter_context(nc.psum_tensor(f"{name}{i}_{uid[0]}", list(shape), dt))) for i in range(n)]
        return bufs[0] if n == 1 else Rot(bufs)

    def src_rows(L, b, t0, n):
        if L == 0:
            if t0 < NCTX:
                return ctx_in[b, t0:t0 + n, :]
            return x_in[b, t0 - NCTX:t0 - NCTX + n, :]
        return x1_d[b, t0:t0 + n, :]

    ev_ctr = [0]

    def evac_copy(out, in_, reads, writes):
        ev_ctr[0] += 1
        if ev_ctr[0] % 2:
            S.op("act", lambda E: E.activation(out=out, in_=in_, func=AF.Copy), reads, writes)
        else:
            S.op("dve", lambda E: E.tensor_copy(out=out, in_=in_), reads, writes)

    def rstd_from_ms(ms, rs, n):
        S.op("dve", lambda E: E.tensor_scalar(out=rs[:, 0:n], in0=ms[:, 0:n], scalar1=EPS, scalar2=None, op0=ALU.add), [ms], [rs])
        S.op("act", lambda E: E.activation(out=rs[:, 0:n], in_=rs[:, 0:n], func=AF.Sqrt), [rs], [rs])
        S.op("dve", lambda E: E.reciprocal(out=rs[:, 0:n], in_=rs[:, 0:n]), [rs], [rs])

    with ExitStack() as g_es:
        kc_f = alloc(g_es, "kc_f", [128, 6, 128], F32, const=True)
        ident_bf = alloc(g_es, "ident_bf", [128, 128], BF16, const=True)
        neg_bf = alloc(g_es, "neg_bf", [128, 2, 4, 128], BF16, const=True)
        rt_f = alloc(g_es, "rt_f", [128, 128], F32, const=True)
        S.dma("sp", kc_f[:, :, :], k_const.rearrange("c p n -> p c n"), [], [kc_f])
        S.dma("sp", rt_f[:, :], k_rt[:, :], [], [rt_f])
        S.op("dve", lambda E: E.tensor_copy(out=ident_bf[:, :], in_=kc_f[:, 0, :]), [kc_f], [ident_bf])
        for d_ in range(2):
            S.op("dve", lambda E: E.tensor_copy(out=neg_bf[:, d_, :, :], in_=kc_f[:, 4 + d_, :].unsqueeze(1).to_broadcast([128, 4, 128])),
                 [kc_f], [neg_bf])
        tri = [kc_f[:, 1, :], kc_f[:, 2, :]]
        ones_f = kc_f[:, 3, :]

        for L in range(depth):
            last = (L == depth - 1)
            lam_init = 0.8 - 0.6 * math.exp(-0.3 * L)
            S.barrier()
            with ExitStack() as es:
                cT_sb = alloc(es, "a_cT", [128, 8, NB + 1], F32)
                sc_sb = alloc(es, "a_sc", [128, 8, NB + 1], F32)
                wa = alloc(es, "a_wa", [128, 8, 512], F32, n=2)
                bab = alloc(es, "a_bab", [NB + 1, 3 * D], F32)
                mr = alloc(es, "a_mr", [NB + 1, 3 * D], F32)
                psa = palloc(es, "a_ps", [128, 512], F32, n=2)
                for r_ in range(NB + 1):
                    S.dma("sp", cT_sb[:, :, r_], cvec[r_].rearrange("(k p) -> p k", p=128), [], [cT_sb], allow_slow_non_contiguous=True)
                S.dma("sp", bab[:, :], b_ada[L:L + 1, :].partition_broadcast(NB + 1), [], [bab])
                S.op("act", lambda E: E.activation(out=sc_sb[:, :, :], in_=cT_sb[:, :, :], func=AF.Silu), [cT_sb], [sc_sb])
                for n in range(6):
                    w = wa.next()
                    S.dma("sp", w[:, :, :], w_ada[L, :, n * 512:(n + 1) * 512].rearrange("(k p) n -> p k n", p=128), [], [w])
                    ps = psa.next()

                    def mm(E):
                        for k in range(8):
                            ins = E.matmul(ps[0:NB + 1, :], sc_sb[:, k, :], w[:, k, :], start=(k == 0), stop=(k == 7))
                        return ins
                    S.op("pe", mm, [sc_sb, w], [ps])
                    S.op("dve", lambda E: E.tensor_tensor(out=mr[:, n * 512:(n + 1) * 512], in0=ps[0:NB + 1, :],
                                                         in1=bab[:, n * 512:(n + 1) * 512], op=ALU.add), [ps, bab], [mr])
                S.dma("pool", modrow[L], mr[:, :], [mr], [])
            S.barrier()

            with ExitStack() as es:
                win_bf = alloc(es, "win_bf", [128, 8, DIN], BF16)
                with ExitStack() as es2:
                    wst = alloc(es2, "wst", [128, 1544], F32, n=3)
                    ci = 0
                    for k in range(8):
                        for c4 in range(4):
                            st_ = wst.next()
                            c0 = c4 * 1544
                            S.dma("sp", st_[:, :], w_in[L, k * 128:(k + 1) * 128, c0:c0 + 1544], [], [st_])
                            e = ("act", "dve", "pool")[ci % 3]
                            ci += 1
                            if e == "act":
                                S.op("act", lambda E: E.activation(out=win_bf[:, k, c0:c0 + 1544], in_=st_[:, :], func=AF.Copy), [st_], [win_bf])
                            else:
                                S.op(e, lambda E: E.tensor_copy(out=win_bf[:, k, c0:c0 + 1544], in_=st_[:, :]), [st_], [win_bf])
                S.barrier()
                win_bf.const = True
                modT = alloc(es, "modT", [128, NB + 1, 24], F32, const=True)
                for r_ in range(NB + 1):
                    S.dma("sp", modT[:, r_, :], modrow[L, r_].rearrange("(j p) -> p j", p=128), [], [modT], allow_slow_non_contiguous=True)
                S.op("dve", lambda E: E.tensor_scalar(out=modT[:, :, 8:16], in0=modT[:, :, 8:16], scalar1=1.0, scalar2=None, op0=ALU.add),
                     [modT], [modT])
                cng_bc = alloc(es, "cng_bc", [128, 512], F32, const=True)
                S.dma("sp", cng_bc[:, :], chunk_norm_g[L:L + 1, :].partition_broadcast(128), [], [cng_bc])
                bs_bc = alloc(es, "bs_bc", [128, 512], F32, const=True)
                S.dma("sp", bs_bc[:, :], chunk_bs[L:L + 1].rearrange("o g t -> o (g t)").partition_broadcast(128), [], [bs_bc])
                wsT_f = alloc(es, "wsT_f", [128, 4, 128], F32)
                wsT = alloc(es, "wsT", [128, 4, 128], BF16, const=True)
                for g_ in range(4):
                    S.dma("sp", wsT_f[:, g_, :], chunk_ws[L, g_].rearrange("t s -> s t"), [], [wsT_f], allow_slow_non_contiguous=True)
                S.op("dve", lambda E: E.tensor_copy(out=wsT[:, :, :], in_=wsT_f[:, :, :]), [wsT_f], [wsT])

                xt_r = alloc(es, "p_xt", [128, D], F32, n=4)
                junk = alloc(es, "p_junk", [128, D], BF16)
                ms_r = alloc(es, "p_ms", [128, 4], F32, n=8)
                rs_r = alloc(es, "p_rs", [128, 4], F32, n=8)
                xn_r = alloc(es, "p_xn", [128, D], BF16, n=4)
                hT_r = alloc(es, "p_hT", [128, 8, 512], BF16, n=2)
                hTa_r = Rot([Buf(hT_r.bufs[i].t) for i in range(2)])
                hTd_r = Rot([Buf(hT_r.bufs[i].t) for i in range(2)])
                stf_r = alloc(es, "p_stf", [128, 512], F32, n=3)
                stb_r = alloc(es, "p_stb", [128, 512], BF16, n=3)
                cs_r = alloc(es, "p_cs", [128, 2, 512], F32, n=2)
                t1_r = alloc(es, "p_t1", [128, 512], F32, n=2)
                t2_r = alloc(es, "p_t2", [128, 512], F32, n=2)
                ug_r = alloc(es, "p_ug", [128, 4, 512], BF16, n=2)
                gu_r = alloc(es, "p_gu", [128, 512], F32, n=2)
                gv_r = alloc(es, "p_gv", [128, 512], F32, n=4)
                vn_r = alloc(es, "p_vn", [128, 512], BF16, n=4)
                tm_r = alloc(es, "p_tm", [128, 512], F32, n=2)
                ya_r = alloc(es, "p_ya", [128, 512], BF16, n=2)
                dts_r = alloc(es, "p_dts", [128, 32], F32, n=2)
                psT_r = palloc(es, "p_psT", [128, 8, 128], BF16, n=2)
                psM_r = palloc(es, "p_psM", [128, 512], F32, n=4)
                psX_r = palloc(es, "p_psX", [128, 512], F32, n=2)
                print("P1 sbuf remaining", nc.sbuf_bytes_remaining)

                def p1_group(b, t0, N):
                    if True:
                        is_ctx = t0 < NCTX
                        r = NB if is_ctx else b
                        ctx_only = is_ctx and last
                        nt = N // 128
                        hT = hT_r.next()
                        hTs = [hTa_r.next(), hTd_r.next()]
                        xns = []
                        xts = []
                        ms = ms_r.next()
                        rs = rs_r.next()
                        for i in range(nt):
                            xt = xt_r.next()
                            S.dma("sp", xt[:, :], src_rows(L, b, t0 + i * 128, 128), [], [xt])
                            S.op("act", lambda E: E.activation(out=junk[:, :], in_=xt[:, :], func=AF.Square, scale=1.0 / 32.0,
                                                               accum_out=ms[:, i:i + 1]), [xt], [junk, ms])
                            xts.append(xt)
                        rstd_from_ms(ms, rs, nt)
                        for i in range(nt):
                            xn = xn_r.next()
                            S.op("act", lambda E: E.activation(out=xn[:, :], in_=xts[i][:, :], func=AF.Identity, scale=rs[:, i:i + 1]), [xts[i], rs], [xn])
                            xns.append(xn)
                        yield
                        for i in range(nt):
                            xn = xns[i]
                            psT = psT_r.next()

                            def tr(E):
                                for k in range(8):
                                    ins = E.transpose(out=psT[:, k, :], in_=xn[:, k * 128:(k + 1) * 128], identity=ident_bf[:, :])
                                return ins
                            S.op("pe", tr, [xn, ident_bf], [psT])
                            for k in range(8):
                                if i % 2 == 0:
                                    S.op("act", lambda E: E.activation(out=hT[:, k, i * 128:(i + 1) * 128], in_=psT[:, k, :], func=AF.Identity,
                                                                       scale=modT[:, r, 8 + k:9 + k], bias=modT[:, r, k:k + 1]), [psT, modT], [hTs[0]])
                                else:
                                    S.op("dve", lambda E: E.tensor_scalar(out=hT[:, k, i * 128:(i + 1) * 128], in0=psT[:, k, :],
                                                                          scalar1=modT[:, r, 8 + k:9 + k], scalar2=modT[:, r, k:k + 1],
                                                                          op0=ALU.mult, op1=ALU.add), [psT, modT], [hTs[1]])
                        yield

                        def fm_mm(ps, col0):
                            def mm(E):
                                for k in range(8):
                                    ins = E.matmul(ps[:, 0:N], win_bf[:, k, col0:col0 + 128], hT[:, k, 0:N], start=(k == 0), stop=(k == 7))
                                return ins
                            S.op("pe", mm, [win_bf] + hTs, [ps])

                        def tm_mm(ps, i, col0, ncol):
                            def mm(E):
                                for k in range(8):
                                    ins = E.matmul(ps[:, 0:ncol], hT[:, k, i * 128:(i + 1) * 128], win_bf[:, k, col0:col0 + ncol],
                                                   start=(k == 0), stop=(k == 7))
                                return ins
                            S.op("pe", mm, [win_bf] + hTs, [ps])

                        for j in range(12):
                            ps = psM_r.next()
                            fm_mm(ps, C_XBC + j * 128)
                            st_ = stb_r.next()
                            evac_copy(st_[:, 0:N], ps[:, 0:N], [ps], [st_])
                            S.dma("pool", xbcT[b, j * 128:(j + 1) * 128, t0:t0 + N], st_[:, 0:N], [st_], [])
                        if not is_ctx:
                            cs = cs_r.next()
                            S.dma("sp", cs[:, :, 0:N], k_rope[:, :, t0 - NCTX:t0 - NCTX + N].rearrange("c p n -> p c n"), [], [cs])
                        fams = [(C_K, kT)] if ctx_only else [(C_K, kT), (C_Q, qT)]
                        tiles = [(c0, dst, j) for (c0, dst) in fams for j in range(4)]
                        pend = []
                        for ti in range(len(tiles) + 1):
                            if ti < len(tiles):
                                c0, dst, j = tiles[ti]
                                ps = psM_r.next()
                                fm_mm(ps, c0 + j * 128)
                                if is_ctx:
                                    sb = stb_r.next()
                                    evac_copy(sb[:, 0:N], ps[:, 0:N], [ps], [sb])
                                    S.dma("pool", dst[b, j * 128:(j + 1) * 128, t0:t0 + N], sb[:, 0:N], [sb], [])
                                else:
                                    qf = stf_r.next()
                                    S.op("act", lambda E: E.activation(out=qf[:, 0:N], in_=ps[:, 0:N], func=AF.Copy), [ps], [qf])
                                    pend.append((qf, dst, j))
                            if ti >= 1 and not is_ctx:
                                qf, dst, j = pend.pop(0)
                                sb = stb_r.next()
                                psr = psX_r.next()
                                S.op("pe", lambda E: E.matmul(psr[:, 0:N], rt_f[:, :], qf[:, 0:N], start=True, stop=True), [qf, rt_f], [psr])
                                t1 = t1_r.next()
                                t2 = t2_r.next()
                                S.op("pool", lambda E: E.tensor_tensor(out=t1[:, 0:N], in0=qf[:, 0:N], in1=cs[:, 0, 0:N], op=ALU.mult),
                                     [qf, cs], [t1])
                                S.op("dve", lambda E: E.tensor_tensor(out=t2[:, 0:N], in0=psr[:, 0:N], in1=cs[:, 1, 0:N], op=ALU.mult),
                                     [psr, cs], [t2])
                                S.op("pool", lambda E: E.tensor_tensor(out=sb[:, 0:N], in0=t1[:, 0:N], in1=t2[:, 0:N], op=ALU.add),
                                     [t1, t2], [sb])
                                S.dma("pool", dst[b, j * 128:(j + 1) * 128, t0:t0 + N], sb[:, 0:N], [sb], [])
                        yield
                        for i in range(nt):
                            ps = psX_r.next()
                            tm_mm(ps, i, C_DT, 32)
                            ds = dts_r.next()
                            evac_copy(ds[:, :], ps[:, 0:32], [ps], [ds])
                            S.dma("pool", dtraw[b, t0 + i * 128:t0 + (i + 1) * 128, :], ds[:, :], [ds], [])
                        for i in range(nt):
                            ps = psM_r.next()
                            tm_mm(ps, i, C_V, 512)
                            sb = stb_r.next()
                            evac_copy(sb[:, :], ps[:, :], [ps], [sb])
                            S.dma("pool", vtok[b, t0 + i * 128:t0 + (i + 1) * 128, :], sb[:, :], [sb], [])
                        if ctx_only:
                            return
                        ug = ug_r.next()
                        for j in range(4):
                            ps = psM_r.next()
                            fm_mm(ps, C_U + j * 128)
                            gu = gu_r.next()
                            S.op("act", lambda E: E.activation(out=gu[:, 0:N], in_=ps[:, 0:N], func=AF.Gelu_apprx_tanh), [ps], [gu])
                            ps2 = psM_r.next()
                            fm_mm(ps2, C_GA + j * 128)
                            sg = stf_r.next()
                            S.op("act", lambda E: E.activation(out=sg[:, 0:N], in_=ps2[:, 0:N], func=AF.Silu), [ps2], [sg])
                            S.op("pool", lambda E: E.tensor_tensor(out=ug[:, j, 0:N], in0=gu[:, 0:N], in1=sg[:, 0:N], op=ALU.mult), [gu, sg], [ug])
                        for j in range(4):
                            ps = psM_r.next()
                            fm_mm(ps, C_GC + j * 128)
                            sb = stb_r.next()
                            S.op("act", lambda E: E.activation(out=sb[:, 0:N], in_=ps[:, 0:N], func=AF.Silu), [ps], [sb])
                            S.dma("act", gcT[b, j * 128:(j + 1) * 128, t0:t0 + N], sb[:, 0:N], [sb], [])
                        vns = []
                        ms = ms_r.next()
                        rs = rs_r.next()
                        gvs = []
                        for i in range(nt):
                            ps = psM_r.next()
                            tm_mm(ps, i, C_VM, 512)
                            gv = gv_r.next()
                            S.op("act", lambda E: E.activation(out=gv[:, :], in_=ps[:, :], func=AF.Gelu_apprx_tanh), [ps], [gv])
                            S.op("act", lambda E: E.activation(out=junk[:, 0:512], in_=gv[:, :], func=AF.Square, scale=1.0 / math.sqrt(512.0),
                                                               accum_out=ms[:, i:i + 1]), [gv], [junk, ms])
                            gvs.append(gv)
                        rstd_from_ms(ms, rs, nt)
                        for i in range(nt):
                            vn = vn_r.next()
                            S.op("dve", lambda E: E.scalar_tensor_tensor(out=vn[:, :], in0=gvs[i][:, :], scalar=rs[:, i:i + 1], in1=cng_bc[:, :],
                                                                         op0=ALU.mult, op1=ALU.mult), [gvs[i], rs, cng_bc], [vn])
                            vns.append(vn)
                        for i in range(nt):
                            for (c0, dst, dc0) in ((C_Z, zs, 0), (C_Z + 512, zs, 512)):
                                ps = psM_r.next()
                                tm_mm(ps, i, c0, 512)
                                sb = stb_r.next()
                                S.op("act", lambda E: E.activation(out=sb[:, :], in_=ps[:, :], func=AF.Silu), [ps], [sb])
                                S.dma("act", dst[b, t0 + i * 128:t0 + (i + 1) * 128, dc0:dc0 + 512], sb[:, :], [sb], [])
                        for i in range(nt):
                            vn = vns[i]
                            psm = psX_r.next()

                            def mix(E):
                                for g in range(4):
                                    ins = E.matmul(psm[:, g * 128:(g + 1) * 128], vn[:, g * 128:(g + 1) * 128], wsT[:, g, :], start=True, stop=True)
                                return ins
                            S.op("pe", mix, [vn, wsT], [psm])
                            tm = tm_r.next()
                            S.op("dve", lambda E: E.tensor_tensor(out=tm[:, :], in0=psm[:, :], in1=bs_bc[:, :], op=ALU.add), [psm, bs_bc], [tm])
                            ya = ya_r.next()
                            S.op("pool", lambda E: E.tensor_tensor(out=ya[:, :].rearrange("p (g t) -> p g t", g=4),
                                                                   in0=tm[:, :].rearrange("p (g t) -> p g t", g=4),
                                                                   in1=ug[:, :, i * 128:(i + 1) * 128], op=ALU.mult), [tm, ug], [ya])
                            S.dma("pool", catT[b, 0:512, t0 + i * 128:t0 + (i + 1) * 128].rearrange("(g p) t -> p g t", p=128),
                                  ya[:, :].rearrange("p (g t) -> p g t", g=4), [ya], [])

                glist = []
                for b in range(NB):
                    glist.append(p1_group(b, 0, NCTX))
                    for g in range((NLAT + 511) // 512):
                        glist.append(p1_group(b, NCTX + g * 512, min(512, NLAT - g * 512)))
                next(glist[0])
                next(glist[0])
                for gi in range(len(glist)):
                    if gi + 1 < len(glist):
                        next(glist[gi + 1])
                    next(glist[gi])
                    if gi + 1 < len(glist):
                        next(glist[gi + 1])
                    for _ in glist[gi]:
                        pass
                S.barrier()

            with ExitStack() as es:
                cw = alloc(es, "s_cw", [128, 12, 5], F32, const=True)
                cb = alloc(es, "s_cb", [128, 12], F32, const=True)
                for k_ in range(5):
                    S.dma("sp", cw[:, :, k_], conv_w[L, k_].rearrange("(j p) -> p j", p=128), [], [cw], allow_slow_non_contiguous=True)
                S.dma("sp", cb[:, :], conv_b[L].rearrange("(j p) -> p j", p=128), [], [cb], allow_slow_non_contiguous=True)
                dtb_bc = alloc(es, "s_dtb", [128, 32], F32, const=True)
                aneg_bc = alloc(es, "s_aneg", [128, 32], F32, const=True)
                S.dma("sp", dtb_bc[:, :], dt_bias[L:L + 1, :].partition_broadcast(128), [], [dtb_bc])
                S.dma("sp", aneg_bc[:, :], a_log[L:L + 1, :].partition_broadcast(128), [], [aneg_bc])
                S.op("act", lambda E: E.activation(out=aneg_bc[:, :], in_=aneg_bc[:, :], func=AF.Exp), [aneg_bc], [aneg_bc])
                S.op("dve", lambda E: E.tensor_scalar(out=aneg_bc[:, :], in0=aneg_bc[:, :], scalar1=-1.0, scalar2=None, op0=ALU.mult),
                     [aneg_bc], [aneg_bc])
                raw_r = alloc(es, "s_raw", [128, 12, 516], BF16, n=2)
                dg = alloc(es, "s_dg", [128, 12, 5, 128], BF16, const=True)
                for j in range(12):
                    for k in range(5):
                        S.op("dve", lambda E: E.tensor_scalar(out=dg[:, j, k, :], in0=kc_f[:, 0, :], scalar1=cw[:, j, k:k + 1], scalar2=None,
                                                              op0=ALU.mult), [kc_f, cw], [dg])
                psC_r = palloc(es, "s_psC", [128, 512], F32, n=4)
                sil_r = [alloc(es, f"s_sil{j}_", [128, 512], BF16, n=2) for j in range(12)]
                xhs_r = alloc(es, "s_xhs", [128, 1024], BF16, n=2)
                bts_r = alloc(es, "s_bts", [128, 256], BF16, n=2)
                dtt_r = alloc(es, "s_dtt", [128, 4, 32], F32, n=2)
                dtw = [alloc(es, f"s_dtw{i}", [128, 4, 32], F32) for i in range(4)]
                dto_r = alloc(es, "s_dto", [128, 4, 64], F32, n=2)
                psT_r = palloc(es, "s_psT", [128, 8, 128], BF16, n=2)
                psB_r = palloc(es, "s_psB", [128, 2, 128], BF16, n=2)
                for b in range(NB):
                    groups = [(0, NCTX, 0, NCTX)] + [(NCTX + g * 512, min(512, NLAT - g * 512), NCTX, T) for g in range((NLAT + 511) // 512)]
                    for (t0, N, s0, s1) in groups:
                        nt = N // 128
                        raw = raw_r.next()
                        lo = 0
                        hi = N + 4
                        if t0 == s0:
                            S.op("pool", lambda E: E.memset(raw[:, :, 0:2], 0.0), [], [raw])
                            lo = 2
                        if t0 + N == s1:
                            S.op("pool", lambda E: E.memset(raw[:, :, N + 2:N + 4], 0.0), [], [raw])
                            hi = N + 2
                        S.dma("sp", raw[:, :, lo:hi], xbcT[b, :, t0 - 2 + lo:t0 - 2 + hi].rearrange("(j p) t -> p j t", p=128), [], [raw])
                        sils = []
                        for j in range(12):
                            acc = psC_r.next()

                            def cmm(E):
                                for k in range(5):
                                    ins = E.matmul(acc[:, 0:N], dg[:, j, k, :], raw[:, j, k:k + N], start=(k == 0), stop=(k == 4))
                                return ins
                            S.op("pe", cmm, [dg, raw], [acc])
                            sil = sil_r[j].next()
                            S.op("act", lambda E: E.activation(out=sil[:, 0:N], in_=acc[:, 0:N], func=AF.Silu, bias=cb[:, j:j + 1], scale=1.0),
                                 [acc, cb], [sil])
                            sils.append(sil)
                            if j >= 8:
                                dst = bT_d if j < 10 else cT_d
                                jj = (j - 8) % 2
                                S.dma("pool", dst[b, jj * 128:(jj + 1) * 128, t0:t0 + N], sil[:, 0:N], [sil], [])
                        for i in range(nt):
                            psT = psT_r.next()

                            def tr(E):
                                for j in range(8):
                                    ins = E.transpose(out=psT[:, j, :], in_=sils[j][:, i * 128:(i + 1) * 128], identity=ident_bf[:, :])
                                return ins
                            S.op("pe", tr, sils[0:8] + [ident_bf], [psT])
                            xhs = xhs_r.next()
                            evac_copy(xhs[:, :].rearrange("p (j t) -> p j t", j=8), psT[:, :, :], [psT], [xhs])
                            S.dma("pool", xh_d[b, t0 + i * 128:t0 + (i + 1) * 128, :], xhs[:, :], [xhs], [])
                            psB = psB_r.next()

                            def tr2(E):
                                for j in range(2):
                                    ins = E.transpose(out=psB[:, j, :], in_=sils[8 + j][:, i * 128:(i + 1) * 128], identity=ident_bf[:, :])
                                return ins
                            S.op("pe", tr2, sils[8:10] + [ident_bf], [psB])
                            bts = bts_r.next()
                            evac_copy(bts[:, :].rearrange("p (j t) -> p j t", j=2), psB[:, :, :], [psB], [bts])
                            S.dma("pool", btm_d[b, t0 + i * 128:t0 + (i + 1) * 128, :], bts[:, :], [bts], [])
                        dtt = dtt_r.next()
                        S.dma("sp", dtt[:, 0:nt, :], dtraw[b, t0:t0 + N, :].rearrange("(i p) c -> p i c", p=128), [], [dtt])
                        v_, ax, ee, ll = dtw
                        bc = dtb_bc[:, :].unsqueeze(1).to_broadcast([128, nt, 32])
                        S.op("dve", lambda E: E.tensor_tensor(out=v_[:, 0:nt, :], in0=dtt[:, 0:nt, :], in1=bc, op=ALU.add), [dtt, dtb_bc], [v_])
                        S.op("act", lambda E: E.activation(out=ax[:, 0:nt, :], in_=v_[:, 0:nt, :], func=AF.Abs), [v_], [ax])
                        S.op("act", lambda E: E.activation(out=ee[:, 0:nt, :], in_=ax[:, 0:nt, :], func=AF.Exp, scale=-1.0), [ax], [ee])
                        S.op("dve", lambda E: E.tensor_scalar(out=ee[:, 0:nt, :], in0=ee[:, 0:nt, :], scalar1=1.0, scalar2=None, op0=ALU.add), [ee], [ee])
                        S.op("act", lambda E: E.activation(out=ll[:, 0:nt, :], in_=ee[:, 0:nt, :], func=AF.Ln), [ee], [ll])
                        dto = dto_r.next()
                        S.op("dve", lambda E: E.scalar_tensor_tensor(out=dto[:, 0:nt, 0:32], in0=v_[:, 0:nt, :], scalar=0.0, in1=ll[:, 0:nt, :],
                                                                     op0=ALU.max, op1=ALU.add), [v_, ll], [dto])
                        S.op("dve", lambda E: E.tensor_tensor(out=dto[:, 0:nt, 32:64], in0=dto[:, 0:nt, 0:32],
                                                             in1=aneg_bc[:, :].unsqueeze(1).to_broadcast([128, nt, 32]), op=ALU.mult),
                             [dto, aneg_bc], [dto])
                        S.dma("pool", dta[b, t0:t0 + N, :].rearrange("(i p) c -> p i c", p=128), dto[:, 0:nt, :], [dto], [])
                S.barrier()

            with ExitStack() as es:
                def mk_stream(sn):
                    B_ = {}
                    B_["xh"] = alloc(es, f"c{sn}_xh", [128, 1024], BF16, n=2)
                    B_["btm"] = alloc(es, f"c{sn}_btm", [128, 256], BF16, n=2)
                    B_["bT"] = alloc(es, f"c{sn}_bT", [128, 2, 128], BF16, n=2)
                    B_["cT"] = alloc(es, f"c{sn}_cT", [128, 2, 128], BF16, n=2)
                    B_["dta"] = alloc(es, f"c{sn}_dta", [128, 64], F32, n=2)
                    B_["acs"] = alloc(es, f"c{sn}_acs", [128, 48], F32)
                    B_["dec"] = alloc(es, f"c{sn}_dec", [128, 48], F32)
                    B_["nacs"] = alloc(es, f"c{sn}_nacs", [128, 16], F32)
                    B_["dtd"] = alloc(es, f"c{sn}_dtd", [128, 16], F32)
                    B_["rhsD"] = alloc(es, f"c{sn}_rhsD", [128, 16, 128], F32)
                    B_["ex"] = alloc(es, f"c{sn}_ex", [128, 4, 128], BF16, n=2)
                    B_["MT"] = alloc(es, f"c{sn}_MT", [128, 16, 128], BF16)
                    B_["cbT"] = alloc(es, f"c{sn}_cbT", [128, 2, 128], BF16)
                    B_["xdt"] = alloc(es, f"c{sn}_xdt", [128, 1024], BF16)
                    B_["xdd"] = alloc(es, f"c{sn}_xdd", [128, 1024], BF16)
                    B_["yo"] = Rot([alloc(es, f"c{sn}_yo", [128, 512], F32)])
                    B_["y"] = Rot([alloc(es, f"c{sn}_y", [128, 1024], F32)])
                    B_["stf"] = [alloc(es, f"c{sn}_stf{g}_", [128, 512], F32) for g in range(2)]
                    B_["stt"] = Rot([alloc(es, f"c{sn}_stt", [128, 512], F32)])
                    B_["stb"] = [alloc(es, f"c{sn}_stb{g}_", [128, 512], BF16) for g in range(2)]
                    B_.update(psets[sn % 2])
                    return B_
                psets = [{"psS": palloc(es, f"c{i}_psS", [128, 512], F32), "psD": palloc(es, f"c{i}_psD", [128, 4, 128], F32),
                          "psY": palloc(es, f"c{i}_psY", [128, 512], F32), "psO": palloc(es, f"c{i}_psO", [128, 512], F32)} for i in range(2)]
                streams = [mk_stream(i) for i in range(2 * NB)]
                print("scan sbuf remaining", nc.sbuf_bytes_remaining)

                STOPAT = int(os.environ.get("STOPAT", "99"))

                def scan_stream(b, dr, B_):
                    psS, psD, psY, psO = B_["psS"], B_["psD"], B_["psY"], B_["psO"]
                    stf, stb = B_["stf"], B_["stb"]
                    if dr == 0:
                        order = list(range(NCH))
                        ydst = yf_d
                    else:
                        order = list(range(NCC - 1, -1, -1)) + list(range(NCH - 1, NCC - 1, -1))
                        ydst = yb_d
                    for g in range(2):
                        S.op("pool", lambda E: E.memset(stf[g][:, :], 0.0), [], [stf[g]])
                        S.op("pool", lambda E: E.memset(stb[g][:, :], 0.0), [], [stb[g]])
                    yield
                    for c in order:
                        tk = slice(c * 128, (c + 1) * 128)
                        xh = B_["xh"].next()
                        S.dma("sp", xh[:, :], xh_d[b, tk, :], [], [xh])
                        btm = B_["btm"].next()
                        S.dma("sp", btm[:, :], btm_d[b, tk, :], [], [btm])
                        bT = B_["bT"].next()
                        S.dma("sp", bT[:, :, :], bT_d[b, :, tk].rearrange("(g p) t -> p g t", p=128), [], [bT])
                        cT = B_["cT"].next()
                        S.dma("sp", cT[:, :, :], cT_d[b, :, tk].rearrange("(g p) t -> p g t", p=128), [], [cT])
                        da = B_["dta"].next()
                        S.dma("sp", da[:, :], dta[b, tk, :], [], [da])
                        dt_d = da[:, dr * 16:(dr + 1) * 16]
                        a_d = da[:, 32 + dr * 16:32 + (dr + 1) * 16]

                        def cs_mm(E):
                            E.matmul(psS[:, 0:16], tri[dr], a_d, start=True, stop=True)
                            E.matmul(psS[:, 16:32], ones_f, a_d, start=True, stop=True)
                            for g in range(2):
                                ins = E.matmul(psS[:, 256 + g * 128:256 + (g + 1) * 128], bT[:, g, :], cT[:, g, :], start=True, stop=True)
                            return ins
                        S.op("pe", cs_mm, [kc_f, da, bT, cT], [psS])
                        rhsD = B_["rhsD"]
                        S.op("pool", lambda E: E.tensor_tensor(out=rhsD[:, :, :], in0=a_d.unsqueeze(2).to_broadcast([128, 16, 128]),
                                                              in1=tri[dr].unsqueeze(1).to_broadcast([128, 16, 128]), op=ALU.mult), [da, kc_f], [rhsD])
                        pass
                        if STOPAT < 1:
                            continue
                        acs = B_["acs"]
                        S.op("act", lambda E: E.activation(out=acs[:, 0:32], in_=psS[:, 0:32], func=AF.Copy), [psS], [acs])
                        cbT = B_["cbT"]
                        S.op("act", lambda E: E.activation(out=cbT[:, :, :], in_=psS[:, 256:512].rearrange("p (g l) -> p g l", g=2), func=AF.Copy), [psS], [cbT])
                        yield
                        S.op("dve", lambda E: E.tensor_tensor(out=acs[:, 32:48], in0=acs[:, 16:32], in1=acs[:, 0:16], op=ALU.subtract), [acs], [acs])
                        nacs = B_["nacs"]
                        S.op("pool", lambda E: E.tensor_scalar(out=nacs[:, :], in0=acs[:, 0:16], scalar1=-1.0, scalar2=None, op0=ALU.mult), [acs], [nacs])
                        yield
                        dec = B_["dec"]
                        S.op("act", lambda E: E.activation(out=dec[:, :], in_=acs[:, :], func=AF.Exp), [acs], [dec])
                        xdt = B_["xdt"]
                        S.op("dve", lambda E: E.tensor_tensor(out=xdt[:, :].rearrange("p (h e) -> p h e", h=16),
                                                             in0=xh[:, :].rearrange("p (h e) -> p h e", h=16),
                                                             in1=dt_d.unsqueeze(2).to_broadcast([128, 16, 64]), op=ALU.mult), [xh, da], [xdt])
                        yield
                        dtd = B_["dtd"]
                        S.op("dve", lambda E: E.tensor_tensor(out=dtd[:, :], in0=dt_d, in1=dec[:, 32:48], op=ALU.mult), [da, dec], [dtd])
                        yield
                        xdd = B_["xdd"]
                        S.op("pool", lambda E: E.tensor_tensor(out=xdd[:, :].rearrange("p (h e) -> p h e", h=16),
                                                              in0=xh[:, :].rearrange("p (h e) -> p h e", h=16),
                                                              in1=dtd[:, :].unsqueeze(2).to_broadcast([128, 16, 64]), op=ALU.mult), [xh, dtd], [xdd])
                        MT = B_["MT"]
                        y = B_["y"].next()
                        if STOPAT < 2:
                            continue
                        for g in range(2):
                            for q4 in range(2):
                                h0 = g * 8 + q4 * 4

                                def d_mm(E):
                                    E.matmul(psD[:, :, :], ones_f, rhsD[:, h0:h0 + 4, :], start=True, stop=False)
                                    return E.matmul(psD[:, :, :], ident_bf[:, :], neg_bf[:, dr, :, :], start=False, stop=True)
                                S.op("pe", d_mm, [rhsD, kc_f, ident_bf, neg_bf], [psD])
                                pass
                                ex = B_["ex"].next()
                                for h4 in range(4):
                                    h = h0 + h4
                                    S.op("act", lambda E: E.activation(out=ex[:, h4, :], in_=psD[:, h4, :], func=AF.Exp, bias=nacs[:, h:h + 1], scale=1.0),
                                         [psD, nacs], [ex])
                                yield
                                S.op("dve", lambda E: E.tensor_tensor(out=MT[:, h0:h0 + 4, :], in0=ex[:, :, :],
                                                                     in1=cbT[:, g, :].unsqueeze(1).to_broadcast([128, 4, 128]), op=ALU.mult), [ex, cbT], [MT])
                                yield
                            if STOPAT < 3:
                                continue

                            def yd_mm(E):
                                for h8 in range(8):
                                    h = g * 8 + h8
                                    ins = E.matmul(psY[:, h8 * 64:(h8 + 1) * 64], MT[:, h, :], xdt[:, h * 64:(h + 1) * 64], start=True, stop=True)
                                return ins
                            S.op("pe", yd_mm, [MT, xdt], [psY])
                            S.op("pe", lambda E: E.matmul(psO[:, :], cT[:, g, :], stb[g][:, :], start=True, stop=True), [cT, stb[g]], [psO])
                            pass
                            if STOPAT < 4:
                                continue
                            yo = B_["yo"].next()
                            S.op("dve", lambda E: E.tensor_tensor(out=yo[:, :].rearrange("p (h e) -> p h e", h=8),
                                                                 in0=psO[:, :].rearrange("p (h e) -> p h e", h=8),
                                                                 in1=dec[:, g * 8:(g + 1) * 8].unsqueeze(2).to_broadcast([128, 8, 64]), op=ALU.mult),
                                 [psO, dec], [yo])
                            pass
                            S.op("dve", lambda E: E.tensor_tensor(out=y[:, g * 512:(g + 1) * 512], in0=psY[:, :], in1=yo[:, :], op=ALU.add), [psY, yo], [y])
                            S.op("pe", lambda E: E.matmul(psO[:, :], btm[:, g * 128:(g + 1) * 128], xdd[:, g * 512:(g + 1) * 512], start=True, stop=True),
                                 [btm, xdd], [psO])
                            stt = B_["stt"].next()
                            S.op("pool", lambda E: E.tensor_tensor(out=stt[:, :].rearrange("p (h e) -> p h e", h=8),
                                                                  in0=stf[g][:, :].rearrange("p (h e) -> p h e", h=8),
                                                                  in1=dec[:, 16 + g * 8:16 + (g + 1) * 8].unsqueeze(2).to_broadcast([128, 8, 64]), op=ALU.mult),
                                 [stf[g], dec], [stt])
                            pass
                            S.op("dve", lambda E: E.tensor_tensor(out=stf[g][:, :], in0=psO[:, :], in1=stt[:, :], op=ALU.add), [psO, stt], [stf[g]])
                            yield
                            S.op("act", lambda E: E.activation(out=stb[g][:, :], in_=stf[g][:, :], func=AF.Copy), [stf[g]], [stb[g]])
                            yield
                        if not (last and c < NCC) and STOPAT > 5:
                            S.dma("act", ydst[b, tk, :], y[:, :], [y], [])
                        yield

                gens = [scan_stream(b, dr, streams[b * 2 + dr]) for b in range(NB) for dr in range(2)]
                alive = [True] * len(gens)
                if os.environ.get("SEQ_SCAN"):
                    for gen in gens:
                        for _ in gen:
                            pass
                    alive = [False] * len(gens)
                while any(alive):
                    for gi, gen in enumerate(gens):
                        if alive[gi]:
                            try:
                                next(gen)
                            except StopIteration:
                                alive[gi] = False
                S.barrier()

            with ExitStack() as es:
                dsk_bc = alloc(es, "c_dsk", [128, 16], F32, const=True)
                sng_bc = alloc(es, "c_sng", [128, 1024], F32, const=True)
                S.dma("sp", dsk_bc[:, :], ssm_d[L:L + 1, :].partition_broadcast(128), [], [dsk_bc])
                S.dma("sp", sng_bc[:, :], ssm_norm_g[L:L + 1, :].partition_broadcast(128), [], [sng_bc])
                xh_r = alloc(es, "e_xh", [128, 1024], BF16, n=4)
                yf_r = alloc(es, "e_yf", [128, 1024], F32, n=4)
                yb_r = alloc(es, "e_yb", [128, 1024], F32, n=4)
                zs_r = alloc(es, "e_zs", [128, 1024], BF16, n=4)
                xd_r = alloc(es, "e_xd", [128, 1024], F32, n=4)
                yn_r = alloc(es, "e_yn", [128, 1024], BF16, n=4)
                ybT_r = alloc(es, "e_ybT", [128, 8, 128], BF16, n=4)
                junk = alloc(es, "e_junk", [128, 512], BF16)
                ms_r = alloc(es, "e_ms", [128, 2], F32, n=4)
                rs_r = alloc(es, "e_rs", [128, 2], F32, n=4)
                psT_r = palloc(es, "e_psT", [128, 8, 128], BF16, n=2)
                for b in range(NB if not os.environ.get("SKIP_OUT") else 0):
                    for c in (range(NCC, NCH) if last else range(NCH)):
                        tk = slice(c * 128, (c + 1) * 128)
                        xh = xh_r.next()
                        S.dma("sp", xh[:, :], xh_d[b, tk, :], [], [xh])
                        yf = yf_r.next()
                        S.dma("sp", yf[:, :], yf_d[b, tk, :], [], [yf])
                        yb = yb_r.next()
                        S.dma("sp", yb[:, :], yb_d[b, tk, :], [], [yb])
                        zt = zs_r.next()
                        S.dma("sp", zt[:, :], zs[b, tk, :], [], [zt])
                        xd = xd_r.next()
                        S.op("pool", lambda E: E.tensor_tensor(out=xd[:, :].rearrange("p (h e) -> p h e", h=16),
                                                              in0=xh[:, :].rearrange("p (h e) -> p h e", h=16),
                                                              in1=dsk_bc[:, :].unsqueeze(2).to_broadcast([128, 16, 64]), op=ALU.mult), [xh, dsk_bc], [xd])
                        S.op("dve", lambda E: E.tensor_tensor(out=yf[:, :], in0=yf[:, :], in1=yb[:, :], op=ALU.add), [yf, yb], [yf])
                        S.op("dve", lambda E: E.tensor_tensor(out=yf[:, :], in0=yf[:, :], in1=xd[:, :], op=ALU.add), [yf, xd], [yf])
                        S.op("dve", lambda E: E.tensor_tensor(out=yb[:, :], in0=yf[:, :], in1=zt[:, :], op=ALU.mult), [yf, zt], [yb])
                        ms = ms_r.next()
                        rs = rs_r.next()
                        for g in range(2):
                            S.op("act", lambda E: E.activation(out=junk[:, :], in_=yb[:, g * 512:(g + 1) * 512], func=AF.Square,
                                                               scale=1.0 / math.sqrt(512.0), accum_out=ms[:, g:g + 1]), [yb], [junk, ms])
                        rstd_from_ms(ms, rs, 2)
                        yn = yn_r.next()
                        for g in range(2):
                            S.op("dve", lambda E: E.scalar_tensor_tensor(out=yn[:, g * 512:(g + 1) * 512], in0=yb[:, g * 512:(g + 1) * 512],
                                                                         scalar=rs[:, g:g + 1], in1=sng_bc[:, g * 512:(g + 1) * 512],
                                                                         op0=ALU.mult, op1=ALU.mult), [yb, rs, sng_bc], [yn])
                        psT = psT_r.next()

                        def tr(E):
                            for j in range(8):
                                ins = E.transpose(out=psT[:, j, :], in_=yn[:, j * 128:(j + 1) * 128], identity=ident_bf[:, :])
                            return ins
                        S.op("pe", tr, [yn, ident_bf], [psT])
                        ybT = ybT_r.next()
                        S.op("act", lambda E: E.activation(out=ybT[:, :, :], in_=psT[:, :, :], func=AF.Copy), [psT], [ybT])
                        S.dma("act", catT[b, 512:1536, tk].rearrange("(j p) t -> p j t", p=128), ybT[:, :, :], [ybT], [])
                S.barrier()

            with ExitStack() as es:
                lp = alloc(es, "t_lp", [128, 256], F32)
                lw = alloc(es, "t_lw", [128, 128], F32)
                lsum = alloc(es, "t_ls", [128, 2], F32)
                nlam = alloc(es, "t_nlam", [128, 1], F32, const=True)
                sgc = alloc(es, "t_sgc", [128, 1], F32, const=True)
                S.dma("sp", lp[:, :], lam_p[L:L + 1, :].partition_broadcast(128), [], [lp])
                S.dma("sp", sgc[:, :], subln_g[L].rearrange("(e o) -> e o", o=1), [], [sgc])
                S.op("dve", lambda E: E.tensor_scalar(out=sgc[:, :], in0=sgc[:, :], scalar1=1.0 - lam_init, scalar2=None, op0=ALU.mult),
                     [sgc], [sgc])
                for i2 in range(2):
                    S.op("dve", lambda E: E.tensor_tensor(out=lw[:, i2 * 64:(i2 + 1) * 64], in0=lp[:, i2 * 128:i2 * 128 + 64],
                                                         in1=lp[:, i2 * 128 + 64:i2 * 128 + 128], op=ALU.mult), [lp], [lw])
                    S.op("act", lambda E: E.activation(out=lp[:, i2 * 64:(i2 + 1) * 64], in_=lw[:, i2 * 64:(i2 + 1) * 64], func=AF.Copy,
                                                       accum_out=lsum[:, i2:i2 + 1]), [lw], [lp, lsum])
                S.op("act", lambda E: E.activation(out=lsum[:, :], in_=lsum[:, :], func=AF.Exp), [lsum], [lsum])
                S.op("dve", lambda E: E.tensor_tensor(out=nlam[:, :], in0=lsum[:, 1:2], in1=lsum[:, 0:1], op=ALU.subtract), [lsum], [nlam])
                S.op("dve", lambda E: E.tensor_scalar(out=nlam[:, :], in0=nlam[:, :], scalar1=-lam_init, scalar2=None, op0=ALU.add), [nlam], [nlam])
                kTh_r = alloc(es, "t_kT", [128, T], BF16, n=2)
                vh_r = alloc(es, "t_vh", [128, NCH, 128], BF16, n=2)
                qz_r = alloc(es, "t_qz", [128, 2, 512], BF16, n=2)
                for qb in qz_r.bufs:
                    S.op("pool", lambda E: E.memset(qb[:, :, :], 0.0), [], [qb])
                gc_r = alloc(es, "t_gc", [128, 512], BF16, n=3)
                pT_r = alloc(es, "t_pT", [128, 2, 512], BF16, n=5)
                pac1_r = alloc(es, "t_pac1", [128, 512], F32, n=2)
                den0_r = alloc(es, "t_den0", [128, 512], F32, n=2)
                ones_bf = alloc(es, "t_ones", [128, 128], BF16, const=True)
                S.op("dve", lambda E: E.tensor_copy(out=ones_bf[:, :], in_=ones_f), [kc_f], [ones_bf])
                fin_pending = [None]
                osb_r = alloc(es, "t_osb", [128, 2, 512], F32, n=3)
                rden_r = alloc(es, "t_rden", [128, 2, 512], F32, n=1)
                o_r = alloc(es, "t_o", [128, 512], F32, n=2)
                sq_r = alloc(es, "t_sq", [128, 512], F32, n=2)
                rsd_r = alloc(es, "t_rsd", [128, 512], F32, n=2)
                ycT_r = alloc(es, "t_ycT", [128, 512], BF16, n=2)
                psS_r = palloc(es, "t_psS", [128, 2, 512], F32, n=2)
                psO = palloc(es, "t_psO", [128, 2, 512], F32)
                psD = palloc(es, "t_psD", [128, 512], F32)
                psDen = palloc(es, "t_psDen", [128, 512], F32)
                print("attn sbuf remaining", nc.sbuf_bytes_remaining)
                for b in range(NB):
                    for h in range(4):
                        kTh = kTh_r.next()
                        S.dma("sp", kTh[:, :], kT[b, h * 128:(h + 1) * 128, :], [], [kTh])
                        vh = vh_r.next()
                        S.dma("sp", vh[:, :, :], vtok[b, :, h * 128:(h + 1) * 128].rearrange("(c p) e -> p c e", p=128), [], [vh])
                        qgroups = [(NCTX + g * 512, min(512, NLAT - g * 512), 0, NCH) for g in range((NLAT + 511) // 512)]
                        if not last:
                            qgroups = [(0, NCTX, 0, NCC)] + qgroups
                        for (t0, N, k0, k1) in qgroups:
                            qz = qz_r.next()
                            S.dma("sp", qz[0:64, 0, 0:N], qT[b, h * 128:h * 128 + 64, t0:t0 + N], [], [qz])
                            S.dma("sp", qz[64:128, 1, 0:N], qT[b, h * 128 + 64:(h + 1) * 128, t0:t0 + N], [], [qz])
                            gc = gc_r.next()
                            S.dma("sp", gc[:, 0:N], gcT[b, h * 128:(h + 1) * 128, t0:t0 + N], [], [gc])
                            pac1 = pac1_r.next()
                            pend = []
                            for idx in range(k0, k1 + 1):
                                if idx < k1:
                                    ps = psS_r.next()

                                    def smm(E):
                                        for i2 in range(2):
                                            ins = E.matmul(ps[:, i2, 0:N], kTh[i2 * 64:(i2 + 1) * 64, idx * 128:(idx + 1) * 128],
                                                           qz[i2 * 64:(i2 + 1) * 64, i2, 0:N], start=True, stop=True, tile_position=(i2 * 64, 0))
                                        return ins
                                    S.op("pe", smm, [kTh, qz], [ps])
                                    pend.append(ps)
                                if idx > k0:
                                    kt = idx - 1
                                    ps = pend.pop(0)
                                    pT = pT_r.next()
                                    S.op("act", lambda E: E.activation(out=pT[:, :, 0:N], in_=ps[:, :, 0:N], func=AF.Exp, scale=0.125), [ps], [pT])

                                    def pv(E):
                                        for i2 in range(2):
                                            ins = E.matmul(psO[:, i2, 0:N], vh[:, kt, :], pT[:, i2, 0:N], start=(kt == k0), stop=(kt == k1 - 1))
                                        return ins
                                    S.op("pe", pv, [pT, vh], [psO])
                                    S.op("pe", lambda E: E.matmul(psDen[:, 0:N], ones_bf[:, :], pT[:, 0, 0:N], start=(kt == k0), stop=(kt == k1 - 1)),
                                         [pT, ones_bf], [psDen])
                                    if kt == k0:
                                        S.op("dve", lambda E: E.tensor_copy(out=pac1[:, 0:N], in_=pT[:, 1, 0:N]), [pT], [pac1])
                                    else:
                                        S.op("dve", lambda E: E.tensor_tensor(out=pac1[:, 0:N], in0=pac1[:, 0:N], in1=pT[:, 1, 0:N], op=ALU.add),
                                             [pT, pac1], [pac1])
                                    if fin_pending[0] is not None and (kt - k0) % 2 == 1:
                                        try:
                                            next(fin_pending[0])
                                        except StopIteration:
                                            fin_pending[0] = None
                            if fin_pending[0] is not None:
                                for _ in fin_pending[0]:
                                    pass
                                fin_pending[0] = None
                            osb = osb_r.next()
                            S.op("act", lambda E: E.activation(out=osb[:, 0, 0:N], in_=psO[:, 0, 0:N], func=AF.Copy), [psO], [osb])
                            S.op("dve", lambda E: E.tensor_copy(out=osb[:, 1, 0:N], in_=psO[:, 1, 0:N]), [psO], [osb])
                            den0 = den0_r.next()
                            S.op("act", lambda E: E.activation(out=den0[:, 0:N], in_=psDen[:, 0:N], func=AF.Copy), [psDen], [den0])

                            def fin(osb=osb, gc=gc, N=N, b=b, h=h, t0=t0, pac1=pac1, den0=den0):
                                S.op("pe", lambda E: E.matmul(psD[:, 0:N], ones_f, pac1[:, 0:N], start=True, stop=True), [kc_f, pac1], [psD])
                                yield
                                rden = rden_r
                                for q0 in range(0, N, 256):
                                    S.op("dve", lambda E: E.reciprocal(out=rden[:, 0, q0:q0 + 256], in_=den0[:, q0:q0 + 256]), [den0], [rden])
                                    yield
                                    S.op("dve", lambda E: E.reciprocal(out=rden[:, 1, q0:q0 + 256], in_=psD[:, q0:q0 + 256]), [psD], [rden])
                                    yield
                                S.op("pool", lambda E: E.tensor_tensor(out=osb[:, :, 0:N], in0=osb[:, :, 0:N], in1=rden[:, :, 0:N], op=ALU.mult),
                                     [osb, rden], [osb])
                                yield
                                o = o_r.next()
                                S.op("dve", lambda E: E.scalar_tensor_tensor(out=o[:, 0:N], in0=osb[:, 1, 0:N], scalar=nlam[:, 0:1], in1=osb[:, 0, 0:N],
                                                                             op0=ALU.mult, op1=ALU.add), [osb, nlam], [o])
                                yield
                                sq = sq_r.next()
                                S.op("pool", lambda E: E.tensor_tensor(out=sq[:, 0:N], in0=o[:, 0:N], in1=o[:, 0:N], op=ALU.mult), [o], [sq])
                                yield
                                S.op("pe", lambda E: E.matmul(psD[:, 0:N], ones_f, sq[:, 0:N], start=True, stop=True), [kc_f, sq], [psD])
                                yield
                                rsd = rsd_r.next()
                                S.op("dve", lambda E: E.tensor_scalar(out=rsd[:, 0:N], in0=psD[:, 0:N], scalar1=1.0 / 128.0, scalar2=EPS,
                                                                     op0=ALU.mult, op1=ALU.add), [psD], [rsd])
                                yield
                                S.op("act", lambda E: E.activation(out=rsd[:, 0:N], in_=rsd[:, 0:N], func=AF.Ln), [rsd], [rsd])
                                yield
                                S.op("act", lambda E: E.activation(out=rsd[:, 0:N], in_=rsd[:, 0:N], func=AF.Exp, scale=-0.5), [rsd], [rsd])
                                yield
                                S.op("pool", lambda E: E.tensor_tensor(out=o[:, 0:N], in0=o[:, 0:N], in1=rsd[:, 0:N], op=ALU.mult), [o, rsd], [o])
                                yield
                                ycT = ycT_r.next()
                                S.op("dve", lambda E: E.scalar_tensor_tensor(out=ycT[:, 0:N], in0=o[:, 0:N], scalar=sgc[:, 0:1], in1=gc[:, 0:N],
                                                                             op0=ALU.mult, op1=ALU.mult), [o, sgc, gc], [ycT])
                                yield
                                S.dma("pool", catT[b, 1536 + h * 128:1536 + (h + 1) * 128, t0:t0 + N], ycT[:, 0:N], [ycT], [])
                            fin_pending[0] = fin()
                if fin_pending[0] is not None:
                    for _ in fin_pending[0]:
                        pass
                    fin_pending[0] = None
                S.barrier()

            with ExitStack() as es:
                wo_bf = alloc(es, "o_wo", [128, 16, D], BF16)
                with ExitStack() as es2:
                    wst = alloc(es2, "o_wst", [128, D], F32, n=3)
                    for j in range(16):
                        st_ = wst.next()
                        S.dma("sp", st_[:, :], w_out[L, j * 128:(j + 1) * 128, :], [], [st_])
                        if j % 2:
                            S.op("dve", lambda E: E.tensor_copy(out=wo_bf[:, j, :], in_=st_[:, :]), [st_], [wo_bf])
                        else:
                            S.op("act", lambda E: E.activation(out=wo_bf[:, j, :], in_=st_[:, :], func=AF.Copy), [st_], [wo_bf])
                S.barrier()
                wo_bf.const = True
                gate_bc = alloc(es, "o_gate", [128, NB + 1, D], F32, const=True)
                for r in range(NB + 1):
                    S.dma("sp", gate_bc[:, r, :], modrow[L, r:r + 1, 2 * D:3 * D].partition_broadcast(128), [], [gate_bc])
                fng_bc = alloc(es, "o_fng", [128, D], F32, const=True)
                S.dma("sp", fng_bc[:, :], final_g[0:1, :].partition_broadcast(128), [], [fng_bc])
                cat_r = alloc(es, "o_cat", [128, 16, 128], BF16, n=2)
                xr_r = alloc(es, "o_xr", [128, D], F32, n=2)
                tmp_r = alloc(es, "o_tmp", [128, D], F32, n=2)
                xo_r = alloc(es, "o_xo", [128, D], F32, n=2)
                junk = alloc(es, "o_junk", [128, D], BF16)
                ms_r = alloc(es, "o_ms", [128, 2], F32, n=2)
                rs_r = alloc(es, "o_rs", [128, 2], F32, n=2)
                psO_r = palloc(es, "o_ps", [128, D], F32, n=2)
                for b in range(NB):
                    tiles = list(range(NCC, NCH)) if last else list(range(NCH))
                    for c in tiles:
                        tk = slice(c * 128, (c + 1) * 128)
                        r = NB if c < NCC else b
                        cat = cat_r.next()
                        S.dma("sp", cat[:, :, :], catT[b, :, tk].rearrange("(j p) t -> p j t", p=128), [], [cat])
                        xr = xr_r.next()
                        S.dma("sp", xr[:, :], src_rows(L, b, c * 128, 128), [], [xr])
                        ps = psO_r.next()

                        def mm(E):
                            for hf in range(2):
                                for j in range(16):
                                    ins = E.matmul(ps[:, hf * 512:(hf + 1) * 512], cat[:, j, :], wo_bf[:, j, hf * 512:(hf + 1) * 512],
                                                   start=(j == 0), stop=(j == 15))
                            return ins
                        S.op("pe", mm, [cat, wo_bf], [ps])
                        tmp = tmp_r.next()
                        S.op("dve", lambda E: E.tensor_tensor(out=tmp[:, :], in0=ps[:, :], in1=gate_bc[:, r, :], op=ALU.mult), [ps, gate_bc], [tmp])
                        xo = xo_r.next()
                        S.op("pool", lambda E: E.tensor_tensor(out=xo[:, :], in0=tmp[:, :], in1=xr[:, :], op=ALU.add), [tmp, xr], [xo])
                        if not last:
                            S.dma("pool", x1_d[b, tk, :], xo[:, :], [xo], [])
                        else:
                            ms = ms_r.next()
                            rs = rs_r.next()
                            S.op("act", lambda E: E.activation(out=junk[:, :], in_=xo[:, :], func=AF.Square, scale=1.0 / 32.0,
                                                               accum_out=ms[:, 0:1]), [xo], [junk, ms])
                            rstd_from_ms(ms, rs, 1)
                            S.op("dve", lambda E: E.scalar_tensor_tensor(out=tmp[:, :], in0=xo[:, :], scalar=rs[:, 0:1], in1=fng_bc[:, :],
                                                                         op0=ALU.mult, op1=ALU.mult), [xo, rs, fng_bc], [tmp])
                            S.dma("pool", y_out[b, (c - NCC) * 128:(c - NCC + 1) * 128, :], tmp[:, :], [tmp], [])
                S.barrier()
        S.barrier()
    print("instr counts", S.cnt, "waits", S.nwait)
    return nc


def host_consts(NLAT):
    idx = np.arange(128)
    ident = np.eye(128, dtype=np.float32)
    triF = (idx[:, None] <= idx[None, :]).astype(np.float32)
    triB = (idx[:, None] >= idx[None, :]).astype(np.float32)
    ones = np.ones((128, 128), np.float32)
    negF = np.where(idx[None, :] < idx[:, None], NEG, 0.0).astype(np.float32)
    negB = np.where(idx[None, :] > idx[:, None], NEG, 0.0).astype(np.float32)
    k_const = np.stack([ident, triF, triB, ones, negF, negB]).astype(np.float32)
    R = np.zeros((128, 128), np.float32)
    for m in range(128):
        if m % 64 < 32:
            R[m, m + 32] = -1.0
        else:
            R[m, m - 32] = 1.0
    k_rt = np.ascontiguousarray(R.T)
    n = np.arange(NLAT)
    row = (n // 64).astype(np.float32)
    col = (n % 64).astype(np.float32)
    inv = (10000.0 ** (-np.arange(16, dtype=np.float32) / 16)).astype(np.float32)
    ang = np.concatenate([row[:, None] * inv, col[:, None] * inv], axis=-1).astype(np.float32)
    cos = np.cos(ang).astype(np.float32)
    sin = np.sin(ang).astype(np.float32)
    cosT = np.tile(cos.T, (4, 1))
    sinT = np.tile(sin.T, (4, 1))
    k_rope = np.ascontiguousarray(np.stack([cosT, sinT]).astype(np.float32))
    return k_const, k_rt, k_rope


def make_in_maps(inputs, n_cores, NB, NLAT):
    k_const, k_rt, k_rope = host_consts(NLAT)
    f = lambda a: np.ascontiguousarray(np.asarray(a, dtype=np.float32))
    shared = {
        "w_ada": f(inputs["w_ada"]), "b_ada": f(inputs["b_ada"]), "w_in": f(inputs["w_in"]), "w_out": f(inputs["w_out"]),
        "chunk_norm_g": f(inputs["chunk_norm_g"]), "chunk_ws": f(inputs["chunk_ws"]), "chunk_bs": f(inputs["chunk_bs"]),
        "ssm_conv_w": f(inputs["ssm_conv_w"]), "ssm_conv_b": f(inputs["ssm_conv_b"]),
        "ssm_dt_bias": f(inputs["ssm_dt_bias"]).reshape(DEPTH, 32), "ssm_a_log": f(inputs["ssm_a_log"]).reshape(DEPTH, 32),
        "ssm_d": f(inputs["ssm_d"]), "ssm_norm_g": f(inputs["ssm_norm_g"]),
        "diff_lambda_p": f(inputs["diff_lambda_p"]).reshape(DEPTH, 256), "diff_subln_g": f(inputs["diff_subln_g"]),
        "final_norm_g": f(inputs["final_norm_g"]).reshape(1, D),
        "k_const": k_const, "k_rt": k_rt, "k_rope": k_rope,
    }
    x = f(inputs["x"])
    ctx = f(inputs["ctx"])
    c = f(inputs["c"])
    c_ctx = f(inputs["c_ctx"]).reshape(1, D)
    maps = []
    for i in range(n_cores):
        m = dict(shared)
        m["x"] = np.ascontiguousarray(x[i * NB:(i + 1) * NB, :NLAT])
        m["ctx"] = np.ascontiguousarray(ctx[i * NB:(i + 1) * NB])
        m["cvec"] = np.ascontiguousarray(np.concatenate([c[i * NB:(i + 1) * NB], c_ctx], axis=0))
        maps.append(m)
    return maps


def kernel(**inputs):
    n_cores = 8
    NB = 2
    nc = build(NB, NCTX_FULL, NLAT_FULL)
    maps = make_in_maps(inputs, n_cores, NB, NLAT_FULL)
    res = run_bass_kernel_spmd(nc, maps, core_ids=list(range(n_cores)))
    return np.concatenate([np.asarray(r["y"], dtype=np.float32) for r in res.results], axis=0)
```
